# Optimizing a Trainium2 kernel written in Bass

```python
import jax, jax.numpy as jnp
from jax import lax
import numpy as np

D_MODEL = 2048
BATCH = 2
SEQ = 4096
DEPTH = 2

MEM_LEN = 256
EPS = 1e-6
D_FF = 5632
MLA_HEADS = 8
MLA_Q_RANK = 512
MLA_KV_RANK = 512
MLA_NOPE = 128
MLA_ROPE = 64
MLA_V = 128
MLA_WIDTH = MLA_HEADS * MLA_V
ROPE_THETA = 10000.0
Q_BLOCK = 128
GLA_HEADS = 4
GLA_DK = 64
GLA_DV = 128
GLA_WIDTH = GLA_HEADS * GLA_DV
GLA_GATE_RANK = 16
GLA_TAU = 16.0
GLA_CHUNK = 64
CONV_DIM = 512
CONV_WIDTH = 3
X_HEADS = 4
X_HEAD_DIM = D_MODEL // X_HEADS
N_BRANCH = 3
IN_SPLITS = (MLA_Q_RANK, MLA_KV_RANK, MLA_ROPE,
             GLA_HEADS * GLA_DK, GLA_HEADS * GLA_DK, GLA_WIDTH, GLA_WIDTH, GLA_GATE_RANK,
             3 * CONV_DIM, N_BRANCH * D_MODEL)
D_IN_PROJ = sum(IN_SPLITS)

kernel_name = "hybrid_mla_gla_shortconv_gated_macaron"


def _split(z, sizes):
    outs, off = [], 0
    for n in sizes:
        outs.append(z[..., off:off + n])
        off += n
    return outs


def rmsnorm(x, g):
    xf = x.astype(jnp.float32)
    y = xf * lax.rsqrt(jnp.mean(xf * xf, axis=-1, keepdims=True) + EPS)
    return (y * g.astype(jnp.float32)).astype(x.dtype)


def swiglu(x, w_in, w_out):
    gate, up = jnp.split(x @ w_in, 2, axis=-1)
    return (jax.nn.silu(gate) * up) @ w_out


def rope_tables(seq):
    inv_freq = 1.0 / (ROPE_THETA ** (jnp.arange(0, MLA_ROPE, 2, dtype=jnp.float32) / MLA_ROPE))
    ang = jnp.arange(seq, dtype=jnp.float32)[:, None] * inv_freq[None, :]
    return jnp.cos(ang), jnp.sin(ang)


def apply_rope(x, cos, sin):
    x1, x2 = jnp.split(x, 2, axis=-1)
    cos = cos.astype(x.dtype)
    sin = sin.astype(x.dtype)
    return jnp.concatenate([x1 * cos - x2 * sin, x2 * cos + x1 * sin], axis=-1)


def mla(c_q, c_kv, k_rope, q_norm_g, kv_norm_g, w_uq, w_ukv, cos, sin):
    B, S, _ = c_q.shape
    q = (rmsnorm(c_q, q_norm_g) @ w_uq).reshape(B, S, MLA_HEADS, MLA_NOPE + MLA_ROPE)
    q_nope, q_pe = q[..., :MLA_NOPE], q[..., MLA_NOPE:]
    kv = (rmsnorm(c_kv, kv_norm_g) @ w_ukv).reshape(B, S, MLA_HEADS, MLA_NOPE + MLA_V)
    k_nope, v = kv[..., :MLA_NOPE], kv[..., MLA_NOPE:]
    q_pe = apply_rope(q_pe, cos[:, None, :], sin[:, None, :])
    k_pe = apply_rope(k_rope, cos, sin)
    scale = (MLA_NOPE + MLA_ROPE) ** -0.5
    nb = S // Q_BLOCK
    qn_b = q_nope.reshape(B, nb, Q_BLOCK, MLA_HEADS, MLA_NOPE).transpose(1, 0, 2, 3, 4)
    qp_b = q_pe.reshape(B, nb, Q_BLOCK, MLA_HEADS, MLA_ROPE).transpose(1, 0, 2, 3, 4)
    k_pos = jnp.arange(S)

    def block(args):
        i, qn, qp = args
        s = (jnp.einsum('bqhd,bkhd->bhqk', qn, k_nope)
             + jnp.einsum('bqhr,bkr->bhqk', qp, k_pe)).astype(jnp.float32) * scale
        q_pos = i * Q_BLOCK + jnp.arange(Q_BLOCK)
        s = jnp.where(q_pos[:, None] >= k_pos[None, :], s, -jnp.inf)
        p = jax.nn.softmax(s, axis=-1).astype(v.dtype)
        return jnp.einsum('bhqk,bkhd->bqhd', p, v)

    o = lax.map(block, (jnp.arange(nb), qn_b, qp_b))
    return o.transpose(1, 0, 2, 3, 4).reshape(B, S, MLA_WIDTH)


def gla(q, k, v, r, a_low, w_a2, b_a, norm_g):
    B, S, _ = q.shape
    H, DK, DV, C = GLA_HEADS, GLA_DK, GLA_DV, GLA_CHUNK
    N = S // C
    f32 = jnp.float32
    log_a = jax.nn.log_sigmoid((a_low @ w_a2 + b_a).astype(f32)) / GLA_TAU

    def heads(t, d):
        return t.astype(f32).reshape(B, N, C, H, d).transpose(0, 3, 1, 2, 4)

    qh = heads(q, DK) * DK ** -0.5
    kh = heads(k, DK)
    vh = heads(v, DV)
    bcum = jnp.cumsum(heads(log_a, DK), axis=3)
    causal = jnp.tril(jnp.ones((C, C), dtype=bool))[:, :, None]
    decay = jnp.exp(jnp.where(causal, bcum[..., :, None, :] - bcum[..., None, :, :], -jnp.inf))
    attn = jnp.einsum('bhntd,bhnsd,bhntsd->bhnts', qh, kh, decay)
    o_intra = jnp.einsum('bhnts,bhnsv->bhntv', attn, vh)
    b_last = bcum[..., -1:, :]
    q_dec = qh * jnp.exp(bcum)
    k_dec = kh * jnp.exp(b_last - bcum)
    chunk_decay = jnp.exp(b_last[..., 0, :])

    def step(state, xs):
        qd, kd, vc, cd = xs
        o = jnp.einsum('bhtd,bhdv->bhtv', qd, state)
        state = cd[..., None] * state + jnp.einsum('bhsd,bhsv->bhdv', kd, vc)
        return state, o

    xs = (jnp.moveaxis(q_dec, 2, 0), jnp.moveaxis(k_dec, 2, 0),
          jnp.moveaxis(vh, 2, 0), jnp.moveaxis(chunk_decay, 2, 0))
    _, o_inter = lax.scan(step, jnp.zeros((B, H, DK, DV), f32), xs)
    o = o_intra + jnp.moveaxis(o_inter, 0, 2)
    o = o.transpose(0, 2, 3, 1, 4).reshape(B, S, H, DV)
    o = o * lax.rsqrt(jnp.mean(o * o, axis=-1, keepdims=True) + EPS)
    o = o.reshape(B, S, GLA_WIDTH) * norm_g.astype(f32)
    return (o * jax.nn.silu(r.astype(f32))).astype(q.dtype)


def short_conv(bch, conv_w):
    b_g, c_g, h_in = jnp.split(bch, 3, axis=-1)
    z = c_g * h_in
    y = lax.conv_general_dilated(z, conv_w.astype(z.dtype), window_strides=(1,),
                                 padding=((CONV_WIDTH - 1, 0),),
                                 dimension_numbers=('NWC', 'WIO', 'NWC'),
                                 feature_group_count=CONV_DIM)
    return b_g * y


def cross_attn(hn, memn, w_q, w_kv, w_o):
    B, S, _ = hn.shape
    M = memn.shape[1]
    q = (hn @ w_q).reshape(B, S, X_HEADS, X_HEAD_DIM)
    k, v = jnp.split(memn @ w_kv, 2, axis=-1)
    k = k.reshape(B, M, X_HEADS, X_HEAD_DIM)
    v = v.reshape(B, M, X_HEADS, X_HEAD_DIM)
    s = jnp.einsum('bshd,bmhd->bhsm', q, k).astype(jnp.float32) * X_HEAD_DIM ** -0.5
    p = jax.nn.softmax(s, axis=-1).astype(v.dtype)
    o = jnp.einsum('bhsm,bmhd->bshd', p, v).reshape(B, S, D_MODEL)
    return o @ w_o


def setup_inputs(seed: int = 0) -> dict:
    key = jax.random.key(seed)
    k = jax.random.split(key, 29)
    L, D, F = DEPTH, D_MODEL, D_FF
    f32 = jnp.float32

    def w(kk, shape, fan_in):
        return jax.random.normal(kk, shape, f32) * fan_in ** -0.5

    def gain(kk, shape):
        return 1.0 + 0.02 * jax.random.normal(kk, shape, f32)

    def bias(kk, shape, s):
        return s * jax.random.normal(kk, shape, f32)

    return {
        "x": jax.random.normal(k[0], (BATCH, SEQ, D), f32),
        "mem": jax.random.normal(k[1], (BATCH, MEM_LEN, D), f32),
        "ffn1_norm": gain(k[2], (L, D)),
        "ffn1_w_in": w(k[3], (L, D, 2 * F), D),
        "ffn1_w_out": w(k[4], (L, F, D), F),
        "mix_norm": gain(k[5], (L, D)),
        "mix_w_in": w(k[6], (L, D, D_IN_PROJ), D),
        "mix_b_gate": bias(k[7], (L, N_BRANCH * D), 0.02),
        "mla_q_norm": gain(k[8], (L, MLA_Q_RANK)),
        "mla_kv_norm": gain(k[9], (L, MLA_KV_RANK)),
        "mla_w_uq": w(k[10], (L, MLA_Q_RANK, MLA_HEADS * (MLA_NOPE + MLA_ROPE)), MLA_Q_RANK),
        "mla_w_ukv": w(k[11], (L, MLA_KV_RANK, MLA_HEADS * (MLA_NOPE + MLA_V)), MLA_KV_RANK),
        "mla_w_proj": w(k[12], (L, MLA_WIDTH, D), MLA_WIDTH),
        "gla_w_a2": w(k[13], (L, GLA_GATE_RANK, GLA_HEADS * GLA_DK), GLA_GATE_RANK),
        "gla_b_a": bias(k[14], (L, GLA_HEADS * GLA_DK), 0.1),
        "gla_norm": gain(k[15], (L, GLA_WIDTH)),
        "gla_w_proj": w(k[16], (L, GLA_WIDTH, D), GLA_WIDTH),
        "conv_w": w(k[17], (L, CONV_WIDTH, 1, CONV_DIM), CONV_WIDTH),
        "conv_w_proj": w(k[18], (L, CONV_DIM, D), CONV_DIM),
        "mix_w_out": w(k[19], (L, D, D), D),
        "xattn_norm": gain(k[20], (L, D)),
        "mem_norm": gain(k[21], (L, D)),
        "xattn_w_q": w(k[22], (L, D, D), D),
        "xattn_w_kv": w(k[23], (L, D, 2 * D), D),
        "xattn_w_o": w(k[24], (L, D, D), D),
        "ffn2_norm": gain(k[25], (L, D)),
        "ffn2_w_in": w(k[26], (L, D, 2 * F), D),
        "ffn2_w_out": w(k[27], (L, F, D), F),
        "final_norm": gain(k[28], (D,)),
    }


def reference(x, mem, ffn1_norm, ffn1_w_in, ffn1_w_out, mix_norm, mix_w_in, mix_b_gate,
              mla_q_norm, mla_kv_norm, mla_w_uq, mla_w_ukv, mla_w_proj,
              gla_w_a2, gla_b_a, gla_norm, gla_w_proj, conv_w, conv_w_proj, mix_w_out,
              xattn_norm, mem_norm, xattn_w_q, xattn_w_kv, xattn_w_o,
              ffn2_norm, ffn2_w_in, ffn2_w_out, final_norm):
    cos, sin = rope_tables(x.shape[1])
    h = x
    for l in range(DEPTH):
        h = h + 0.5 * swiglu(rmsnorm(h, ffn1_norm[l]), ffn1_w_in[l], ffn1_w_out[l])
        u = rmsnorm(h, mix_norm[l])
        z = u @ mix_w_in[l]
        (c_q, c_kv, k_rope, g_q, g_k, g_v, g_r, a_low, conv_in, gate_pre) = _split(z, IN_SPLITS)
        y_mla = mla(c_q, c_kv, k_rope, mla_q_norm[l], mla_kv_norm[l],
                    mla_w_uq[l], mla_w_ukv[l], cos, sin) @ mla_w_proj[l]
        y_gla = gla(g_q, g_k, g_v, g_r, a_low, gla_w_a2[l], gla_b_a[l], gla_norm[l]) @ gla_w_proj[l]
        y_conv = short_conv(conv_in, conv_w[l]) @ conv_w_proj[l]
        gates = jax.nn.sigmoid((gate_pre + mix_b_gate[l]).astype(jnp.float32)).astype(h.dtype)
        g_mla, g_gla, g_conv = jnp.split(gates, N_BRANCH, axis=-1)
        merged = g_mla * y_mla + g_gla * y_gla + g_conv * y_conv
        h = h + merged @ mix_w_out[l]
        h = h + cross_attn(rmsnorm(h, xattn_norm[l]), rmsnorm(mem, mem_norm[l]),
                           xattn_w_q[l], xattn_w_kv[l], xattn_w_o[l])
        h = h + 0.5 * swiglu(rmsnorm(h, ffn2_norm[l]), ffn2_w_in[l], ffn2_w_out[l])
    return rmsnorm(h, final_norm)
```

```python
import numpy as np
import ml_dtypes
from contextlib import ExitStack
import concourse.bass as bass
import concourse.mybir as mybir
from concourse.bass_utils import run_bass_kernel_spmd

F32 = mybir.dt.float32
BF16 = mybir.dt.bfloat16
AF = mybir.ActivationFunctionType
ALU = mybir.AluOpType

NCORES = 8
T = 1024
D = 2048
FF = 5632
KC = 16
NTB = 8
NHC = 44
HG = 4
CPG = 11
EPS = 1e-6
DEPTH = 2
MEM = 256
SCALE_MLA = 192 ** -0.5
SCALE_X = 512 ** -0.5
XK_COLS = 5120
XS_COLS = 266
NEG = -30000.0
FUSED = True

ENGS = ['pe', 'act', 'dve', 'pool', 'sp']


class Res:
    __slots__ = ('w', 'r')

    def __init__(self):
        self.w = None
        self.r = []


class Prog:
    def __init__(self, nc, stack):
        self.nc = nc
        self.stack = stack
        self.streams = {e: [] for e in ENGS}
        self.cnt = {e: 0 for e in ENGS}
        self.sem = {e: stack.enter_context(nc.semaphore('s_' + e)) for e in ENGS}
        self.waited = {e: {} for e in ENGS}
        self.semobj = {}
        self.dma_sems = {}
        self.pool_pending = None

    def _waits(self, eng, reads, writes):
        need = {}
        for r in reads:
            if r.w is not None:
                s, v = r.w
                if need.get(s, 0) < v:
                    need[s] = v
        for w in writes:
            if w.w is not None:
                s, v = w.w
                if need.get(s, 0) < v:
                    need[s] = v
            for (s, v) in w.r:
                if need.get(s, 0) < v:
                    need[s] = v
        out = []
        me = id(self.sem[eng])
        for s, v in need.items():
            if s == me and eng == 'pe':
                continue
            if self.waited[eng].get(s, 0) < v:
                self.waited[eng][s] = v
                out.append((self.semobj[s], v))
        return out

    def _tok(self, semh, v):
        self.semobj[id(semh)] = semh
        return (id(semh), v)

    def _mark(self, tok, reads, writes):
        for r in reads:
            r.r.append(tok)
        for w in writes:
            w.w = tok
            w.r = []

    def op(self, eng, fn, reads=(), writes=()):
        waits = self._waits(eng, reads, writes)
        if eng == 'pool' and self.pool_pending:
            for s, v in self.pool_pending:
                if s != id(self.sem['pool']) and self.waited['pool'].get(s, 0) < v:
                    self.waited['pool'][s] = v
                    waits.append((self.semobj[s], v))
            self.pool_pending = None
        self.cnt[eng] += 1
        semh = self.sem[eng]
        tok = self._tok(semh, self.cnt[eng])
        self.streams[eng].append((waits, fn, (semh, 1)))
        self._mark(tok, reads, writes)
        return tok

    def dma(self, q, out, in_, reads=(), writes=(), key='d'):
        waits = self._waits(q, reads, writes)
        if key not in self.dma_sems:
            self.dma_sems[key] = [self.stack.enter_context(self.nc.semaphore('d_' + key)), 0]
        ent = self.dma_sems[key]
        ent[1] += 16
        tok = self._tok(ent[0], ent[1])

        def fn(e, out=out, in_=in_):
            return e.dma_start(out=out, in_=in_)
        self.streams[q].append((waits, fn, (ent[0], 16)))
        self._mark(tok, reads, writes)
        return tok

    def custom(self, q, fn, inc, reads=(), writes=(), key='cc'):
        waits = self._waits(q, reads, writes)
        if key not in self.dma_sems:
            self.dma_sems[key] = [self.stack.enter_context(self.nc.semaphore('d_' + key)), 0]
        ent = self.dma_sems[key]
        ent[1] += inc
        tok = self._tok(ent[0], ent[1])
        self.streams[q].append((waits, fn, (ent[0], inc)))
        self._mark(tok, reads, writes)
        return tok

    def barrier(self, full=False):
        toks = [(id(self.sem[e]), self.cnt[e]) for e in ENGS if self.cnt[e] > 0]
        for k, ent in self.dma_sems.items():
            toks.append((id(ent[0]), ent[1]))
        for e in ENGS:
            if e == 'pool' and not full:
                self.pool_pending = toks
                continue
            out = []
            for s, v in toks:
                if s == id(self.sem[e]) and e == 'pe':
                    continue
                if self.waited[e].get(s, 0) < v:
                    self.waited[e][s] = v
                    out.append((self.semobj[s], v))
            if out:
                self.streams[e].append((out, None, None))

    def emit(self):
        nc = self.nc
        streams = self.streams

        def run(e, lst):
            for waits, fn, inc in lst:
                for s, v in waits:
                    e.wait_ge(s, v)
                if fn is not None:
                    ins = fn(e)
                    if inc is not None:
                        ins.then_inc(inc[0], inc[1])

        with nc.Block() as block:
            @block.tensor
            def _(e):
                run(e, streams['pe'])

            @block.scalar
            def _(e):
                run(e, streams['act'])

            @block.vector
            def _(e):
                run(e, streams['dve'])

            @block.gpsimd
            def _(e):
                run(e, streams['pool'])

            @block.sync
            def _(e):
                run(e, streams['sp'])


class Region:
    def __init__(self, arena, lo, hi):
        self.arena, self.lo, self.hi, self.off = arena, lo, hi, lo

    def reset(self):
        self.off = self.lo

    def alloc(self, nbytes, dtype=BF16):
        nbytes = (nbytes + 63) // 64 * 64
        assert self.off + nbytes <= self.hi, ('region overflow', self.off, nbytes, self.hi)
        a = self.arena[:, self.off // 2:(self.off + nbytes) // 2]
        self.off += nbytes
        if dtype != BF16:
            a = a.bitcast(dtype)
        return a


def v3(ap, b):
    return ap.rearrange('p (a b) -> p a b', b=b)


def tile_w(W, ncols):
    K, N = W.shape
    kc = K // 128
    t = W.reshape(kc, 128, N // ncols, ncols).transpose(2, 1, 0, 3)
    return np.ascontiguousarray(t).reshape(N // ncols, 128, kc * ncols)


def prep_ffn_w1(w_in):
    g = w_in[:, :FF].reshape(KC, 128, NHC, 128)
    u = w_in[:, FF:].reshape(KC, 128, NHC, 128)
    t = np.stack([g, u], axis=3).transpose(2, 1, 0, 3, 4)
    return np.ascontiguousarray(t).reshape(NHC, 128, KC * 256)


def prep_ffn_w2(w_out):
    t = w_out.reshape(HG, CPG, 128, 4, 512).transpose(0, 3, 2, 1, 4)
    return np.ascontiguousarray(t).reshape(HG * 4, 128, CPG * 512)


def layer_weights(inp, l):
    W = {}
    win = inp['mix_w_in'][l]
    W['f1w1'] = prep_ffn_w1(inp['ffn1_w_in'][l])
    W['f1w2'] = prep_ffn_w2(inp['ffn1_w_out'][l])
    W['f2w1'] = prep_ffn_w1(inp['ffn2_w_in'][l])
    W['f2w2'] = prep_ffn_w2(inp['ffn2_w_out'][l])
    W['wq'] = tile_w(win[:, 0:512], 256)
    W['wkv'] = tile_w(win[:, 512:1024], 256)
    kr = win[:, 1024:1088]
    W['wkr'] = tile_w(np.concatenate([kr, kr[:, 32:], kr[:, :32]], axis=1), 128)
    W['wgla'] = tile_w(win[:, 1088:2624], 256)
    W['walow'] = tile_w(win[:, 2624:2640], 16)
    W['wconv'] = tile_w(win[:, 2640:4176], 256)
    gates = win[:, 4176:]
    mt = np.zeros((16, 2, 128, 32, 128), np.float32)
    gm = gates[:, 0:2048].reshape(16, 128, 16, 128)
    gg = gates[:, 2048:4096].reshape(16, 128, 16, 128)
    gc = gates[:, 4096:6144].reshape(16, 128, 16, 128)
    pm = inp['mla_w_proj'][l].reshape(8, 128, 16, 128)
    pg = inp['gla_w_proj'][l].reshape(4, 128, 16, 128)
    pc = inp['conv_w_proj'][l].reshape(4, 128, 16, 128)
    mt[:, 0, :, 0:16] = gm.transpose(2, 1, 0, 3)
    mt[:, 0, :, 16:32] = gg.transpose(2, 1, 0, 3)
    mt[:, 1, :, 0:16] = gc.transpose(2, 1, 0, 3)
    mt[:, 1, :, 16:24] = pm.transpose(2, 1, 0, 3)
    mt[:, 1, :, 24:28] = pg.transpose(2, 1, 0, 3)
    mt[:, 1, :, 28:32] = pc.transpose(2, 1, 0, 3)
    W['wmerge'] = mt.reshape(32, 128, 32 * 128)
    W['wout'] = tile_w(inp['mix_w_out'][l], 256)
    W['xq'] = tile_w(inp['xattn_w_q'][l], 256)
    W['xk'] = tile_w(inp['xattn_w_kv'][l][:, :D], 256)
    W['xv'] = tile_w(inp['xattn_w_kv'][l][:, D:], 256)
    W['xo'] = tile_w(inp['xattn_w_o'][l], 256)
    uq = inp['mla_w_uq'][l].reshape(512, 8, 192)
    ukv = inp['mla_w_ukv'][l].reshape(512, 8, 256)
    hh = np.concatenate([uq[:, :, 0:128], uq[:, :, 128:192], uq[:, :, 160:192], uq[:, :, 128:160], ukv], axis=2)
    hh = hh.reshape(4, 128, 8, 512).transpose(2, 1, 0, 3)
    W['wmla'] = np.ascontiguousarray(hh).reshape(8, 128, 4 * 512)
    W['wa2'] = np.ascontiguousarray(inp['gla_w_a2'][l])
    W['ba'] = np.ascontiguousarray(inp['gla_b_a'][l][None, :])
    for nm in ('ffn1_norm', 'mix_norm', 'xattn_norm', 'mem_norm', 'ffn2_norm', 'mla_q_norm', 'mla_kv_norm', 'gla_norm'):
        W['g_' + nm] = np.ascontiguousarray(inp[nm][l][None, :])
    W['convw'] = np.ascontiguousarray(inp['conv_w'][l][:, 0, :].reshape(3, 4, 128).transpose(2, 1, 0)).reshape(128, 12)
    W['bgate'] = np.ascontiguousarray(inp['mix_b_gate'][l].reshape(3, 16, 128).transpose(2, 0, 1)).reshape(128, 48)
    return {k: np.ascontiguousarray(v, dtype=np.float32) for k, v in W.items()}


def const_bf():
    c = np.zeros((128, 384), np.float32)
    c[:, 0:128] = np.eye(128)
    c[:, 128:256] = 1.0
    c[:, 256:384] = np.triu(np.ones((128, 128)))
    return c.astype(ml_dtypes.bfloat16)


def const_f32():
    c = np.zeros((128, 384), np.float32)
    tri = np.triu(np.ones((128, 128), np.float32))
    c[:, 0:128] = -tri / 16.0
    c[:, 128:256] = -(1.0 - tri) / 16.0
    c[:, 256:384] = -1.0 / 16.0
    return c


def core_tables(c):
    j = c % 4
    ctl = np.zeros((128, 16), np.float32)
    for i in range(4):
        ctl[:, i] = 0.0 if i < j else NEG
        ctl[:, 4 + i] = 1.0 if i < j else 0.0
        ctl[:, 8 + i] = 0.0 if i < j else 1.0
        ctl[:, 12 + i] = 1.0 if i == j - 1 else 0.0
    inv_freq = (1.0 / (np.float32(10000.0) ** (np.arange(0, 64, 2, dtype=np.float32) / np.float32(64)))).astype(np.float32)
    pos = (np.arange(T, dtype=np.float32) + np.float32(j * T))
    ang = (pos[:, None] * inv_freq[None, :]).astype(np.float32)
    cos, sin = np.cos(ang).astype(np.float32).T, np.sin(ang).astype(np.float32).T
    rope = np.zeros((64, 2 * T), np.float32)
    rope[0:32, 0:T] = cos
    rope[32:64, 0:T] = cos
    rope[0:32, T:] = -sin
    rope[32:64, T:] = sin
    return ctl, rope


LAYER_KEYS = ['f1w1', 'f1w2', 'f2w1', 'f2w2', 'wq', 'wkv', 'wkr', 'wgla', 'walow', 'wconv', 'wmerge', 'wout', 'xq', 'xk', 'xv',
              'xo', 'wmla', 'wa2', 'ba', 'g_ffn1_norm', 'g_mix_norm', 'g_xattn_norm', 'g_mem_norm', 'g_ffn2_norm',
              'g_mla_q_norm', 'g_mla_kv_norm', 'g_gla_norm', 'convw', 'bgate']


class LazyDram(dict):
    def __init__(self, nc, shapes):
        super().__init__()
        self.nc, self.shapes = nc, shapes

    def __missing__(self, name):
        k = name.split('_', 1)[1]
        ap = self.nc.dram_tensor(name, list(self.shapes[k]), F32, kind='ExternalInput').ap()
        self[name] = ap
        return ap


SHAPES = {'f1w1': (44, 128, 4096), 'f2w1': (44, 128, 4096), 'f1w2': (16, 128, 5632), 'f2w2': (16, 128, 5632),
          'wq': (2, 128, 4096), 'wkv': (2, 128, 4096), 'wkr': (1, 128, 2048), 'wgla': (6, 128, 4096), 'walow': (1, 128, 256),
          'wconv': (6, 128, 4096), 'wmerge': (32, 128, 4096), 'wout': (8, 128, 4096), 'xq': (8, 128, 4096), 'xk': (8, 128, 4096),
          'xv': (8, 128, 4096), 'xo': (8, 128, 4096), 'wmla': (8, 128, 2048), 'wa2': (16, 256), 'ba': (1, 256),
          'g_ffn1_norm': (1, D), 'g_mix_norm': (1, D), 'g_xattn_norm': (1, D), 'g_mem_norm': (1, D), 'g_ffn2_norm': (1, D),
          'g_mla_q_norm': (1, 512), 'g_mla_kv_norm': (1, 512), 'g_gla_norm': (1, 512), 'convw': (128, 12), 'bgate': (128, 48)}


class Builder:
    def __init__(self, nc, st, layers, shapes, dbg=None):
        self.nc, self.st = nc, st
        self.P = Prog(nc, st)
        self.dbg = dbg or {}
        self.dram = LazyDram(nc, shapes)
        self.d_cbf = nc.dram_tensor('cbf', [128, 384], BF16, kind='ExternalInput').ap()
        self.d_cf32 = nc.dram_tensor('cf32', [128, 384], F32, kind='ExternalInput').ap()
        self.d_ctl = nc.dram_tensor('ctl', [128, 16], F32, kind='ExternalInput').ap()
        self.d_rope = nc.dram_tensor('rope', [64, 2 * T], F32, kind='ExternalInput').ap()
        TOTAL = 207 * 1024
        self.arena = st.enter_context(nc.sbuf_tensor('arena', [128, TOTAL // 2], BF16))
        P = self.P
        self.RH = Region(self.arena, 0, 65536)
        self.RU = Region(self.arena, 65536, 98304)
        self.RW = Region(self.arena, 98304, 98304 + 36864)
        self.RC = Region(self.arena, 135168, 135168 + 13312)
        self.RX = Region(self.arena, 148480, TOTAL)
        self.h = v3(self.RH.alloc(65536, F32), D)
        self.uT = v3(self.RU.alloc(32768), T)
        self.wsl = [self.RW.alloc(12288) for _ in range(3)]
        self.r_w = [Res() for _ in range(3)]
        self.wi = 0
        self.gb = self.RC.alloc(8192, F32)
        self.cbf = self.RC.alloc(768)
        self.cf32 = self.RC.alloc(1536, F32)
        self.ctl = self.RC.alloc(64, F32)
        self.small = self.RC.alloc(1024, F32)
        self.convw = self.RC.alloc(64, F32)
        self.bgate = self.RC.alloc(192, F32)
        self.wa2 = self.RC.alloc(512)
        self.ba = self.RC.alloc(512)
        self.ident = self.cbf[:, 0:128]
        self.ones = self.cbf[:, 128:256]
        self.tri = self.cbf[:, 256:384]
        self.r_gb = Res()
        self.r_hsp = Res()
        self.r_gk = Res()
        self.r_gsd = Res()
        self.r_xkd = Res()
        self.r_xsd = Res()
        self.cc_after_xk = None
        self.r_c = Res()
        self.r_lc = Res()
        self.r_small = [Res() for _ in range(256)]
        self.si = 0
        self.r_h = [Res() for _ in range(NTB)]
        self.r_uT = [Res() for _ in range(NTB)]
        self.ps = [st.enter_context(nc.psum_tensor('ps%d' % i, [128, 512], F32)) for i in range(8)]
        self.r_ps = [Res() for _ in range(8)]
        self.ps_set = list(range(8))
        self.psi = 0
        P.dma('sp', self.cbf[:, :], self.d_cbf, writes=[self.r_c], key='c')
        P.dma('sp', self.cf32[:, :], self.d_cf32, writes=[self.r_c], key='c')
        P.dma('sp', self.ctl[:, :], self.d_ctl, writes=[self.r_c], key='c')
        P.barrier()
        self.ndump = 0

    def getps(self):
        i = self.ps_set[self.psi % len(self.ps_set)]
        self.psi += 1
        return i

    def scal(self):
        i = self.si % 256
        self.si += 1
        return self.small[:, i:i + 1], self.r_small[i]

    def wload(self, src):
        i = self.wi % 3
        self.wi += 1
        n = src.shape[1]
        assert n * 2 <= 12288
        self.P.dma('pool', self.wsl[i][:, 0:n], src, writes=[self.r_w[i]], key='w%d' % i)
        return self.wsl[i], self.r_w[i]

    def dump(self, name, ap, reads, dtype=F32):
        d = self.nc.dram_tensor('dbg_' + name, list(ap.shape), dtype, kind='ExternalOutput').ap()
        self.P.dma('sp', d, ap, reads=reads, key='dbg')

    def load_gain(self, src, n=D):
        self.P.dma('sp', self.gb[:, 0:n], src[0, :].partition_broadcast(128), writes=[self.r_gb], key='gb')

    def norm_T(self, src_fn, src_res_fn, F, dstT, dst_res_fn, ntb, ub, r_ub, reads_extra=()):
        P = self.P
        nk = F // 128
        for tb in range(ntb):
            src = src_fn(tb)
            ss, r_ss = self.scal()
            rs, r_rs = self.scal()
            b = tb % 2
            P.op('act', lambda e, src=src, b=b, ss=ss: e.activation(out=ub[b][:, 0:F], in_=src, func=AF.Square, accum_out=ss),
                 reads=[src_res_fn(tb)] + list(reads_extra), writes=[r_ub[b], r_ss])
            P.op('dve', lambda e, ss=ss, rs=rs: e.tensor_scalar(out=rs, in0=ss, scalar1=1.0 / F, scalar2=EPS, op0=ALU.mult, op1=ALU.add),
                 reads=[r_ss], writes=[r_rs])
            P.op('act', lambda e, rs=rs: e.activation(out=rs, in_=rs, func=AF.Sqrt), reads=[r_rs], writes=[r_rs])
            P.op('dve', lambda e, rs=rs: e.reciprocal(out=rs, in_=rs), reads=[r_rs], writes=[r_rs])
            P.op('dve', lambda e, src=src, b=b, rs=rs: e.scalar_tensor_tensor(out=ub[b][:, 0:F], in0=src, scalar=rs, in1=self.gb[:, 0:F], op0=ALU.mult, op1=ALU.mult),
                 reads=[src_res_fn(tb), r_rs, self.r_gb], writes=[r_ub[b]])
            for k4 in range(nk // 4):
                pi = self.getps()
                pt = self.ps[pi][:, :].bitcast(BF16)

                def tr(e, b=b, k4=k4, pt=pt):
                    ins = None
                    for j in range(4):
                        kc = k4 * 4 + j
                        ins = e.transpose(out=pt[:, j * 128:(j + 1) * 128], in_=ub[b][:, kc * 128:(kc + 1) * 128], identity=self.ident)
                    return ins
                P.op('pe', tr, reads=[r_ub[b]], writes=[self.r_ps[pi]])
                eng = 'act' if k4 % 2 == 0 else 'dve'

                def cp(e, tb=tb, k4=k4, pt=pt, eng=eng):
                    s = pt[:, 0:512].rearrange('p (a b) -> p a b', b=128)
                    d = dstT[:, k4 * 4:(k4 + 1) * 4, tb * 128:(tb + 1) * 128]
                    return e.copy(out=d, in_=s) if eng == 'act' else e.tensor_copy(out=d, in_=s)
                P.op(eng, cp, reads=[self.r_ps[pi]], writes=[dst_res_fn(tb)])

    def norm_h(self, gain_ap, dstT=None, dst_res=None):
        X = self.RX
        save = X.off
        ub = [X.alloc(4096), X.alloc(4096)]
        r_ub = [Res(), Res()]
        self.load_gain(gain_ap)
        dstT = self.uT if dstT is None else dstT
        dst_res = self.r_uT if dst_res is None else dst_res
        self.norm_T(lambda tb: self.h[:, tb, :], lambda tb: self.r_h[tb], D, dstT, lambda tb: dst_res[tb], NTB, ub, r_ub)
        self.P.barrier()
        X.off = save

    def lin_fm(self, xT, r_x, nkc, wtiles, ncols, M_list, epi, tgs=(0, 1), ncol_T=512):
        P = self.P
        for ti in range(wtiles.shape[0]):
            wt, r_wt = self.wload(wtiles[ti])
            wv = v3(wt[:, 0:nkc * ncols], ncols)
            for mi, (off, M) in enumerate(M_list):
                for tg in tgs:
                    pi = self.getps()

                    def mm(e, wv=wv, off=off, M=M, tg=tg, pi=pi):
                        ins = None
                        for kc in range(nkc):
                            ins = e.matmul(self.ps[pi][0:M, 0:ncol_T], lhsT=wv[:, kc, off:off + M], rhs=xT[:, kc, tg * ncol_T:(tg + 1) * ncol_T],
                                           start=(kc == 0), stop=(kc == nkc - 1))
                        return ins
                    P.op('pe', mm, reads=[r_wt] + list(r_x(tg)), writes=[self.r_ps[pi]])
                    epi(ti, mi, tg, pi)

    def lin_tok(self, xT, r_x, nkc, wtiles, ncols, epi, ntb=NTB):
        P = self.P
        for ti in range(wtiles.shape[0]):
            wt, r_wt = self.wload(wtiles[ti])
            wv = v3(wt[:, 0:nkc * ncols], ncols)
            for tb in range(ntb):
                pi = self.getps()

                def mm(e, wv=wv, tb=tb, pi=pi):
                    ins = None
                    for kc in range(nkc):
                        ins = e.matmul(self.ps[pi][:, 0:ncols], lhsT=xT[:, kc, tb * 128:(tb + 1) * 128], rhs=wv[:, kc, :],
                                       start=(kc == 0), stop=(kc == nkc - 1))
                    return ins
                P.op('pe', mm, reads=[r_wt] + list(r_x(tb)), writes=[self.r_ps[pi]])
                epi(ti, tb, pi)

    def ffn(self, l, which):
        P = self.P
        X = self.RX
        self.norm_h(self.dram['L%d_g_ffn%d_norm' % (l, which)])
        X.reset()
        actT = [v3(X.alloc(CPG * T * 2), T) for _ in range(2)]
        sg = [X.alloc(2048, F32) for _ in range(2)]
        r_act = [[Res() for _ in range(CPG)] for _ in range(2)]
        r_sg = [Res(), Res()]
        w1 = self.dram['L%d_f%dw1' % (l, which)]
        w2 = self.dram['L%d_f%dw2' % (l, which)]
        uT = self.uT
        for g in range(HG):
            ab = g % 2
            for cl in range(CPG):
                c = g * CPG + cl
                wt, r_wt = self.wload(w1[c])
                wv = v3(wt[:, 0:KC * 256], 256)
                for tg in range(2):
                    pg, pu = self.getps(), self.getps()

                    def mm(e, wv=wv, tg=tg, pg=pg, pu=pu):
                        ins = None
                        for half, pp in ((0, pg), (1, pu)):
                            for kc in range(KC):
                                ins = e.matmul(self.ps[pp][:, :], lhsT=wv[:, kc, half * 128:(half + 1) * 128],
                                               rhs=uT[:, kc, tg * 512:(tg + 1) * 512], start=(kc == 0), stop=(kc == KC - 1))
                        return ins
                    P.op('pe', mm, reads=[r_wt] + self.r_uT[tg * 4:(tg + 1) * 4], writes=[self.r_ps[pg], self.r_ps[pu]])
                    sb = (cl * 2 + tg) % 2
                    P.op('act', lambda e, pg=pg, sb=sb: e.activation(out=sg[sb][:, :], in_=self.ps[pg][:, :], func=AF.Silu),
                         reads=[self.r_ps[pg]], writes=[r_sg[sb]])
                    P.op('dve', lambda e, pu=pu, sb=sb, ab=ab, cl=cl, tg=tg: e.tensor_tensor(out=actT[ab][:, cl, tg * 512:(tg + 1) * 512], in0=sg[sb][:, :], in1=self.ps[pu][:, :], op=ALU.mult),
                         reads=[self.r_ps[pu], r_sg[sb]], writes=[r_act[ab][cl]])
            for ng in range(4):
                wt, r_wt = self.wload(w2[g * 4 + ng])
                wv = v3(wt[:, 0:CPG * 512], 512)
                for tb in range(NTB):
                    po = self.getps()

                    def mm2(e, wv=wv, tb=tb, po=po, ab=ab):
                        ins = None
                        for cl in range(CPG):
                            ins = e.matmul(self.ps[po][:, :], lhsT=actT[ab][:, cl, tb * 128:(tb + 1) * 128], rhs=wv[:, cl, :],
                                           start=(cl == 0), stop=(cl == CPG - 1))
                        return ins
                    P.op('pe', mm2, reads=[r_wt] + r_act[ab], writes=[self.r_ps[po]])
                    P.op('dve', lambda e, po=po, tb=tb, ng=ng: e.scalar_tensor_tensor(out=self.h[:, tb, ng * 512:(ng + 1) * 512], in0=self.ps[po][:, :], scalar=0.5,
                                                                                    in1=self.h[:, tb, ng * 512:(ng + 1) * 512], op0=ALU.mult, op1=ALU.add),
                         reads=[self.r_ps[po], self.r_h[tb]], writes=[self.r_h[tb]])
        P.barrier()
        X.reset()

    def latent(self, l, wkey, gkey, dstT, r_dst, X):
        P = self.P
        save = X.off
        ub = [X.alloc(1024), X.alloc(1024)]
        r_ub = [Res(), Res()]
        lat = [X.alloc(2048, F32) for _ in range(2)]
        r_lat = [Res(), Res()]
        self.load_gain(self.dram['L%d_%s' % (l, gkey)], 512)
        wt = self.dram['L%d_%s' % (l, wkey)]
        w0, r_w0 = self.wload(wt[0])
        w1, r_w1 = self.wload(wt[1])
        wv = [v3(w0[:, 0:KC * 256], 256), v3(w1[:, 0:KC * 256], 256)]
        for tb in range(NTB):
            pi = self.getps()

            def mm(e, tb=tb, pi=pi):
                ins = None
                for half in range(2):
                    for kc in range(KC):
                        ins = e.matmul(self.ps[pi][:, half * 256:(half + 1) * 256], lhsT=self.uT[:, kc, tb * 128:(tb + 1) * 128], rhs=wv[half][:, kc, :],
                                       start=(kc == 0), stop=(kc == KC - 1))
                return ins
            P.op('pe', mm, reads=[r_w0, r_w1, self.r_uT[tb]], writes=[self.r_ps[pi]])
            b = tb % 2
            P.op('act', lambda e, b=b, pi=pi: e.copy(out=lat[b][:, :], in_=self.ps[pi][:, :]), reads=[self.r_ps[pi]], writes=[r_lat[b]])
            src = lat[b][:, :]
            ss, r_ss = self.scal()
            rs, r_rs = self.scal()
            P.op('act', lambda e, src=src, b=b, ss=ss: e.activation(out=ub[b][:, 0:512], in_=src, func=AF.Square, accum_out=ss),
                 reads=[r_lat[b]], writes=[r_ub[b], r_ss])
            P.op('dve', lambda e, ss=ss, rs=rs: e.tensor_scalar(out=rs, in0=ss, scalar1=1.0 / 512, scalar2=EPS, op0=ALU.mult, op1=ALU.add),
                 reads=[r_ss], writes=[r_rs])
            P.op('act', lambda e, rs=rs: e.activation(out=rs, in_=rs, func=AF.Sqrt), reads=[r_rs], writes=[r_rs])
            P.op('dve', lambda e, rs=rs: e.reciprocal(out=rs, in_=rs), reads=[r_rs], writes=[r_rs])
            P.op('dve', lambda e, src=src, b=b, rs=rs: e.scalar_tensor_tensor(out=ub[b][:, 0:512], in0=src, scalar=rs, in1=self.gb[:, 0:512], op0=ALU.mult, op1=ALU.mult),
                 reads=[r_lat[b], r_rs, self.r_gb], writes=[r_ub[b]])
            p2 = self.getps()
            pt = self.ps[p2][:, :].bitcast(BF16)

            def tr(e, b=b, pt=pt):
                ins = None
                for j in range(4):
                    ins = e.transpose(out=pt[:, j * 128:(j + 1) * 128], in_=ub[b][:, j * 128:(j + 1) * 128], identity=self.ident)
                return ins
            P.op('pe', tr, reads=[r_ub[b]], writes=[self.r_ps[p2]])
            P.op('act', lambda e, tb=tb, pt=pt: e.copy(out=dstT[:, 0:4, tb * 128:(tb + 1) * 128], in_=pt[:, 0:512].rearrange('p (a b) -> p a b', b=128)),
                 reads=[self.r_ps[p2]], writes=[r_dst])
        P.barrier()
        X.off = save

    def gla(self, l, mode, X, xs_out=None, gs=None, r_gs=None, oglaT=None, r_ogla=None):
        P = self.P
        uT = self.uT
        M1, M2, M3 = self.cf32[:, 0:128], self.cf32[:, 128:256], self.cf32[:, 256:384]
        alT = X.alloc(2048)
        r_al = Res()
        P.dma('pool', self.wa2[0:16, 0:256], self.dram['L%d_wa2' % l], writes=[self.r_lc], key='lc')
        P.dma('pool', self.ba[0:1, 0:256], self.dram['L%d_ba' % l], writes=[self.r_lc], key='lc')

        def epi_al(ti, mi, tg, pi):
            P.op('act', lambda e: e.copy(out=alT[0:16, tg * 512:(tg + 1) * 512], in_=self.ps[pi][0:16, :]), reads=[self.r_ps[pi]], writes=[r_al])
        self.lin_fm(uT, lambda tg: self.r_uT[tg * 4:(tg + 1) * 4], KC, self.dram['L%d_walow' % l], 16, [(0, 16)], epi_al)
        qk = v3(X.alloc(NTB * 512 * 4, F32), 512)
        V = v3(X.alloc(NTB * 512 * 2), 512)
        r_qk = [Res() for _ in range(NTB)]
        r_V = [Res() for _ in range(NTB)]
        if mode == 'B':
            sr = v3(X.alloc(NTB * 512 * 2), 512)
            r_sr = [Res() for _ in range(NTB)]

        def epi_g(ti, tb, pi):
            if ti < 2:
                P.op('act', lambda e: e.copy(out=qk[:, tb, ti * 256:(ti + 1) * 256], in_=self.ps[pi][:, 0:256]), reads=[self.r_ps[pi]], writes=[r_qk[tb]])
            elif ti < 4:
                P.op('dve', lambda e: e.tensor_copy(out=V[:, tb, (ti - 2) * 256:(ti - 1) * 256], in_=self.ps[pi][:, 0:256]), reads=[self.r_ps[pi]], writes=[r_V[tb]])
            elif mode == 'B':
                P.op('act', lambda e: e.activation(out=sr[:, tb, (ti - 4) * 256:(ti - 3) * 256], in_=self.ps[pi][:, 0:256], func=AF.Silu), reads=[self.r_ps[pi]], writes=[r_sr[tb]])
        wg = self.dram['L%d_wgla' % l]
        self.lin_tok(uT, lambda tb: [self.r_uT[tb]], KC, wg if mode == 'B' else wg[0:4], 256, epi_g)
        S = [X.alloc(512, F32) for _ in range(2)]
        Sb = [X.alloc(256) for _ in range(2)]
        r_S = [Res(), Res()]
        r_Sb = [Res(), Res()]
        Dt = X.alloc(64, F32)
        r_D = Res()
        lsp = X.alloc(1024, F32)
        r_lsp = Res()
        ex = [X.alloc(1024, F32) for _ in range(3)]
        r_ex = [Res() for _ in range(3)]
        kd = X.alloc(512)
        r_kd = Res()
        if mode == 'B':
            qt = X.alloc(512)
            kt = X.alloc(512)
            r_qt, r_kt = Res(), Res()
            qT = [X.alloc(256) for _ in range(2)]
            kT = [X.alloc(256) for _ in range(2)]
            r_qT, r_kT = [Res(), Res()], [Res(), Res()]
            AT = [X.alloc(512) for _ in range(2)]
            r_AT = [Res(), Res()]
            og = X.alloc(1024)
            r_og = Res()
            osb = X.alloc(2048, F32)
            r_osb = Res()
            self.load_gain(self.dram['L%d_g_gla_norm' % l], 512)
        if mode == 'A':
            for hf in range(2):
                P.op('pool', lambda e, hf=hf: e.memset(S[hf][:, :], 0.0), writes=[r_S[hf]])
            P.op('pool', lambda e: e.memset(Dt[:, :], 1.0), writes=[r_D])
        else:
            tmpL = X.alloc(512, F32)
            r_tmpL = Res()
            coef, r_coef = self.scal()
            for hf in range(2):
                P.op('pool', lambda e, hf=hf: e.memset(S[hf][:, :], 0.0), writes=[r_S[hf]])
                for i in range(4):
                    P.op('dve', lambda e, hf=hf, i=i: e.tensor_scalar(out=coef, in0=gs[:, i, 256 + hf:257 + hf], scalar1=self.ctl[:, 4 + i:5 + i], scalar2=self.ctl[:, 8 + i:9 + i], op0=ALU.mult, op1=ALU.add),
                         reads=[r_gs], writes=[r_coef])
                    P.op('dve', lambda e, hf=hf, i=i: e.tensor_scalar(out=tmpL[:, :], in0=gs[:, i, hf * 128:(hf + 1) * 128], scalar1=self.ctl[:, 4 + i:5 + i], scalar2=None, op0=ALU.mult),
                         reads=[r_gs], writes=[r_tmpL])
                    P.op('dve', lambda e, hf=hf: e.scalar_tensor_tensor(out=S[hf][:, :], in0=S[hf][:, :], scalar=coef, in1=tmpL[:, :], op0=ALU.mult, op1=ALU.add),
                         reads=[r_coef, r_tmpL, r_S[hf]], writes=[r_S[hf]])
        for n in range(NTB):
            px = self.getps()

            def mmx(e, n=n, px=px):
                e.matmul(self.ps[px][:, 0:256], lhsT=alT[0:16, n * 128:(n + 1) * 128], rhs=self.wa2[0:16, 0:256], start=True, stop=False)
                return e.matmul(self.ps[px][:, 0:256], lhsT=self.ones[0:1, 0:128], rhs=self.ba[0:1, 0:256], start=False, stop=True)
            P.op('pe', mmx, reads=[r_al, self.r_lc], writes=[self.r_ps[px]])
            P.op('act', lambda e, px=px: e.activation(out=lsp[:, :], in_=self.ps[px][:, 0:256], func=AF.Exp, scale=-1.0), reads=[self.r_ps[px]], writes=[r_lsp])
            P.op('act', lambda e: e.activation(out=lsp[:, :], in_=lsp[:, :], func=AF.Ln, bias=1.0), reads=[r_lsp], writes=[r_lsp])
            pb = self.getps()

            def mmb(e, pb=pb):
                e.matmul(self.ps[pb][:, 0:256], lhsT=M1, rhs=lsp[:, :], start=True, stop=True)
                return e.matmul(self.ps[pb][:, 256:512], lhsT=M2, rhs=lsp[:, :], start=True, stop=True)
            P.op('pe', mmb, reads=[r_lsp], writes=[self.r_ps[pb]])
            pc = self.getps()

            def mmc(e, pc=pc):
                e.matmul(self.ps[pc][:, 0:2], lhsT=lsp[:, 0:128], rhs=M3[:, 0:2], start=True, stop=True)
                return e.matmul(self.ps[pc][:, 2:4], lhsT=lsp[:, 128:256], rhs=M3[:, 0:2], start=True, stop=True)
            P.op('pe', mmc, reads=[r_lsp], writes=[self.r_ps[pc]])
            cd, r_cd = self.scal()
            cd2, r_cd2 = self.scal()
            P.op('act', lambda e, pc=pc, cd=cd: e.activation(out=cd, in_=self.ps[pc][:, 0:1], func=AF.Exp), reads=[self.r_ps[pc]], writes=[r_cd])
            P.op('act', lambda e, pc=pc, cd2=cd2: e.activation(out=cd2, in_=self.ps[pc][:, 2:3], func=AF.Exp), reads=[self.r_ps[pc]], writes=[r_cd2])
            cds = [(cd, r_cd), (cd2, r_cd2)]
            P.op('act', lambda e, pb=pb: e.activation(out=ex[2][:, :], in_=self.ps[pb][:, 256:512], func=AF.Exp), reads=[self.r_ps[pb]], writes=[r_ex[2]])
            P.op('dve', lambda e, n=n: e.tensor_tensor(out=kd[:, :], in0=qk[:, n, 256:512], in1=ex[2][:, :], op=ALU.mult), reads=[r_qk[n], r_ex[2]], writes=[r_kd])
            if mode == 'B':
                P.op('act', lambda e, pb=pb: e.activation(out=ex[0][:, :], in_=self.ps[pb][:, 0:256], func=AF.Exp), reads=[self.r_ps[pb]], writes=[r_ex[0]])
                P.op('act', lambda e, pb=pb: e.activation(out=ex[1][:, :], in_=self.ps[pb][:, 0:256], func=AF.Exp, scale=-1.0), reads=[self.r_ps[pb]], writes=[r_ex[1]])
                P.op('dve', lambda e, n=n: e.scalar_tensor_tensor(out=qt[:, :], in0=qk[:, n, 0:256], scalar=0.125, in1=ex[0][:, :], op0=ALU.mult, op1=ALU.mult),
                     reads=[r_qk[n], r_ex[0]], writes=[r_qt])
                P.op('dve', lambda e, n=n: e.tensor_tensor(out=kt[:, :], in0=qk[:, n, 256:512], in1=ex[1][:, :], op=ALU.mult), reads=[r_qk[n], r_ex[1]], writes=[r_kt])
                ptq = self.getps()
                ptv = self.ps[ptq][:, :].bitcast(BF16)

                def trq(e, ptv=ptv):
                    e.transpose(out=ptv[:, 0:128], in_=qt[:, 0:128], identity=self.ident)
                    e.transpose(out=ptv[:, 128:256], in_=qt[:, 128:256], identity=self.ident)
                    e.transpose(out=ptv[:, 256:384], in_=kt[:, 0:128], identity=self.ident)
                    return e.transpose(out=ptv[:, 384:512], in_=kt[:, 128:256], identity=self.ident)
                P.op('pe', trq, reads=[r_qt, r_kt], writes=[self.r_ps[ptq]])
                for hf in range(2):
                    P.op('act', lambda e, hf=hf, ptv=ptv: e.copy(out=qT[hf][:, :], in_=ptv[:, hf * 128:(hf + 1) * 128]), reads=[self.r_ps[ptq]], writes=[r_qT[hf]])
                    P.op('act', lambda e, hf=hf, ptv=ptv: e.copy(out=kT[hf][:, :], in_=ptv[:, 256 + hf * 128:256 + (hf + 1) * 128]), reads=[self.r_ps[ptq]], writes=[r_kT[hf]])
                pa = [self.getps(), self.getps()]
                for e_ in range(2):
                    def mma(e, e_=e_, pa=pa):
                        ins = None
                        for hf in range(2):
                            ins = e.matmul(self.ps[pa[e_]][:, hf * 128:(hf + 1) * 128], lhsT=kT[hf][e_ * 64:(e_ + 1) * 64, :], rhs=qT[hf][e_ * 64:(e_ + 1) * 64, :],
                                           start=True, stop=True)
                        return ins
                    P.op('pe', mma, reads=r_qT + r_kT, writes=[self.r_ps[pa[e_]]])
                    for hf in range(2):
                        P.op('dve', lambda e, e_=e_, hf=hf, pa=pa: e.tensor_tensor(out=AT[e_][:, hf * 128:(hf + 1) * 128], in0=self.ps[pa[e_]][:, hf * 128:(hf + 1) * 128], in1=self.tri, op=ALU.mult),
                             reads=[self.r_ps[pa[e_]]], writes=[r_AT[e_]])
                for hf in range(2):
                    P.op('act', lambda e, hf=hf: e.copy(out=Sb[hf][:, :], in_=S[hf][:, :]), reads=[r_S[hf]], writes=[r_Sb[hf]])
                po = [self.getps(), self.getps()]
                for e_ in range(2):
                    def mmo(e, e_=e_, po=po, n=n):
                        ins = None
                        for hf in range(2):
                            hd = 2 * hf + e_
                            e.matmul(self.ps[po[e_]][:, hf * 128:(hf + 1) * 128], lhsT=AT[e_][:, hf * 128:(hf + 1) * 128], rhs=V[:, n, hd * 128:(hd + 1) * 128], start=True, stop=False)
                            ins = e.matmul(self.ps[po[e_]][:, hf * 128:(hf + 1) * 128], lhsT=qT[hf][e_ * 64:(e_ + 1) * 64, :], rhs=Sb[hf][e_ * 64:(e_ + 1) * 64, :], start=False, stop=True)
                        return ins
                    P.op('pe', mmo, reads=[r_AT[e_], r_V[n]] + r_qT + r_Sb, writes=[self.r_ps[po[e_]]])
                for e_ in range(2):
                    for hf in range(2):
                        hd = 2 * hf + e_
                        src = self.ps[po[e_]][:, hf * 128:(hf + 1) * 128]
                        ss, r_ss = self.scal()
                        rs, r_rs = self.scal()
                        P.op('act', lambda e, src=src, ss=ss, hd=hd: e.activation(out=osb[:, hd * 128:(hd + 1) * 128], in_=src, func=AF.Square, accum_out=ss),
                             reads=[], writes=[r_osb, r_ss, self.r_ps[po[e_]]])
                        P.op('dve', lambda e, ss=ss, rs=rs: e.tensor_scalar(out=rs, in0=ss, scalar1=1.0 / 128, scalar2=EPS, op0=ALU.mult, op1=ALU.add), reads=[r_ss], writes=[r_rs])
                        P.op('act', lambda e, rs=rs: e.activation(out=rs, in_=rs, func=AF.Sqrt), reads=[r_rs], writes=[r_rs])
                        P.op('dve', lambda e, rs=rs: e.reciprocal(out=rs, in_=rs), reads=[r_rs], writes=[r_rs])
                        P.op('dve', lambda e, src=src, rs=rs, hd=hd: e.scalar_tensor_tensor(out=osb[:, hd * 128:(hd + 1) * 128], in0=src, scalar=rs, in1=self.gb[:, hd * 128:(hd + 1) * 128], op0=ALU.mult, op1=ALU.mult),
                             reads=[r_rs, self.r_gb], writes=[r_osb, self.r_ps[po[e_]]])
                P.op('dve', lambda e, n=n: e.tensor_tensor(out=og[:, :], in0=osb[:, :], in1=sr[:, n, :], op=ALU.mult), reads=[r_osb, r_sr[n]], writes=[r_og])
                pt2 = self.getps()
                ptw = self.ps[pt2][:, :].bitcast(BF16)

                def tro(e, ptw=ptw):
                    ins = None
                    for j in range(4):
                        ins = e.transpose(out=ptw[:, j * 128:(j + 1) * 128], in_=og[:, j * 128:(j + 1) * 128], identity=self.ident)
                    return ins
                P.op('pe', tro, reads=[r_og], writes=[self.r_ps[pt2]])
                P.op('act', lambda e, n=n, ptw=ptw: e.copy(out=oglaT[:, 0:4, n * 128:(n + 1) * 128], in_=ptw[:, 0:512].rearrange('p (a b) -> p a b', b=128)),
                     reads=[self.r_ps[pt2]], writes=[r_ogla])
            for hf in range(2):
                pp = self.getps()
                P.op('pe', lambda e, hf=hf, pp=pp, n=n: e.matmul(self.ps[pp][:, 0:256], lhsT=kd[:, hf * 128:(hf + 1) * 128], rhs=V[:, n, hf * 256:(hf + 1) * 256], start=True, stop=True),
                     reads=[r_kd, r_V[n]], writes=[self.r_ps[pp]])
                cdh, r_cdh = cds[hf]
                for e_ in range(2):
                    P.op('dve', lambda e, hf=hf, e_=e_, pp=pp, cdh=cdh: e.scalar_tensor_tensor(out=S[hf][e_ * 64:(e_ + 1) * 64, :], in0=S[hf][e_ * 64:(e_ + 1) * 64, :], scalar=cdh[e_ * 64:(e_ + 1) * 64, :],
                                                                                             in1=self.ps[pp][e_ * 64:(e_ + 1) * 64, e_ * 128:(e_ + 1) * 128], op0=ALU.mult, op1=ALU.add),
                         reads=[self.r_ps[pp], r_cdh, r_S[hf]] + ([r_Sb[hf]] if mode == 'B' else []), writes=[r_S[hf]])
                if mode == 'A':
                    P.op('dve', lambda e, hf=hf, cdh=cdh: e.tensor_tensor(out=Dt[:, hf:hf + 1], in0=Dt[:, hf:hf + 1], in1=cdh, op=ALU.mult), reads=[r_cdh, r_D], writes=[r_D])
        if mode == 'A':
            for hf in range(2):
                P.dma('sp', xs_out[:, hf * 128:(hf + 1) * 128], S[hf][:, :], reads=[r_S[hf]], writes=[self.r_xsd], key='xsS%d' % hf)
            P.dma('sp', xs_out[:, 256:258], Dt[:, 0:2], reads=[r_D], writes=[self.r_xsd], key='xsD')
        P.barrier()

    def phase_A_exchange(self, l, xk_out, xs_out):
        P = self.P
        X = self.RX
        X.reset()
        ckT = v3(X.alloc(4 * T * 2), T)
        r_ck = Res()
        self.latent(l, 'wkv', 'g_mla_kv_norm', ckT, r_ck, X)
        for kc in range(4):
            P.dma('sp', xk_out[kc], ckT[:, kc, :], reads=[r_ck], writes=[self.r_xkd], key='xk%d' % kc)
        rope = X.alloc(2 * T * 4, F32)
        r_rope = Res()
        P.dma('sp', rope[0:64, :], self.d_rope, writes=[r_rope], key='rope')
        kpe = X.alloc(T * 2)
        r_kpe = Res()
        t1 = X.alloc(2048, F32)
        t2 = X.alloc(2048, F32)
        r_t1, r_t2 = Res(), Res()
        hold = {}

        def epi_kr(ti, mi, tg, pi):
            if mi == 0:
                hold[tg] = pi
                P.op('dve', lambda e: e.tensor_tensor(out=t1[0:64, :], in0=self.ps[pi][0:64, :], in1=rope[0:64, tg * 512:(tg + 1) * 512], op=ALU.mult),
                     reads=[self.r_ps[pi], r_rope], writes=[r_t1])
            else:
                P.op('dve', lambda e: e.tensor_tensor(out=t2[0:64, :], in0=self.ps[pi][0:64, :], in1=rope[0:64, T + tg * 512:T + (tg + 1) * 512], op=ALU.mult),
                     reads=[self.r_ps[pi], r_rope], writes=[r_t2])
                P.op('dve', lambda e: e.tensor_tensor(out=kpe[0:64, tg * 512:(tg + 1) * 512], in0=t1[0:64, :], in1=t2[0:64, :], op=ALU.add),
                     reads=[r_t1, r_t2], writes=[r_kpe])
        for tg in range(2):
            self.lin_fm(self.uT, lambda tg_: self.r_uT[tg_ * 4:(tg_ + 1) * 4], KC, self.dram['L%d_wkr' % l], 128, [(0, 64), (64, 64)], epi_kr, tgs=(tg,))
        P.dma('sp', xk_out[4][0:64, :], kpe[0:64, :], reads=[r_kpe], writes=[self.r_xkd], key='xk4')
        if self.cc_after_xk is not None:
            self.cc_after_xk()
        zt = X.alloc(64, F32)
        cgs = X.alloc(64, F32)
        r_zt, r_cgs = Res(), Res()
        wc = self.dram['L%d_wconv' % l]
        for part, base in (('cg', 2), ('hin', 4)):
            for k in range(2):
                wt, r_wt = self.wload(wc[base + k])
                wv = v3(wt[:, 0:KC * 256], 256)
                for sub in range(2):
                    ch = k * 2 + sub
                    pi = self.getps()

                    def mm(e, wv=wv, sub=sub, pi=pi):
                        ins = None
                        for kc in range(KC):
                            ins = e.matmul(self.ps[pi][:, 0:2], lhsT=wv[:, kc, sub * 128:(sub + 1) * 128], rhs=self.uT[:, kc, T - 2:T], start=(kc == 0), stop=(kc == KC - 1))
                        return ins
                    P.op('pe', mm, reads=[r_wt] + self.r_uT[4:8], writes=[self.r_ps[pi]])
                    if part == 'cg':
                        P.op('act', lambda e, ch=ch, pi=pi: e.copy(out=cgs[:, ch * 2:ch * 2 + 2], in_=self.ps[pi][:, 0:2]), reads=[self.r_ps[pi]], writes=[r_cgs])
                    else:
                        P.op('dve', lambda e, ch=ch, pi=pi: e.tensor_tensor(out=zt[:, ch * 2:ch * 2 + 2], in0=cgs[:, ch * 2:ch * 2 + 2], in1=self.ps[pi][:, 0:2], op=ALU.mult),
                             reads=[self.r_ps[pi], r_cgs], writes=[r_zt])
        P.dma('sp', xs_out[:, 258:266], zt[:, 0:8], reads=[r_zt], writes=[self.r_xsd], key='xsz')
        P.barrier()
        X.reset()
        self.gla(l, 'A', X, xs_out=xs_out)
        X.reset()

    def mla(self, l, S2, X, gk, ownk, omlaT, r_omla):
        P = self.P
        cqT = v3(S2.alloc(4 * T * 2), T)
        r_cq = Res()
        self.latent(l, 'wq', 'g_mla_q_norm', cqT, r_cq, X)
        rope = S2.alloc(2 * T * 4, F32)
        r_rope = Res()
        P.dma('sp', rope[0:64, :], self.d_rope, writes=[r_rope], key='rope')
        segbuf = [X.alloc(XK_COLS * 2) for _ in range(2)]
        r_segb = [Res(), Res()]
        segi = [0]
        KT = [X.alloc(T * 2) for _ in range(5)]
        VV = [v3(X.alloc(T * 2), 128) for _ in range(5)]
        KPE = [S2.alloc(T * 2) for _ in range(5)]
        r_K = [Res() for _ in range(5)]
        r_Vv = [Res() for _ in range(5)]
        r_KPE = [Res() for _ in range(5)]
        qT = [X.alloc(T * 2) for _ in range(2)]
        qpe = [X.alloc(T * 2) for _ in range(2)]
        r_q = [Res(), Res()]
        r_qpe = [Res(), Res()]
        PT = [X.alloc(1024) for _ in range(5)]
        r_PT = [Res() for _ in range(5)]
        t1 = X.alloc(2048, F32)
        t2 = X.alloc(2048, F32)
        r_t1, r_t2 = Res(), Res()
        rsb = X.alloc(2048, F32)
        r_rsb = Res()
        wm = self.dram['L%d_wmla' % l]
        pti = 0
        for hd in range(8):
            hb = hd % 2
            wt, r_wt = self.wload(wm[hd])
            wv = v3(wt[:, 0:4 * 512], 512)
            self.ps_set = [0, 1, 2, 3]
            for tg in range(2):
                pi = self.getps()

                def mmq(e, tg=tg, pi=pi, wv=wv):
                    ins = None
                    for kc in range(4):
                        ins = e.matmul(self.ps[pi][:, :], lhsT=wv[:, kc, 0:128], rhs=cqT[:, kc, tg * 512:(tg + 1) * 512], start=(kc == 0), stop=(kc == 3))
                    return ins
                P.op('pe', mmq, reads=[r_wt, r_cq], writes=[self.r_ps[pi]])
                P.op('act', lambda e, tg=tg, pi=pi, hb=hb: e.copy(out=qT[hb][:, tg * 512:(tg + 1) * 512], in_=self.ps[pi][:, :]), reads=[self.r_ps[pi]], writes=[r_q[hb]])
                for which in range(2):
                    pj = self.getps()

                    def mmr(e, tg=tg, pj=pj, wv=wv, which=which):
                        ins = None
                        for kc in range(4):
                            ins = e.matmul(self.ps[pj][0:64, :], lhsT=wv[:, kc, 128 + which * 64:192 + which * 64], rhs=cqT[:, kc, tg * 512:(tg + 1) * 512], start=(kc == 0), stop=(kc == 3))
                        return ins
                    P.op('pe', mmr, reads=[r_wt, r_cq], writes=[self.r_ps[pj]])
                    tt, r_tt = (t1, r_t1) if which == 0 else (t2, r_t2)
                    P.op('dve', lambda e, tg=tg, pj=pj, which=which, tt=tt: e.tensor_tensor(out=tt[0:64, :], in0=self.ps[pj][0:64, :], in1=rope[0:64, which * T + tg * 512:which * T + (tg + 1) * 512], op=ALU.mult),
                         reads=[self.r_ps[pj], r_rope], writes=[r_tt])
                P.op('dve', lambda e, tg=tg, hb=hb: e.tensor_tensor(out=qpe[hb][0:64, tg * 512:(tg + 1) * 512], in0=t1[0:64, :], in1=t2[0:64, :], op=ALU.add),
                     reads=[r_t1, r_t2], writes=[r_qpe[hb]])
            for s in range(5):
                sb_ = segi[0] % 2
                segi[0] += 1
                for j5 in range(5):
                    srcd = ownk[j5] if s == 0 else gk[j5][s - 1]
                    P.dma('sp', segbuf[sb_][:, j5 * 1024:(j5 + 1) * 1024], srcd, reads=[self.r_gk, self.r_xkd], writes=[r_segb[sb_]], key='seg%d' % sb_)
                if hd == 0:
                    P.op('pool', lambda e, s=s, sb_=sb_: e.tensor_copy(out=KPE[s][0:64, :], in_=segbuf[sb_][0:64, 4096:5120]), reads=[r_segb[sb_]], writes=[r_KPE[s]])
                sv = v3(segbuf[sb_][:, 0:4096], T)
                r_sg_ = r_segb[sb_]
                for ktg in range(2):
                    pi = self.getps()

                    def mmk(e, sv=sv, ktg=ktg, pi=pi, wv=wv):
                        ins = None
                        for kc in range(4):
                            ins = e.matmul(self.ps[pi][:, :], lhsT=wv[:, kc, 256:384], rhs=sv[:, kc, ktg * 512:(ktg + 1) * 512], start=(kc == 0), stop=(kc == 3))
                        return ins
                    P.op('pe', mmk, reads=[r_wt, r_sg_], writes=[self.r_ps[pi]])
                    eng = 'act' if ktg == 0 else 'dve'
                    P.op(eng, lambda e, s=s, ktg=ktg, pi=pi, eng=eng: (e.copy if eng == 'act' else e.tensor_copy)(out=KT[s][:, ktg * 512:(ktg + 1) * 512], in_=self.ps[pi][:, :]),
                         reads=[self.r_ps[pi]], writes=[r_K[s]])
                for kb4 in range(2):
                    pi = self.getps()

                    def mmv(e, sv=sv, kb4=kb4, pi=pi, wv=wv):
                        ins = None
                        for j in range(4):
                            kb = kb4 * 4 + j
                            for kc in range(4):
                                ins = e.matmul(self.ps[pi][:, j * 128:(j + 1) * 128], lhsT=sv[:, kc, kb * 128:(kb + 1) * 128], rhs=wv[:, kc, 384:512], start=(kc == 0), stop=(kc == 3))
                        return ins
                    P.op('pe', mmv, reads=[r_wt, r_sg_], writes=[self.r_ps[pi]])
                    eng = 'dve' if kb4 == 0 else 'act'
                    P.op(eng, lambda e, s=s, kb4=kb4, pi=pi, eng=eng: (e.copy if eng == 'act' else e.tensor_copy)(out=VV[s][:, kb4 * 4:(kb4 + 1) * 4, :], in_=self.ps[pi][:, :].rearrange('p (a b) -> p a b', b=128)),
                         reads=[self.r_ps[pi]], writes=[r_Vv[s]])
            for tg in range(2):
                units = []
                for s in range(5):
                    for kb in range(8):
                        if s == 0 and kb > 4 * tg + 3:
                            continue
                        r = (kb - 4 * tg) if (s == 0 and kb >= 4 * tg) else 0
                        units.append((s, kb, r, s == 0 and kb >= 4 * tg))
                self.ps_set = [0, 1, 2, 3, 4, 5]
                sp_of = {}

                def emit_scores(u):
                    s, kb, r, diag = units[u]
                    pi = self.getps()
                    sp_of[u] = pi
                    c0 = r * 128

                    def mms(e, s=s, kb=kb, c0=c0, pi=pi, tg=tg, hb=hb):
                        e.matmul(self.ps[pi][:, c0:512], lhsT=KT[s][:, kb * 128:(kb + 1) * 128], rhs=qT[hb][:, tg * 512 + c0:(tg + 1) * 512], start=True, stop=False)
                        return e.matmul(self.ps[pi][:, c0:512], lhsT=KPE[s][0:64, kb * 128:(kb + 1) * 128], rhs=qpe[hb][0:64, tg * 512 + c0:(tg + 1) * 512], start=False, stop=True)
                    P.op('pe', mms, reads=[r_K[s], r_KPE[s], r_q[hb], r_qpe[hb]], writes=[self.r_ps[pi]])

                def emit_pv(u, pt, first, last):
                    s, kb, r, diag = units[u]
                    c0 = r * 128

                    def mmp(e, s=s, kb=kb, c0=c0, pt=pt, first=first, last=last, hb=hb):
                        e.matmul(self.ps[6][:, c0:512], lhsT=VV[s][:, kb, :], rhs=PT[pt][:, c0:512], start=first, stop=last)
                        return e.matmul(self.ps[7][:, c0:512], lhsT=self.ones, rhs=PT[pt][:, c0:512], start=first, stop=last)
                    P.op('pe', mmp, reads=[r_Vv[s], r_PT[pt]], writes=[self.r_ps[6], self.r_ps[7]])
                LA = 4
                for u0 in range(min(LA, len(units))):
                    emit_scores(u0)
                for u in range(len(units)):
                    s, kb, r, diag = units[u]
                    c0 = r * 128
                    pi = sp_of[u]
                    pt = pti % 5
                    pti += 1
                    if s == 0:
                        P.op('act', lambda e, pi=pi, pt=pt, c0=c0: e.activation(out=PT[pt][:, c0:512], in_=self.ps[pi][:, c0:512], func=AF.Exp, scale=SCALE_MLA),
                             reads=[self.r_ps[pi]], writes=[r_PT[pt]])
                    else:
                        P.op('act', lambda e, pi=pi, pt=pt, s=s: e.activation(out=PT[pt][:, :], in_=self.ps[pi][:, :], func=AF.Exp, scale=SCALE_MLA, bias=self.ctl[:, s - 1:s]),
                             reads=[self.r_ps[pi]], writes=[r_PT[pt]])
                    if diag:
                        P.op('pool', lambda e, pt=pt, c0=c0: e.tensor_tensor(out=PT[pt][:, c0:c0 + 128], in0=PT[pt][:, c0:c0 + 128], in1=self.tri, op=ALU.mult),
                             reads=[r_PT[pt]], writes=[r_PT[pt]])
                    if u + LA < len(units):
                        emit_scores(u + LA)
                    emit_pv(u, pt, u == 0, u == len(units) - 1)
                P.op('dve', lambda e: e.reciprocal(out=rsb[:, :], in_=self.ps[7][:, :]), reads=[self.r_ps[7]], writes=[r_rsb])
                P.op('dve', lambda e, tg=tg, hd=hd: e.tensor_tensor(out=omlaT[:, hd, tg * 512:(tg + 1) * 512], in0=self.ps[6][:, :], in1=rsb[:, :], op=ALU.mult),
                     reads=[self.r_ps[6], r_rsb], writes=[r_omla])
        self.ps_set = list(range(8))
        P.barrier()

    def conv(self, l, X, gs, r_gs, oconvT, r_oconv):
        P = self.P
        P.dma('sp', self.convw[:, 0:12], self.dram['L%d_convw' % l], writes=[self.r_lc], key='lc')
        z = X.alloc((T + 2) * 4 + 56, F32)
        y = X.alloc(T * 4, F32)
        cg = X.alloc(T * 4, F32)
        halo = X.alloc(64, F32)
        r_z, r_y, r_cg, r_halo = Res(), Res(), Res(), Res()
        for i in range(4):
            if i == 0:
                P.op('dve', lambda e, i=i: e.tensor_scalar(out=halo[:, 0:8], in0=gs[:, i, 258:266], scalar1=self.ctl[:, 12 + i:13 + i], scalar2=None, op0=ALU.mult), reads=[r_gs], writes=[r_halo])
            else:
                P.op('dve', lambda e, i=i: e.scalar_tensor_tensor(out=halo[:, 0:8], in0=gs[:, i, 258:266], scalar=self.ctl[:, 12 + i:13 + i], in1=halo[:, 0:8], op0=ALU.mult, op1=ALU.add),
                     reads=[r_gs, r_halo], writes=[r_halo])
        wc = self.dram['L%d_wconv' % l]
        for k in range(2):
            wts = [self.wload(wc[base + k]) for base in (2, 4, 0)]
            for sub in range(2):
                ch = 2 * k + sub
                for tg in range(2):
                    def mmfor(wt, r_wt, sub=sub, tg=tg):
                        pi = self.getps()
                        wv = v3(wt[:, 0:KC * 256], 256)

                        def mm(e, wv=wv, pi=pi, sub=sub, tg=tg):
                            ins = None
                            for kc in range(KC):
                                ins = e.matmul(self.ps[pi][:, :], lhsT=wv[:, kc, sub * 128:(sub + 1) * 128], rhs=self.uT[:, kc, tg * 512:(tg + 1) * 512], start=(kc == 0), stop=(kc == KC - 1))
                            return ins
                        P.op('pe', mm, reads=[r_wt] + self.r_uT[tg * 4:(tg + 1) * 4], writes=[self.r_ps[pi]])
                        return pi
                    pc = mmfor(*wts[0])
                    P.op('act', lambda e, pc=pc, tg=tg: e.copy(out=cg[:, tg * 512:(tg + 1) * 512], in_=self.ps[pc][:, :]), reads=[self.r_ps[pc]], writes=[r_cg])
                    ph = mmfor(*wts[1])
                    P.op('dve', lambda e, ph=ph, tg=tg: e.tensor_tensor(out=z[:, 2 + tg * 512:2 + (tg + 1) * 512], in0=cg[:, tg * 512:(tg + 1) * 512], in1=self.ps[ph][:, :], op=ALU.mult),
                         reads=[self.r_ps[ph], r_cg], writes=[r_z])
                P.op('dve', lambda e, ch=ch: e.tensor_copy(out=z[:, 0:2], in_=halo[:, ch * 2:ch * 2 + 2]), reads=[r_halo], writes=[r_z])
                cw = self.convw
                P.op('dve', lambda e, ch=ch: e.tensor_scalar(out=y[:, :], in0=z[:, 2:T + 2], scalar1=cw[:, ch * 3 + 2:ch * 3 + 3], scalar2=None, op0=ALU.mult), reads=[r_z, self.r_lc], writes=[r_y])
                P.op('dve', lambda e, ch=ch: e.scalar_tensor_tensor(out=y[:, :], in0=z[:, 1:T + 1], scalar=cw[:, ch * 3 + 1:ch * 3 + 2], in1=y[:, :], op0=ALU.mult, op1=ALU.add), reads=[r_z, r_y], writes=[r_y])
                P.op('dve', lambda e, ch=ch: e.scalar_tensor_tensor(out=y[:, :], in0=z[:, 0:T], scalar=cw[:, ch * 3:ch * 3 + 1], in1=y[:, :], op0=ALU.mult, op1=ALU.add), reads=[r_z, r_y], writes=[r_y])
                for tg in range(2):
                    pb = mmfor(*wts[2], sub=sub, tg=tg)
                    P.op('dve', lambda e, pb=pb, tg=tg, ch=ch: e.tensor_tensor(out=oconvT[:, ch, tg * 512:(tg + 1) * 512], in0=y[:, tg * 512:(tg + 1) * 512], in1=self.ps[pb][:, :], op=ALU.mult),
                         reads=[self.r_ps[pb], r_y], writes=[r_oconv])
        P.barrier()

    def merge(self, l, X, omlaT, r_omla, oglaT, r_ogla, oconvT, r_oconv, mT, r_mT):
        P = self.P
        P.dma('sp', self.bgate[:, 0:48], self.dram['L%d_bgate' % l], writes=[self.r_lc], key='lc')
        sig = [X.alloc(2048, F32) for _ in range(2)]
        acc = [X.alloc(2048, F32) for _ in range(2)]
        r_sig = [Res(), Res()]
        r_acc = [Res(), Res()]
        wmg = self.dram['L%d_wmerge' % l]
        srcs = [(omlaT, r_omla, 8), (oglaT, r_ogla, 4), (oconvT, r_oconv, 4)]
        k = 0
        for ncx in range(16):
            w0, r_w0 = self.wload(wmg[ncx * 2])
            w1, r_w1 = self.wload(wmg[ncx * 2 + 1])
            v0 = v3(w0[:, 0:4096], 128)
            v1 = v3(w1[:, 0:4096], 128)
            gate_w = [(v0, 0, r_w0), (v0, 16, r_w0), (v1, 0, r_w1)]
            proj_off = [16, 24, 28]
            for tg in range(2):
                a = k % 2
                k += 1
                for br in range(3):
                    gv, goff, r_g = gate_w[br]
                    pg = self.getps()

                    def mmg(e, gv=gv, goff=goff, pg=pg, tg=tg):
                        ins = None
                        for kc in range(KC):
                            ins = e.matmul(self.ps[pg][:, :], lhsT=gv[:, goff + kc, :], rhs=self.uT[:, kc, tg * 512:(tg + 1) * 512], start=(kc == 0), stop=(kc == KC - 1))
                        return ins
                    P.op('pe', mmg, reads=[r_g] + self.r_uT[tg * 4:(tg + 1) * 4], writes=[self.r_ps[pg]])
                    sb = br % 2
                    P.op('act', lambda e, pg=pg, sb=sb, br=br, ncx=ncx: e.activation(out=sig[sb][:, :], in_=self.ps[pg][:, :], func=AF.Sigmoid, bias=self.bgate[:, br * 16 + ncx:br * 16 + ncx + 1]),
                         reads=[self.r_ps[pg], self.r_lc], writes=[r_sig[sb]])
                    src, r_src, nk = srcs[br]
                    py = self.getps()

                    def mmy(e, src=src, nk=nk, po=proj_off[br], py=py, tg=tg, v1=v1):
                        ins = None
                        for kk in range(nk):
                            ins = e.matmul(self.ps[py][:, :], lhsT=v1[:, po + kk, :], rhs=src[:, kk, tg * 512:(tg + 1) * 512], start=(kk == 0), stop=(kk == nk - 1))
                        return ins
                    P.op('pe', mmy, reads=[r_w1, r_src], writes=[self.r_ps[py]])
                    if br == 0:
                        P.op('dve', lambda e, py=py, sb=sb, a=a: e.tensor_tensor(out=acc[a][:, :], in0=sig[sb][:, :], in1=self.ps[py][:, :], op=ALU.mult),
                             reads=[self.r_ps[py], r_sig[sb]], writes=[r_acc[a]])
                    else:
                        P.op('dve', lambda e, py=py, sb=sb: e.tensor_tensor(out=sig[sb][:, :], in0=sig[sb][:, :], in1=self.ps[py][:, :], op=ALU.mult),
                             reads=[self.r_ps[py], r_sig[sb]], writes=[r_sig[sb]])
                        if br == 1:
                            P.op('dve', lambda e, sb=sb, a=a: e.tensor_tensor(out=acc[a][:, :], in0=acc[a][:, :], in1=sig[sb][:, :], op=ALU.add), reads=[r_sig[sb], r_acc[a]], writes=[r_acc[a]])
                        else:
                            P.op('dve', lambda e, sb=sb, a=a, ncx=ncx, tg=tg: e.tensor_tensor(out=mT[:, ncx, tg * 512:(tg + 1) * 512], in0=acc[a][:, :], in1=sig[sb][:, :], op=ALU.add),
                                 reads=[r_sig[sb], r_acc[a]], writes=[r_mT[tg]])
        P.barrier()

    def proj_residual(self, xT, r_x, wtiles):
        P = self.P

        def epi(ti, tb, pi):
            P.op('dve', lambda e: e.tensor_tensor(out=self.h[:, tb, ti * 256:(ti + 1) * 256], in0=self.ps[pi][:, 0:256], in1=self.h[:, tb, ti * 256:(ti + 1) * 256], op=ALU.add),
                 reads=[self.r_ps[pi], self.r_h[tb]], writes=[self.r_h[tb]])
        self.lin_tok(xT, r_x, KC, wtiles, 256, epi)
        P.barrier()

    def xattn(self, l, mem_d):
        P = self.P
        X = self.RX
        X.reset()
        self.norm_h(self.dram['L%d_g_xattn_norm' % l])
        memT = v3(X.alloc(KC * MEM * 2), MEM)
        r_memT = [Res(), Res()]
        KTm = v3(X.alloc(KC * MEM * 2), MEM)
        r_KTm = Res()
        Vm = v3(X.alloc(2 * D * 2), D)
        r_Vm = [Res(), Res()]
        save = X.off
        mm_ = v3(X.alloc(2 * D * 4, F32), D)
        r_mm = [Res(), Res()]
        for mb in range(2):
            P.dma('sp', mm_[:, mb, :], mem_d[mb * 128:(mb + 1) * 128, :], writes=[r_mm[mb]], key='mem%d' % mb)
        ub = [X.alloc(4096), X.alloc(4096)]
        r_ub = [Res(), Res()]
        self.load_gain(self.dram['L%d_g_mem_norm' % l])
        self.norm_T(lambda tb: mm_[:, tb, :], lambda tb: r_mm[tb], D, memT, lambda tb: r_memT[tb], 2, ub, r_ub)
        P.barrier()
        X.off = save
        qx = v3(X.alloc(KC * T * 2), T)
        r_qx = [[Res() for _ in range(2)] for _ in range(KC)]
        PTm = [X.alloc(1024) for _ in range(4)]
        r_PTm = [Res() for _ in range(4)]
        rsb = X.alloc(2048, F32)
        r_rsb = Res()

        def epi_k(ti, mi, tg, pi):
            ch = ti * 2 + mi
            P.op('act' if ch % 2 == 0 else 'dve', lambda e: (e.copy if ch % 2 == 0 else e.tensor_copy)(out=KTm[:, ch, :], in_=self.ps[pi][:, 0:256]),
                 reads=[self.r_ps[pi]], writes=[r_KTm])
        self.lin_fm(memT, lambda tg: r_memT, KC, self.dram['L%d_xk' % l], 256, [(0, 128), (128, 128)], epi_k, tgs=(0,), ncol_T=256)

        def epi_v(ti, tb, pi):
            P.op('act' if tb == 0 else 'dve', lambda e: (e.copy if tb == 0 else e.tensor_copy)(out=Vm[:, tb, ti * 256:(ti + 1) * 256], in_=self.ps[pi][:, 0:256]),
                 reads=[self.r_ps[pi]], writes=[r_Vm[tb]])
        self.lin_tok(memT, lambda tb: [r_memT[tb]], KC, self.dram['L%d_xv' % l], 256, epi_v, ntb=2)

        def epi_q(ti, mi, tg, pi):
            ch = ti * 2 + mi
            P.op('act' if tg == 0 else 'dve', lambda e: (e.copy if tg == 0 else e.tensor_copy)(out=qx[:, ch, tg * 512:(tg + 1) * 512], in_=self.ps[pi][:, :]),
                 reads=[self.r_ps[pi]], writes=[r_qx[ch][tg]])
        self.lin_fm(self.uT, lambda tg: self.r_uT[tg * 4:(tg + 1) * 4], KC, self.dram['L%d_xq' % l], 256, [(0, 128), (128, 128)], epi_q)
        k = 0
        for hx in range(4):
            for tg in range(2):
                pts = []
                for mb in range(2):
                    pi = self.getps()

                    def mms(e, mb=mb, pi=pi, hx=hx, tg=tg):
                        ins = None
                        for dc in range(4):
                            ins = e.matmul(self.ps[pi][:, :], lhsT=KTm[:, hx * 4 + dc, mb * 128:(mb + 1) * 128], rhs=qx[:, hx * 4 + dc, tg * 512:(tg + 1) * 512], start=(dc == 0), stop=(dc == 3))
                        return ins
                    P.op('pe', mms, reads=[r_KTm] + [r_qx[hx * 4 + dc][tg] for dc in range(4)], writes=[self.r_ps[pi]])
                    pt = k % 4
                    k += 1
                    pts.append(pt)
                    P.op('act', lambda e, pi=pi, pt=pt: e.activation(out=PTm[pt][:, :], in_=self.ps[pi][:, :], func=AF.Exp, scale=SCALE_X), reads=[self.r_ps[pi]], writes=[r_PTm[pt]])
                psu = self.getps()

                def mmsum(e, psu=psu, pts=pts):
                    e.matmul(self.ps[psu][:, :], lhsT=self.ones, rhs=PTm[pts[0]][:, :], start=True, stop=False)
                    return e.matmul(self.ps[psu][:, :], lhsT=self.ones, rhs=PTm[pts[1]][:, :], start=False, stop=True)
                P.op('pe', mmsum, reads=[r_PTm[pts[0]], r_PTm[pts[1]]], writes=[self.r_ps[psu]])
                P.op('dve', lambda e, psu=psu: e.reciprocal(out=rsb[:, :], in_=self.ps[psu][:, :]), reads=[self.r_ps[psu]], writes=[r_rsb])
                for dvc in range(4):
                    po = self.getps()

                    def mmo(e, po=po, pts=pts, hx=hx, dvc=dvc):
                        e.matmul(self.ps[po][:, :], lhsT=Vm[:, 0, hx * 512 + dvc * 128:hx * 512 + (dvc + 1) * 128], rhs=PTm[pts[0]][:, :], start=True, stop=False)
                        return e.matmul(self.ps[po][:, :], lhsT=Vm[:, 1, hx * 512 + dvc * 128:hx * 512 + (dvc + 1) * 128], rhs=PTm[pts[1]][:, :], start=False, stop=True)
                    P.op('pe', mmo, reads=r_Vm + [r_PTm[pts[0]], r_PTm[pts[1]]], writes=[self.r_ps[po]])
                    ch = hx * 4 + dvc
                    P.op('dve', lambda e, po=po, ch=ch, tg=tg: e.tensor_tensor(out=qx[:, ch, tg * 512:(tg + 1) * 512], in0=self.ps[po][:, :], in1=rsb[:, :], op=ALU.mult),
                         reads=[self.r_ps[po], r_rsb], writes=[r_qx[ch][tg]])
        P.barrier()
        allq = [r_qx[ch][tg] for ch in range(KC) for tg in range(2)]
        self.proj_residual(qx, lambda tb: allq, self.dram['L%d_xo' % l])
        X.reset()

    def phase_B_mixer(self, l, gk, ownk, gs_d, hspill):
        P = self.P
        X = self.RX
        X.reset()
        S2 = Region(self.arena, 0, 65536)
        omlaT = v3(S2.alloc(8 * T * 2), T)
        oglaT = v3(S2.alloc(4 * T * 2), T)
        oconvT = v3(S2.alloc(4 * T * 2), T)
        gs = v3(S2.alloc(4 * XS_COLS * 4 + 32, F32)[:, 0:4 * XS_COLS], XS_COLS)
        r_omla, r_ogla, r_oconv, r_gs = Res(), Res(), Res(), Res()
        for i in range(4):
            P.dma('sp', gs[:, i, :], gs_d[i], reads=[self.r_gsd], writes=[r_gs], key='gs')
        s2save = S2.off
        if 'skip_mla' not in self.dbg:
            self.mla(l, S2, X, gk, ownk, omlaT, r_omla)
        if 'omla' in self.dbg:
            self.dump('omla', omlaT[:, :, :], [r_omla], BF16)
        if 'stop_mla' in self.dbg:
            return True
        X.reset()
        S2.off = s2save
        self.gla(l, 'B', X, gs=gs, r_gs=r_gs, oglaT=oglaT, r_ogla=r_ogla)
        if 'ogla' in self.dbg:
            self.dump('ogla', oglaT[:, :, :], [r_ogla], BF16)
        if 'stop_gla' in self.dbg:
            return True
        X.reset()
        self.conv(l, X, gs, r_gs, oconvT, r_oconv)
        if 'oconv' in self.dbg:
            self.dump('oconv', oconvT[:, :, :], [r_oconv], BF16)
        if 'stop_conv' in self.dbg:
            return True
        X.reset()
        mT = v3(X.alloc(KC * T * 2), T)
        r_mT = [Res(), Res()]
        self.merge(l, X, omlaT, r_omla, oglaT, r_ogla, oconvT, r_oconv, mT, r_mT)
        self.load_h(hspill)
        self.proj_residual(mT, lambda tb: r_mT, self.dram['L%d_wout' % l])
        X.reset()

    def load_h(self, src):
        for tb in range(NTB):
            self.P.dma('sp', self.h[:, tb, :], src[tb * 128:(tb + 1) * 128, :], reads=[self.r_hsp], writes=[self.r_h[tb]], key='h%d' % tb)

    def store_h(self, dst, key='hs'):
        for tb in range(NTB):
            self.P.dma('sp', dst[tb * 128:(tb + 1) * 128, :], self.h[:, tb, :], reads=[self.r_h[tb]], writes=[self.r_hsp], key=key)

    def final_norm(self, g_d, y_d):
        P = self.P
        X = self.RX
        X.reset()
        self.load_gain(g_d)
        ob = [X.alloc(8192, F32), X.alloc(8192, F32)]
        r_ob = [Res(), Res()]
        junk = X.alloc(4096)
        r_junk = Res()
        for tb in range(NTB):
            ss, r_ss = self.scal()
            rs, r_rs = self.scal()
            b = tb % 2
            P.op('act', lambda e, tb=tb, ss=ss: e.activation(out=junk[:, :], in_=self.h[:, tb, :], func=AF.Square, accum_out=ss), reads=[self.r_h[tb]], writes=[r_junk, r_ss])
            P.op('dve', lambda e, ss=ss, rs=rs: e.tensor_scalar(out=rs, in0=ss, scalar1=1.0 / D, scalar2=EPS, op0=ALU.mult, op1=ALU.add), reads=[r_ss], writes=[r_rs])
            P.op('act', lambda e, rs=rs: e.activation(out=rs, in_=rs, func=AF.Sqrt), reads=[r_rs], writes=[r_rs])
            P.op('dve', lambda e, rs=rs: e.reciprocal(out=rs, in_=rs), reads=[r_rs], writes=[r_rs])
            P.op('dve', lambda e, tb=tb, b=b, rs=rs: e.scalar_tensor_tensor(out=ob[b][:, :], in0=self.h[:, tb, :], scalar=rs, in1=self.gb[:, :], op0=ALU.mult, op1=ALU.mult),
                 reads=[self.r_h[tb], r_rs, self.r_gb], writes=[r_ob[b]])
            P.dma('sp', y_d[tb * 128:(tb + 1) * 128, :], ob[b][:, :], reads=[r_ob[b]], writes=[r_ob[b]], key='y%d' % b)
        P.barrier()


def build_program(stages, fused=False, dbg=None):
    nc = bass.Bass("TRN2", target_bir_lowering=False)
    layers = sorted({int(s[1]) for s in stages})
    st = ExitStack()
    with st:
        B = Builder(nc, st, layers, SHAPES, dbg)
        P = B.P
        hin = nc.dram_tensor('hin', [T, D], F32, kind='ExternalInput').ap()
        hout = nc.dram_tensor('hout', [T, D], F32, kind='ExternalOutput').ap()
        mem = None
        if any(s[0] == 'B' for s in stages):
            mem = nc.dram_tensor('mem', [MEM, D], F32, kind='ExternalInput').ap()
        ends_with_A = stages[-1][0] == 'A'
        xk_out = xs_out = None
        if not fused:
            if ends_with_A:
                xk_out = nc.dram_tensor('xk', [128, XK_COLS], BF16, kind='ExternalOutput').ap()
                xs_out = nc.dram_tensor('xs', [128, XS_COLS], F32, kind='ExternalOutput').ap()
            if stages[0][0] == 'B':
                gk_ = nc.dram_tensor('gk', [4, 128, XK_COLS], BF16, kind='ExternalInput').ap()
                ownk_ = nc.dram_tensor('ownk', [128, XK_COLS], BF16, kind='ExternalInput').ap()
                gk = [gk_[:, :, j * 1024:(j + 1) * 1024] for j in range(5)]
                ownk = [ownk_[:, j * 1024:(j + 1) * 1024] for j in range(5)]
                gs_d = nc.dram_tensor('gs', [4, 128, XS_COLS], F32, kind='ExternalInput').ap()
        RG = [[0, 1, 2, 3], [4, 5, 6, 7]]
        if fused:
            hsp = nc.dram_tensor('hspill', [T, D], F32).ap()
            xk_i = {l: [nc.dram_tensor('xk%d_%d' % (l, j), [128, 1024], BF16) for j in range(5)] for l in layers}
            gk_i = {l: [nc.dram_tensor('gk%d_%d' % (l, j), [4 * 128, 1024], BF16) for j in range(5)] for l in layers}
            xs_i = {l: nc.dram_tensor('xs%d' % l, [128, XS_COLS], F32) for l in layers}
            gs_i = {l: nc.dram_tensor('gs%d' % l, [4 * 128, XS_COLS], F32) for l in layers}
        hsrc = hin
        first = True
        for sname in stages:
            l = int(sname[1])
            if sname[0] == 'A':
                if first:
                    B.load_h(hin)
                B.ffn(l, 1)
                B.norm_h(B.dram['L%d_g_mix_norm' % l])
                if 'h_ffn1' in B.dbg:
                    B.dump('h_ffn1', B.h[:, :, :], B.r_h)
                if fused:
                    def issue_xk(l=l):
                        for a, b in zip(xk_i[l], gk_i[l]):
                            def ccf(e, a=a, b=b):
                                return e.collective_compute("AllGather", ALU.bypass, replica_groups=RG, ins=[a.ap()], outs=[b.ap()])
                            P.custom('pool', ccf, 1, reads=[B.r_xkd], writes=[B.r_gk], key='ccx')
                    B.cc_after_xk = issue_xk
                    B.store_h(hsp)
                    B.phase_A_exchange(l, [t.ap() for t in xk_i[l]], xs_i[l].ap())
                    hsrc = hsp

                    def ccs(e, a=xs_i[l], b=gs_i[l]):
                        return e.collective_compute("AllGather", ALU.bypass, replica_groups=RG, ins=[a.ap()], outs=[b.ap()])
                    P.custom('pool', ccs, 1, reads=[B.r_xsd], writes=[B.r_gsd], key='ccs')
                    gk = [t.ap().rearrange('(s p) c -> s p c', p=128) for t in gk_i[l]]
                    gs_d = gs_i[l].ap().rearrange('(s p) c -> s p c', p=128)
                    ownk = [t.ap() for t in xk_i[l]]
                else:
                    B.phase_A_exchange(l, [xk_out[:, j * 1024:(j + 1) * 1024] for j in range(5)], xs_out)
                    B.store_h(hout)
                    hsrc = hout
            else:
                if first:
                    B.load_h(hin)
                    B.norm_h(B.dram['L%d_g_mix_norm' % l])
                if B.phase_B_mixer(l, gk, ownk, gs_d, hsrc):
                    break
                if 'h_mix' in B.dbg:
                    B.dump('h_mix', B.h[:, :, :], B.r_h)
                if 'stop_mix' in B.dbg:
                    break
                B.xattn(l, mem)
                if 'h_x' in B.dbg:
                    B.dump('h_x', B.h[:, :, :], B.r_h)
                if 'stop_x' in B.dbg:
                    break
                B.ffn(l, 2)
                if l == DEPTH - 1:
                    gfin = nc.dram_tensor('gfinal', [1, D], F32, kind='ExternalInput').ap()
                    B.final_norm(gfin, hout)
                elif sname == stages[-1]:
                    B.store_h(hout)
            first = False
        P.barrier(full=True)
        P.emit()
    return nc


def _input_names(nc):
    names = []
    for alloc in nc.allocations:
        if isinstance(alloc, mybir.MemoryLocationSet) and alloc.kind == 'ExternalInput':
            names.append(alloc.memorylocations[0].name)
    return names


_PROGS = {}


def _run(stages, per_core, dbg=None, fused=False):
    key = (tuple(stages), tuple(sorted(dbg)) if dbg else None, fused)
    if key not in _PROGS:
        _PROGS[key] = build_program(stages, fused=fused, dbg=dbg)
    nc = _PROGS[key]
    names = _input_names(nc)
    in_maps = [{n: pc[n] for n in names if n in pc} for pc in per_core]
    res = run_bass_kernel_spmd(nc, in_maps, core_ids=list(range(NCORES)))
    return res.results


def prepare(inputs):
    inp = {k: np.asarray(v) for k, v in inputs.items()}
    x = np.ascontiguousarray(inp['x'], dtype=np.float32).reshape(NCORES, T, D)
    mem = np.ascontiguousarray(inp['mem'], dtype=np.float32)
    base = {'cbf': const_bf(), 'cf32': const_f32(), 'gfinal': np.ascontiguousarray(inp['final_norm'][None, :], dtype=np.float32)}
    for l in range(DEPTH):
        for k, v in layer_weights(inp, l).items():
            base['L%d_%s' % (l, k)] = v
    per_core = []
    for c in range(NCORES):
        ctl, rope = core_tables(c)
        d = dict(base)
        d.update({'ctl': ctl, 'rope': rope, 'mem': mem[c // 4], 'hin': x[c]})
        per_core.append(d)
    return per_core


def kernel(**inputs):
    per_core = prepare(inputs)

    def exchange(res):
        for c in range(NCORES):
            b0 = (c // 4) * 4
            per_core[c]['hin'] = res[c]['hout']
            per_core[c]['gk'] = np.ascontiguousarray(np.stack([res[b0 + i]['xk'] for i in range(4)]))
            per_core[c]['gs'] = np.ascontiguousarray(np.stack([res[b0 + i]['xs'] for i in range(4)]))
            per_core[c]['ownk'] = res[c]['xk']

    if FUSED:
        r = _run(['A0', 'B0', 'A1', 'B1'], per_core, fused=True)
    else:
        r = _run(['A0'], per_core)
        exchange(r)
        r = _run(['B0', 'A1'], per_core)
        exchange(r)
        r = _run(['B1'], per_core)
    y = np.stack([r[c]['hout'] for c in range(NCORES)]).reshape(2, 4 * T, D)
    return y.astype(np.float32)
```

```python
import numpy as np
import ml_dtypes
from contextlib import ExitStack
import concourse.bass as bass
import concourse.mybir as mybir
from concourse.bass_utils import run_bass_kernel_spmd

F32 = mybir.dt.float32
BF16 = mybir.dt.bfloat16
AF = mybir.ActivationFunctionType
ALU = mybir.AluOpType

NCORES = 8
T = 1024
D = 2048
FF = 5632
KC = 16
NTB = 8
NHC = 44
HG = 4
CPG = 11
EPS = 1e-6
DEPTH = 2
MEM = 256
SCALE_MLA = 192 ** -0.5
SCALE_X = 512 ** -0.5
XK_COLS = 5120
XS_COLS = 266
NEG = -30000.0
FUSED = True

ENGS = ['pe', 'act', 'dve', 'pool', 'sp']


class Res:
    __slots__ = ('w', 'r')

    def __init__(self):
        self.w = None
        self.r = []


class Prog:
    def __init__(self, nc, stack):
        self.nc = nc
        self.stack = stack
        self.streams = {e: [] for e in ENGS}
        self.cnt = {e: 0 for e in ENGS}
        self.sem = {e: stack.enter_context(nc.semaphore('s_' + e)) for e in ENGS}
        self.waited = {e: {} for e in ENGS}
        self.semobj = {}
        self.dma_sems = {}
        self.pool_pending = None

    def _waits(self, eng, reads, writes):
        need = {}
        for r in reads:
            if r.w is not None:
                s, v = r.w
                if need.get(s, 0) < v:
                    need[s] = v
        for w in writes:
            if w.w is not None:
                s, v = w.w
                if need.get(s, 0) < v:
                    need[s] = v
            for (s, v) in w.r:
                if need.get(s, 0) < v:
                    need[s] = v
        out = []
        me = id(self.sem[eng])
        for s, v in need.items():
            if s == me and eng == 'pe':
                continue
            if self.waited[eng].get(s, 0) < v:
                self.waited[eng][s] = v
                out.append((self.semobj[s], v))
        return out

    def _tok(self, semh, v):
        self.semobj[id(semh)] = semh
        return (id(semh), v)

    def _mark(self, tok, reads, writes):
        for r in reads:
            r.r.append(tok)
        for w in writes:
            w.w = tok
            w.r = []

    def op(self, eng, fn, reads=(), writes=()):
        waits = self._waits(eng, reads, writes)
        if eng == 'pool' and self.pool_pending:
            for s, v in self.pool_pending:
                if s != id(self.sem['pool']) and self.waited['pool'].get(s, 0) < v:
                    self.waited['pool'][s] = v
                    waits.append((self.semobj[s], v))
            self.pool_pending = None
        self.cnt[eng] += 1
        semh = self.sem[eng]
        tok = self._tok(semh, self.cnt[eng])
        self.streams[eng].append((waits, fn, (semh, 1)))
        self._mark(tok, reads, writes)
        return tok

    def dma(self, q, out, in_, reads=(), writes=(), key='d'):
        waits = self._waits(q, reads, writes)
        if key not in self.dma_sems:
            self.dma_sems[key] = [self.stack.enter_context(self.nc.semaphore('d_' + key)), 0]
        ent = self.dma_sems[key]
        ent[1] += 16
        tok = self._tok(ent[0], ent[1])

        def fn(e, out=out, in_=in_):
            return e.dma_start(out=out, in_=in_)
        self.streams[q].append((waits, fn, (ent[0], 16)))
        self._mark(tok, reads, writes)
        return tok

    def custom(self, q, fn, inc, reads=(), writes=(), key='cc'):
        waits = self._waits(q, reads, writes)
        if key not in self.dma_sems:
            self.dma_sems[key] = [self.stack.enter_context(self.nc.semaphore('d_' + key)), 0]
        ent = self.dma_sems[key]
        ent[1] += inc
        tok = self._tok(ent[0], ent[1])
        self.streams[q].append((waits, fn, (ent[0], inc)))
        self._mark(tok, reads, writes)
        return tok

    def barrier(self, full=False):
        toks = [(id(self.sem[e]), self.cnt[e]) for e in ENGS if self.cnt[e] > 0]
        for k, ent in self.dma_sems.items():
            toks.append((id(ent[0]), ent[1]))
        for e in ENGS:
            if e == 'pool' and not full:
                self.pool_pending = toks
                continue
            out = []
            for s, v in toks:
                if s == id(self.sem[e]) and e == 'pe':
                    continue
                if self.waited[e].get(s, 0) < v:
                    self.waited[e][s] = v
                    out.append((self.semobj[s], v))
            if out:
                self.streams[e].append((out, None, None))

    def emit(self):
        nc = self.nc
        streams = self.streams

        def run(e, lst):
            for waits, fn, inc in lst:
                for s, v in waits:
                    e.wait_ge(s, v)
                if fn is not None:
                    ins = fn(e)
                    if inc is not None:
                        ins.then_inc(inc[0], inc[1])

        with nc.Block() as block:
            @block.tensor
            def _(e):
                run(e, streams['pe'])

            @block.scalar
            def _(e):
                run(e, streams['act'])

            @block.vector
            def _(e):
                run(e, streams['dve'])

            @block.gpsimd
            def _(e):
                run(e, streams['pool'])

            @block.sync
            def _(e):
                run(e, streams['sp'])


class Region:
    def __init__(self, arena, lo, hi):
        self.arena, self.lo, self.hi, self.off = arena, lo, hi, lo

    def reset(self):
        self.off = self.lo

    def alloc(self, nbytes, dtype=BF16):
        nbytes = (nbytes + 63) // 64 * 64
        assert self.off + nbytes <= self.hi, ('region overflow', self.off, nbytes, self.hi)
        a = self.arena[:, self.off // 2:(self.off + nbytes) // 2]
        self.off += nbytes
        if dtype != BF16:
            a = a.bitcast(dtype)
        return a


def v3(ap, b):
    return ap.rearrange('p (a b) -> p a b', b=b)


def tile_w(W, ncols):
    K, N = W.shape
    kc = K // 128
    t = W.reshape(kc, 128, N // ncols, ncols).transpose(2, 1, 0, 3)
    return np.ascontiguousarray(t).reshape(N // ncols, 128, kc * ncols)


def prep_ffn_w1(w_in):
    g = w_in[:, :FF].reshape(KC, 128, NHC, 128)
    u = w_in[:, FF:].reshape(KC, 128, NHC, 128)
    t = np.stack([g, u], axis=3).transpose(2, 1, 0, 3, 4)
    return np.ascontiguousarray(t).reshape(NHC, 128, KC * 256)


def prep_ffn_w2(w_out):
    t = w_out.reshape(HG, CPG, 128, 4, 512).transpose(0, 3, 2, 1, 4)
    return np.ascontiguousarray(t).reshape(HG * 4, 128, CPG * 512)


def layer_weights(inp, l):
    W = {}
    win = inp['mix_w_in'][l]
    W['f1w1'] = prep_ffn_w1(inp['ffn1_w_in'][l])
    W['f1w2'] = prep_ffn_w2(inp['ffn1_w_out'][l])
    W['f2w1'] = prep_ffn_w1(inp['ffn2_w_in'][l])
    W['f2w2'] = prep_ffn_w2(inp['ffn2_w_out'][l])
    W['wq'] = tile_w(win[:, 0:512], 256)
    W['wkv'] = tile_w(win[:, 512:1024], 256)
    kr = win[:, 1024:1088]
    W['wkr'] = tile_w(np.concatenate([kr, kr[:, 32:], kr[:, :32]], axis=1), 128)
    W['wgla'] = tile_w(win[:, 1088:2624], 256)
    W['walow'] = tile_w(win[:, 2624:2640], 16)
    W['wconv'] = tile_w(win[:, 2640:4176], 256)
    gates = win[:, 4176:]
    mt = np.zeros((16, 2, 128, 32, 128), np.float32)
    gm = gates[:, 0:2048].reshape(16, 128, 16, 128)
    gg = gates[:, 2048:4096].reshape(16, 128, 16, 128)
    gc = gates[:, 4096:6144].reshape(16, 128, 16, 128)
    pm = inp['mla_w_proj'][l].reshape(8, 128, 16, 128)
    pg = inp['gla_w_proj'][l].reshape(4, 128, 16, 128)
    pc = inp['conv_w_proj'][l].reshape(4, 128, 16, 128)
    mt[:, 0, :, 0:16] = gm.transpose(2, 1, 0, 3)
    mt[:, 0, :, 16:32] = gg.transpose(2, 1, 0, 3)
    mt[:, 1, :, 0:16] = gc.transpose(2, 1, 0, 3)
    mt[:, 1, :, 16:24] = pm.transpose(2, 1, 0, 3)
    mt[:, 1, :, 24:28] = pg.transpose(2, 1, 0, 3)
    mt[:, 1, :, 28:32] = pc.transpose(2, 1, 0, 3)
    W['wmerge'] = mt.reshape(32, 128, 32 * 128)
    W['wout'] = tile_w(inp['mix_w_out'][l], 256)
    W['xq'] = tile_w(inp['xattn_w_q'][l], 256)
    W['xk'] = tile_w(inp['xattn_w_kv'][l][:, :D], 256)
    W['xv'] = tile_w(inp['xattn_w_kv'][l][:, D:], 256)
    W['xo'] = tile_w(inp['xattn_w_o'][l], 256)
    uq = inp['mla_w_uq'][l].reshape(512, 8, 192)
    ukv = inp['mla_w_ukv'][l].reshape(512, 8, 256)
    hh = np.concatenate([uq[:, :, 0:128], uq[:, :, 128:192], uq[:, :, 160:192], uq[:, :, 128:160], ukv], axis=2)
    hh = hh.reshape(4, 128, 8, 512).transpose(2, 1, 0, 3)
    W['wmla'] = np.ascontiguousarray(hh).reshape(8, 128, 4 * 512)
    W['wa2'] = np.ascontiguousarray(inp['gla_w_a2'][l])
    W['ba'] = np.ascontiguousarray(inp['gla_b_a'][l][None, :])
    for nm in ('ffn1_norm', 'mix_norm', 'xattn_norm', 'mem_norm', 'ffn2_norm', 'mla_q_norm', 'mla_kv_norm', 'gla_norm'):
        W['g_' + nm] = np.ascontiguousarray(inp[nm][l][None, :])
    W['convw'] = np.ascontiguousarray(inp['conv_w'][l][:, 0, :].reshape(3, 4, 128).transpose(2, 1, 0)).reshape(128, 12)
    W['bgate'] = np.ascontiguousarray(inp['mix_b_gate'][l].reshape(3, 16, 128).transpose(2, 0, 1)).reshape(128, 48)
    return {k: np.ascontiguousarray(v, dtype=np.float32) for k, v in W.items()}


def const_bf():
    c = np.zeros((128, 384), np.float32)
    c[:, 0:128] = np.eye(128)
    c[:, 128:256] = 1.0
    c[:, 256:384] = np.triu(np.ones((128, 128)))
    return c.astype(ml_dtypes.bfloat16)


def const_f32():
    c = np.zeros((128, 384), np.float32)
    tri = np.triu(np.ones((128, 128), np.float32))
    c[:, 0:128] = -tri / 16.0
    c[:, 128:256] = -(1.0 - tri) / 16.0
    c[:, 256:384] = -1.0 / 16.0
    return c


def core_tables(c):
    j = c % 4
    ctl = np.zeros((128, 16), np.float32)
    for i in range(4):
        ctl[:, i] = 0.0 if i < j else NEG
        ctl[:, 4 + i] = 1.0 if i < j else 0.0
        ctl[:, 8 + i] = 0.0 if i < j else 1.0
        ctl[:, 12 + i] = 1.0 if i == j - 1 else 0.0
    inv_freq = (1.0 / (np.float32(10000.0) ** (np.arange(0, 64, 2, dtype=np.float32) / np.float32(64)))).astype(np.float32)
    pos = (np.arange(T, dtype=np.float32) + np.float32(j * T))
    ang = (pos[:, None] * inv_freq[None, :]).astype(np.float32)
    cos, sin = np.cos(ang).astype(np.float32).T, np.sin(ang).astype(np.float32).T
    rope = np.zeros((64, 2 * T), np.float32)
    rope[0:32, 0:T] = cos
    rope[32:64, 0:T] = cos
    rope[0:32, T:] = -sin
    rope[32:64, T:] = sin
    return ctl, rope


LAYER_KEYS = ['f1w1', 'f1w2', 'f2w1', 'f2w2', 'wq', 'wkv', 'wkr', 'wgla', 'walow', 'wconv', 'wmerge', 'wout', 'xq', 'xk', 'xv',
              'xo', 'wmla', 'wa2', 'ba', 'g_ffn1_norm', 'g_mix_norm', 'g_xattn_norm', 'g_mem_norm', 'g_ffn2_norm',
              'g_mla_q_norm', 'g_mla_kv_norm', 'g_gla_norm', 'convw', 'bgate']


class LazyDram(dict):
    def __init__(self, nc, shapes):
        super().__init__()
        self.nc, self.shapes = nc, shapes

    def __missing__(self, name):
        k = name.split('_', 1)[1]
        ap = self.nc.dram_tensor(name, list(self.shapes[k]), F32, kind='ExternalInput').ap()
        self[name] = ap
        return ap


SHAPES = {'f1w1': (44, 128, 4096), 'f2w1': (44, 128, 4096), 'f1w2': (16, 128, 5632), 'f2w2': (16, 128, 5632),
          'wq': (2, 128, 4096), 'wkv': (2, 128, 4096), 'wkr': (1, 128, 2048), 'wgla': (6, 128, 4096), 'walow': (1, 128, 256),
          'wconv': (6, 128, 4096), 'wmerge': (32, 128, 4096), 'wout': (8, 128, 4096), 'xq': (8, 128, 4096), 'xk': (8, 128, 4096),
          'xv': (8, 128, 4096), 'xo': (8, 128, 4096), 'wmla': (8, 128, 2048), 'wa2': (16, 256), 'ba': (1, 256),
          'g_ffn1_norm': (1, D), 'g_mix_norm': (1, D), 'g_xattn_norm': (1, D), 'g_mem_norm': (1, D), 'g_ffn2_norm': (1, D),
          'g_mla_q_norm': (1, 512), 'g_mla_kv_norm': (1, 512), 'g_gla_norm': (1, 512), 'convw': (128, 12), 'bgate': (128, 48)}


class Builder:
    def __init__(self, nc, st, layers, shapes, dbg=None):
        self.nc, self.st = nc, st
        self.P = Prog(nc, st)
        self.dbg = dbg or {}
        self.dram = LazyDram(nc, shapes)
        self.d_cbf = nc.dram_tensor('cbf', [128, 384], BF16, kind='ExternalInput').ap()
        self.d_cf32 = nc.dram_tensor('cf32', [128, 384], F32, kind='ExternalInput').ap()
        self.d_ctl = nc.dram_tensor('ctl', [128, 16], F32, kind='ExternalInput').ap()
        self.d_rope = nc.dram_tensor('rope', [64, 2 * T], F32, kind='ExternalInput').ap()
        TOTAL = 207 * 1024
        self.arena = st.enter_context(nc.sbuf_tensor('arena', [128, TOTAL // 2], BF16))
        P = self.P
        self.RH = Region(self.arena, 0, 65536)
        self.RU = Region(self.arena, 65536, 98304)
        self.RW = Region(self.arena, 98304, 98304 + 36864)
        self.RC = Region(self.arena, 135168, 135168 + 13312)
        self.RX = Region(self.arena, 148480, TOTAL)
        self.h = v3(self.RH.alloc(65536, F32), D)
        self.uT = v3(self.RU.alloc(32768), T)
        self.wsl = [self.RW.alloc(12288) for _ in range(3)]
        self.r_w = [Res() for _ in range(3)]
        self.wi = 0
        self.gb = self.RC.alloc(8192, F32)
        self.cbf = self.RC.alloc(768)
        self.cf32 = self.RC.alloc(1536, F32)
        self.ctl = self.RC.alloc(64, F32)
        self.small = self.RC.alloc(1024, F32)
        self.convw = self.RC.alloc(64, F32)
        self.bgate = self.RC.alloc(192, F32)
        self.wa2 = self.RC.alloc(512)
        self.ba = self.RC.alloc(512)
        self.ident = self.cbf[:, 0:128]
        self.ones = self.cbf[:, 128:256]
        self.tri = self.cbf[:, 256:384]
        self.r_gb = Res()
        self.r_hsp = Res()
        self.r_gk = Res()
        self.r_gsd = Res()
        self.r_xkd = Res()
        self.r_xsd = Res()
        self.cc_after_xk = None
        self.r_c = Res()
        self.r_lc = Res()
        self.r_small = [Res() for _ in range(256)]
        self.si = 0
        self.r_h = [Res() for _ in range(NTB)]
        self.r_uT = [Res() for _ in range(NTB)]
        self.ps = [st.enter_context(nc.psum_tensor('ps%d' % i, [128, 512], F32)) for i in range(8)]
        self.r_ps = [Res() for _ in range(8)]
        self.ps_set = list(range(8))
        self.psi = 0
        P.dma('sp', self.cbf[:, :], self.d_cbf, writes=[self.r_c], key='c')
        P.dma('sp', self.cf32[:, :], self.d_cf32, writes=[self.r_c], key='c')
        P.dma('sp', self.ctl[:, :], self.d_ctl, writes=[self.r_c], key='c')
        P.barrier()
        self.ndump = 0

    def getps(self):
        i = self.ps_set[self.psi % len(self.ps_set)]
        self.psi += 1
        return i

    def scal(self):
        i = self.si % 256
        self.si += 1
        return self.small[:, i:i + 1], self.r_small[i]

    def wload(self, src):
        i = self.wi % 3
        self.wi += 1
        n = src.shape[1]
        assert n * 2 <= 12288
        self.P.dma('pool', self.wsl[i][:, 0:n], src, writes=[self.r_w[i]], key='w%d' % i)
        return self.wsl[i], self.r_w[i]

    def dump(self, name, ap, reads, dtype=F32):
        d = self.nc.dram_tensor('dbg_' + name, list(ap.shape), dtype, kind='ExternalOutput').ap()
        self.P.dma('sp', d, ap, reads=reads, key='dbg')

    def load_gain(self, src, n=D):
        self.P.dma('sp', self.gb[:, 0:n], src[0, :].partition_broadcast(128), writes=[self.r_gb], key='gb')

    def scal_block(self, n):
        if self.si % 256 + n > 256:
            self.si += 256 - self.si % 256
        i = self.si % 256
        self.si += n
        return self.small[:, i:i + n], [self.r_small[i + k] for k in range(n)]

    def rstd_batch(self, ssq, r_ssq, n, F):
        P = self.P
        P.op('dve', lambda e: e.tensor_scalar(out=ssq, in0=ssq, scalar1=1.0 / F, scalar2=EPS, op0=ALU.mult, op1=ALU.add), reads=[], writes=r_ssq)
        P.op('act', lambda e: e.activation(out=ssq, in_=ssq, func=AF.Sqrt), reads=[], writes=r_ssq)
        P.op('dve', lambda e: e.reciprocal(out=ssq, in_=ssq), reads=[], writes=r_ssq)

    def norm_T(self, src_fn, src_res_fn, F, dstT, dst_res_fn, ntb, ub, r_ub, junk):
        P = self.P
        nk = F // 128
        ssq, r_ssq = self.scal_block(ntb)
        for tb in range(ntb):
            P.op('act', lambda e, tb=tb: e.activation(out=junk[:, 0:F], in_=src_fn(tb), func=AF.Square, accum_out=ssq[:, tb:tb + 1]),
                 reads=[src_res_fn(tb)], writes=[r_ssq[tb]])
        self.rstd_batch(ssq, r_ssq, ntb, F)
        nb = len(ub)
        for tb in range(ntb):
            b = tb % nb
            P.op('dve', lambda e, tb=tb, b=b: e.scalar_tensor_tensor(out=ub[b][:, 0:F], in0=src_fn(tb), scalar=ssq[:, tb:tb + 1], in1=self.gb[:, 0:F], op0=ALU.mult, op1=ALU.mult),
                 reads=[src_res_fn(tb), r_ssq[tb], self.r_gb], writes=[r_ub[b]])
            for k4 in range(nk // 4):
                pi = self.getps()
                pt = self.ps[pi][:, :].bitcast(BF16)

                def tr(e, b=b, k4=k4, pt=pt):
                    ins = None
                    for j in range(4):
                        kc = k4 * 4 + j
                        ins = e.transpose(out=pt[:, j * 128:(j + 1) * 128], in_=ub[b][:, kc * 128:(kc + 1) * 128], identity=self.ident)
                    return ins
                P.op('pe', tr, reads=[r_ub[b]], writes=[self.r_ps[pi]])
                eng = 'act' if k4 % 2 == 0 else 'dve'

                def cp(e, tb=tb, k4=k4, pt=pt, eng=eng):
                    s = pt[:, 0:512].rearrange('p (a b) -> p a b', b=128)
                    d = dstT[:, k4 * 4:(k4 + 1) * 4, tb * 128:(tb + 1) * 128]
                    return e.copy(out=d, in_=s) if eng == 'act' else e.tensor_copy(out=d, in_=s)
                P.op(eng, cp, reads=[self.r_ps[pi]], writes=[dst_res_fn(tb)])

    def norm_h(self, gain_ap, dstT=None, dst_res=None):
        X = self.RX
        save = X.off
        ub = [X.alloc(4096), X.alloc(4096), X.alloc(4096)]
        r_ub = [Res(), Res(), Res()]
        junk = X.alloc(4096)
        self.load_gain(gain_ap)
        dstT = self.uT if dstT is None else dstT
        dst_res = self.r_uT if dst_res is None else dst_res
        self.norm_T(lambda tb: self.h[:, tb, :], lambda tb: self.r_h[tb], D, dstT, lambda tb: dst_res[tb], NTB, ub, r_ub, junk)
        self.P.barrier()
        X.off = save

    def lin_fm(self, xT, r_x, nkc, wtiles, ncols, M_list, epi, tgs=(0, 1), ncol_T=512):
        P = self.P
        for ti in range(wtiles.shape[0]):
            wt, r_wt = self.wload(wtiles[ti])
            wv = v3(wt[:, 0:nkc * ncols], ncols)
            for mi, (off, M) in enumerate(M_list):
                for tg in tgs:
                    pi = self.getps()

                    def mm(e, wv=wv, off=off, M=M, tg=tg, pi=pi):
                        ins = None
                        for kc in range(nkc):
                            ins = e.matmul(self.ps[pi][0:M, 0:ncol_T], lhsT=wv[:, kc, off:off + M], rhs=xT[:, kc, tg * ncol_T:(tg + 1) * ncol_T],
                                           start=(kc == 0), stop=(kc == nkc - 1))
                        return ins
                    P.op('pe', mm, reads=[r_wt] + list(r_x(tg)), writes=[self.r_ps[pi]])
                    epi(ti, mi, tg, pi)

    def lin_tok(self, xT, r_x, nkc, wtiles, ncols, epi, ntb=NTB):
        P = self.P
        for ti in range(wtiles.shape[0]):
            wt, r_wt = self.wload(wtiles[ti])
            wv = v3(wt[:, 0:nkc * ncols], ncols)
            for tb in range(ntb):
                pi = self.getps()

                def mm(e, wv=wv, tb=tb, pi=pi):
                    ins = None
                    for kc in range(nkc):
                        ins = e.matmul(self.ps[pi][:, 0:ncols], lhsT=xT[:, kc, tb * 128:(tb + 1) * 128], rhs=wv[:, kc, :],
                                       start=(kc == 0), stop=(kc == nkc - 1))
                    return ins
                P.op('pe', mm, reads=[r_wt] + list(r_x(tb)), writes=[self.r_ps[pi]])
                epi(ti, tb, pi)

    def ffn(self, l, which):
        P = self.P
        X = self.RX
        self.norm_h(self.dram['L%d_g_ffn%d_norm' % (l, which)])
        X.reset()
        actT = [v3(X.alloc(CPG * T * 2), T) for _ in range(2)]
        sg = [X.alloc(2048, F32) for _ in range(2)]
        r_act = [[Res() for _ in range(CPG)] for _ in range(2)]
        r_sg = [Res(), Res()]
        w1 = self.dram['L%d_f%dw1' % (l, which)]
        w2 = self.dram['L%d_f%dw2' % (l, which)]
        uT = self.uT
        for g in range(HG):
            ab = g % 2
            for cl in range(CPG):
                c = g * CPG + cl
                wt, r_wt = self.wload(w1[c])
                wv = v3(wt[:, 0:KC * 256], 256)
                for tg in range(2):
                    pg, pu = self.getps(), self.getps()

                    def mm(e, wv=wv, tg=tg, pg=pg, pu=pu):
                        ins = None
                        for half, pp in ((0, pg), (1, pu)):
                            for kc in range(KC):
                                ins = e.matmul(self.ps[pp][:, :], lhsT=wv[:, kc, half * 128:(half + 1) * 128],
                                               rhs=uT[:, kc, tg * 512:(tg + 1) * 512], start=(kc == 0), stop=(kc == KC - 1))
                        return ins
                    P.op('pe', mm, reads=[r_wt] + self.r_uT[tg * 4:(tg + 1) * 4], writes=[self.r_ps[pg], self.r_ps[pu]])
                    sb = (cl * 2 + tg) % 2
                    P.op('act', lambda e, pg=pg, sb=sb: e.activation(out=sg[sb][:, :], in_=self.ps[pg][:, :], func=AF.Silu),
                         reads=[self.r_ps[pg]], writes=[r_sg[sb]])
                    P.op('dve', lambda e, pu=pu, sb=sb, ab=ab, cl=cl, tg=tg: e.tensor_tensor(out=actT[ab][:, cl, tg * 512:(tg + 1) * 512], in0=sg[sb][:, :], in1=self.ps[pu][:, :], op=ALU.mult),
                         reads=[self.r_ps[pu], r_sg[sb]], writes=[r_act[ab][cl]])
            for ng in range(4):
                wt, r_wt = self.wload(w2[g * 4 + ng])
                wv = v3(wt[:, 0:CPG * 512], 512)
                for tb in range(NTB):
                    po = self.getps()

                    def mm2(e, wv=wv, tb=tb, po=po, ab=ab):
                        ins = None
                        for cl in range(CPG):
                            ins = e.matmul(self.ps[po][:, :], lhsT=actT[ab][:, cl, tb * 128:(tb + 1) * 128], rhs=wv[:, cl, :],
                                           start=(cl == 0), stop=(cl == CPG - 1))
                        return ins
                    P.op('pe', mm2, reads=[r_wt] + r_act[ab], writes=[self.r_ps[po]])
                    P.op('dve', lambda e, po=po, tb=tb, ng=ng: e.scalar_tensor_tensor(out=self.h[:, tb, ng * 512:(ng + 1) * 512], in0=self.ps[po][:, :], scalar=0.5,
                                                                                    in1=self.h[:, tb, ng * 512:(ng + 1) * 512], op0=ALU.mult, op1=ALU.add),
                         reads=[self.r_ps[po], self.r_h[tb]], writes=[self.r_h[tb]])
        P.barrier()
        X.reset()

    def latent(self, l, wkey, gkey, dstT, r_dst, X):
        P = self.P
        save = X.off
        ub = [X.alloc(1024), X.alloc(1024)]
        r_ub = [Res(), Res()]
        lat = [X.alloc(2048, F32) for _ in range(4)]
        r_lat = [Res() for _ in range(4)]
        junk = X.alloc(1024)
        self.load_gain(self.dram['L%d_%s' % (l, gkey)], 512)
        wt = self.dram['L%d_%s' % (l, wkey)]
        w0, r_w0 = self.wload(wt[0])
        w1, r_w1 = self.wload(wt[1])
        wv = [v3(w0[:, 0:KC * 256], 256), v3(w1[:, 0:KC * 256], 256)]
        for half4 in range(2):
            ssq, r_ssq = self.scal_block(4)
            for i in range(4):
                tb = half4 * 4 + i
                pi = self.getps()

                def mm(e, tb=tb, pi=pi):
                    ins = None
                    for half in range(2):
                        for kc in range(KC):
                            ins = e.matmul(self.ps[pi][:, half * 256:(half + 1) * 256], lhsT=self.uT[:, kc, tb * 128:(tb + 1) * 128], rhs=wv[half][:, kc, :],
                                           start=(kc == 0), stop=(kc == KC - 1))
                    return ins
                P.op('pe', mm, reads=[r_w0, r_w1, self.r_uT[tb]], writes=[self.r_ps[pi]])
                P.op('act', lambda e, i=i, pi=pi, ssq=ssq: e.activation(out=junk[:, 0:512], in_=self.ps[pi][:, :], func=AF.Square, accum_out=ssq[:, i:i + 1]),
                     reads=[], writes=[self.r_ps[pi], r_ssq[i]])
                P.op('dve', lambda e, i=i, pi=pi: e.tensor_copy(out=lat[i][:, :], in_=self.ps[pi][:, :]), reads=[], writes=[self.r_ps[pi], r_lat[i]])
            self.rstd_batch(ssq, r_ssq, 4, 512)
            for i in range(4):
                tb = half4 * 4 + i
                b = i % 2
                P.op('dve', lambda e, i=i, b=b, ssq=ssq: e.scalar_tensor_tensor(out=ub[b][:, 0:512], in0=lat[i][:, :], scalar=ssq[:, i:i + 1], in1=self.gb[:, 0:512], op0=ALU.mult, op1=ALU.mult),
                     reads=[r_lat[i], r_ssq[i], self.r_gb], writes=[r_ub[b]])
                p2 = self.getps()
                pt = self.ps[p2][:, :].bitcast(BF16)

                def tr(e, b=b, pt=pt):
                    ins = None
                    for j in range(4):
                        ins = e.transpose(out=pt[:, j * 128:(j + 1) * 128], in_=ub[b][:, j * 128:(j + 1) * 128], identity=self.ident)
                    return ins
                P.op('pe', tr, reads=[r_ub[b]], writes=[self.r_ps[p2]])
                P.op('act', lambda e, tb=tb, pt=pt: e.copy(out=dstT[:, 0:4, tb * 128:(tb + 1) * 128], in_=pt[:, 0:512].rearrange('p (a b) -> p a b', b=128)),
                     reads=[self.r_ps[p2]], writes=[r_dst])
        P.barrier()
        X.off = save

    def gla(self, l, mode, X, xs_out=None, gs=None, r_gs=None, oglaT=None, r_ogla=None):
        P = self.P
        uT = self.uT
        M1, M2, M3 = self.cf32[:, 0:128], self.cf32[:, 128:256], self.cf32[:, 256:384]
        alT = X.alloc(2048)
        r_al = Res()
        P.dma('pool', self.wa2[0:16, 0:256], self.dram['L%d_wa2' % l], writes=[self.r_lc], key='lc')
        P.dma('pool', self.ba[0:1, 0:256], self.dram['L%d_ba' % l], writes=[self.r_lc], key='lc')

        def epi_al(ti, mi, tg, pi):
            P.op('act', lambda e: e.copy(out=alT[0:16, tg * 512:(tg + 1) * 512], in_=self.ps[pi][0:16, :]), reads=[self.r_ps[pi]], writes=[r_al])
        self.lin_fm(uT, lambda tg: self.r_uT[tg * 4:(tg + 1) * 4], KC, self.dram['L%d_walow' % l], 16, [(0, 16)], epi_al)
        qk = v3(X.alloc(NTB * 512 * 4, F32), 512)
        V = v3(X.alloc(NTB * 512 * 2), 512)
        r_qk = [Res() for _ in range(NTB)]
        r_V = [Res() for _ in range(NTB)]
        if mode == 'B':
            sr = v3(X.alloc(NTB * 512 * 2), 512)
            r_sr = [Res() for _ in range(NTB)]

        def epi_g(ti, tb, pi):
            if ti < 2:
                P.op('act', lambda e: e.copy(out=qk[:, tb, ti * 256:(ti + 1) * 256], in_=self.ps[pi][:, 0:256]), reads=[self.r_ps[pi]], writes=[r_qk[tb]])
            elif ti < 4:
                P.op('dve', lambda e: e.tensor_copy(out=V[:, tb, (ti - 2) * 256:(ti - 1) * 256], in_=self.ps[pi][:, 0:256]), reads=[self.r_ps[pi]], writes=[r_V[tb]])
            elif mode == 'B':
                P.op('act', lambda e: e.activation(out=sr[:, tb, (ti - 4) * 256:(ti - 3) * 256], in_=self.ps[pi][:, 0:256], func=AF.Silu), reads=[self.r_ps[pi]], writes=[r_sr[tb]])
        wg = self.dram['L%d_wgla' % l]
        self.lin_tok(uT, lambda tb: [self.r_uT[tb]], KC, wg if mode == 'B' else wg[0:4], 256, epi_g)
        S = [X.alloc(512, F32) for _ in range(2)]
        Sb = [X.alloc(256) for _ in range(2)]
        r_S = [Res(), Res()]
        r_Sb = [Res(), Res()]
        Dt = X.alloc(64, F32)
        r_D = Res()
        lsp = X.alloc(1024, F32)
        r_lsp = Res()
        ex = [X.alloc(1024, F32) for _ in range(3)]
        r_ex = [Res() for _ in range(3)]
        kd = X.alloc(512)
        r_kd = Res()
        if mode == 'B':
            qt = X.alloc(512)
            kt = X.alloc(512)
            r_qt, r_kt = Res(), Res()
            qT = [X.alloc(256) for _ in range(2)]
            kT = [X.alloc(256) for _ in range(2)]
            r_qT, r_kT = [Res(), Res()], [Res(), Res()]
            AT = [X.alloc(512) for _ in range(2)]
            r_AT = [Res(), Res()]
            og = X.alloc(1024)
            r_og = Res()
            osb = X.alloc(2048, F32)
            r_osb = Res()
            self.load_gain(self.dram['L%d_g_gla_norm' % l], 512)
        if mode == 'A':
            for hf in range(2):
                P.op('pool', lambda e, hf=hf: e.memset(S[hf][:, :], 0.0), writes=[r_S[hf]])
            P.op('pool', lambda e: e.memset(Dt[:, :], 1.0), writes=[r_D])
        else:
            tmpL = X.alloc(512, F32)
            r_tmpL = Res()
            coef, r_coef = self.scal()
            for hf in range(2):
                P.op('pool', lambda e, hf=hf: e.memset(S[hf][:, :], 0.0), writes=[r_S[hf]])
                for i in range(3):
                    P.op('dve', lambda e, hf=hf, i=i: e.tensor_scalar(out=coef, in0=gs[:, i, 256 + hf:257 + hf], scalar1=self.ctl[:, 4 + i:5 + i], scalar2=self.ctl[:, 8 + i:9 + i], op0=ALU.mult, op1=ALU.add),
                         reads=[r_gs], writes=[r_coef])
                    P.op('dve', lambda e, hf=hf, i=i: e.tensor_scalar(out=tmpL[:, :], in0=gs[:, i, hf * 128:(hf + 1) * 128], scalar1=self.ctl[:, 4 + i:5 + i], scalar2=None, op0=ALU.mult),
                         reads=[r_gs], writes=[r_tmpL])
                    P.op('dve', lambda e, hf=hf: e.scalar_tensor_tensor(out=S[hf][:, :], in0=S[hf][:, :], scalar=coef, in1=tmpL[:, :], op0=ALU.mult, op1=ALU.add),
                         reads=[r_coef, r_tmpL, r_S[hf]], writes=[r_S[hf]])
        if mode == 'A' and self.cc_after_xk is not None:
            self.cc_after_xk()
        for n in range(NTB):
            px = self.getps()

            def mmx(e, n=n, px=px):
                e.matmul(self.ps[px][:, 0:256], lhsT=alT[0:16, n * 128:(n + 1) * 128], rhs=self.wa2[0:16, 0:256], start=True, stop=False)
                return e.matmul(self.ps[px][:, 0:256], lhsT=self.ones[0:1, 0:128], rhs=self.ba[0:1, 0:256], start=False, stop=True)
            P.op('pe', mmx, reads=[r_al, self.r_lc], writes=[self.r_ps[px]])
            P.op('act', lambda e, px=px: e.activation(out=lsp[:, :], in_=self.ps[px][:, 0:256], func=AF.Exp, scale=-1.0), reads=[self.r_ps[px]], writes=[r_lsp])
            P.op('act', lambda e: e.activation(out=lsp[:, :], in_=lsp[:, :], func=AF.Ln, bias=1.0), reads=[r_lsp], writes=[r_lsp])
            pb = self.getps()

            def mmb(e, pb=pb):
                e.matmul(self.ps[pb][:, 0:256], lhsT=M1, rhs=lsp[:, :], start=True, stop=True)
                return e.matmul(self.ps[pb][:, 256:512], lhsT=M2, rhs=lsp[:, :], start=True, stop=True)
            P.op('pe', mmb, reads=[r_lsp], writes=[self.r_ps[pb]])
            pc = self.getps()

            def mmc(e, pc=pc):
                e.matmul(self.ps[pc][:, 0:2], lhsT=lsp[:, 0:128], rhs=M3[:, 0:2], start=True, stop=True)
                return e.matmul(self.ps[pc][:, 2:4], lhsT=lsp[:, 128:256], rhs=M3[:, 0:2], start=True, stop=True)
            P.op('pe', mmc, reads=[r_lsp], writes=[self.r_ps[pc]])
            cd, r_cd = self.scal()
            cd2, r_cd2 = self.scal()
            P.op('act', lambda e, pc=pc, cd=cd: e.activation(out=cd, in_=self.ps[pc][:, 0:1], func=AF.Exp), reads=[self.r_ps[pc]], writes=[r_cd])
            P.op('act', lambda e, pc=pc, cd2=cd2: e.activation(out=cd2, in_=self.ps[pc][:, 2:3], func=AF.Exp), reads=[self.r_ps[pc]], writes=[r_cd2])
            cds = [(cd, r_cd), (cd2, r_cd2)]
            P.op('act', lambda e, pb=pb: e.activation(out=ex[2][:, :], in_=self.ps[pb][:, 256:512], func=AF.Exp), reads=[self.r_ps[pb]], writes=[r_ex[2]])
            P.op('dve', lambda e, n=n: e.tensor_tensor(out=kd[:, :], in0=qk[:, n, 256:512], in1=ex[2][:, :], op=ALU.mult), reads=[r_qk[n], r_ex[2]], writes=[r_kd])
            if mode == 'B':
                P.op('act', lambda e, pb=pb: e.activation(out=ex[0][:, :], in_=self.ps[pb][:, 0:256], func=AF.Exp), reads=[self.r_ps[pb]], writes=[r_ex[0]])
                P.op('act', lambda e, pb=pb: e.activation(out=ex[1][:, :], in_=self.ps[pb][:, 0:256], func=AF.Exp, scale=-1.0), reads=[self.r_ps[pb]], writes=[r_ex[1]])
                P.op('dve', lambda e, n=n: e.scalar_tensor_tensor(out=qt[:, :], in0=qk[:, n, 0:256], scalar=0.125, in1=ex[0][:, :], op0=ALU.mult, op1=ALU.mult),
                     reads=[r_qk[n], r_ex[0]], writes=[r_qt])
                P.op('dve', lambda e, n=n: e.tensor_tensor(out=kt[:, :], in0=qk[:, n, 256:512], in1=ex[1][:, :], op=ALU.mult), reads=[r_qk[n], r_ex[1]], writes=[r_kt])
                ptq = self.getps()
                ptv = self.ps[ptq][:, :].bitcast(BF16)

                def trq(e, ptv=ptv):
                    e.transpose(out=ptv[:, 0:128], in_=qt[:, 0:128], identity=self.ident)
                    e.transpose(out=ptv[:, 128:256], in_=qt[:, 128:256], identity=self.ident)
                    e.transpose(out=ptv[:, 256:384], in_=kt[:, 0:128], identity=self.ident)
                    return e.transpose(out=ptv[:, 384:512], in_=kt[:, 128:256], identity=self.ident)
                P.op('pe', trq, reads=[r_qt, r_kt], writes=[self.r_ps[ptq]])
                for hf in range(2):
                    P.op('act', lambda e, hf=hf, ptv=ptv: e.copy(out=qT[hf][:, :], in_=ptv[:, hf * 128:(hf + 1) * 128]), reads=[self.r_ps[ptq]], writes=[r_qT[hf]])
                    P.op('act', lambda e, hf=hf, ptv=ptv: e.copy(out=kT[hf][:, :], in_=ptv[:, 256 + hf * 128:256 + (hf + 1) * 128]), reads=[self.r_ps[ptq]], writes=[r_kT[hf]])
                pa = [self.getps(), self.getps()]
                for e_ in range(2):
                    def mma(e, e_=e_, pa=pa):
                        ins = None
                        for hf in range(2):
                            ins = e.matmul(self.ps[pa[e_]][:, hf * 128:(hf + 1) * 128], lhsT=kT[hf][e_ * 64:(e_ + 1) * 64, :], rhs=qT[hf][e_ * 64:(e_ + 1) * 64, :],
                                           start=True, stop=True)
                        return ins
                    P.op('pe', mma, reads=r_qT + r_kT, writes=[self.r_ps[pa[e_]]])
                    for hf in range(2):
                        P.op('dve', lambda e, e_=e_, hf=hf, pa=pa: e.tensor_tensor(out=AT[e_][:, hf * 128:(hf + 1) * 128], in0=self.ps[pa[e_]][:, hf * 128:(hf + 1) * 128], in1=self.tri, op=ALU.mult),
                             reads=[self.r_ps[pa[e_]]], writes=[r_AT[e_]])
                for hf in range(2):
                    P.op('act', lambda e, hf=hf: e.copy(out=Sb[hf][:, :], in_=S[hf][:, :]), reads=[r_S[hf]], writes=[r_Sb[hf]])
                po = [self.getps(), self.getps()]
                for e_ in range(2):
                    def mmo(e, e_=e_, po=po, n=n):
                        ins = None
                        for hf in range(2):
                            hd = 2 * hf + e_
                            e.matmul(self.ps[po[e_]][:, hf * 128:(hf + 1) * 128], lhsT=AT[e_][:, hf * 128:(hf + 1) * 128], rhs=V[:, n, hd * 128:(hd + 1) * 128], start=True, stop=False)
                            ins = e.matmul(self.ps[po[e_]][:, hf * 128:(hf + 1) * 128], lhsT=qT[hf][e_ * 64:(e_ + 1) * 64, :], rhs=Sb[hf][e_ * 64:(e_ + 1) * 64, :], start=False, stop=True)
                        return ins
                    P.op('pe', mmo, reads=[r_AT[e_], r_V[n]] + r_qT + r_Sb, writes=[self.r_ps[po[e_]]])
                for e_ in range(2):
                    for hf in range(2):
                        hd = 2 * hf + e_
                        src = self.ps[po[e_]][:, hf * 128:(hf + 1) * 128]
                        ss, r_ss = self.scal()
                        rs, r_rs = self.scal()
                        P.op('act', lambda e, src=src, ss=ss, hd=hd: e.activation(out=osb[:, hd * 128:(hd + 1) * 128], in_=src, func=AF.Square, accum_out=ss),
                             reads=[], writes=[r_osb, r_ss, self.r_ps[po[e_]]])
                        P.op('dve', lambda e, ss=ss, rs=rs: e.tensor_scalar(out=rs, in0=ss, scalar1=1.0 / 128, scalar2=EPS, op0=ALU.mult, op1=ALU.add), reads=[r_ss], writes=[r_rs])
                        P.op('act', lambda e, rs=rs: e.activation(out=rs, in_=rs, func=AF.Sqrt), reads=[r_rs], writes=[r_rs])
                        P.op('dve', lambda e, rs=rs: e.reciprocal(out=rs, in_=rs), reads=[r_rs], writes=[r_rs])
                        P.op('dve', lambda e, src=src, rs=rs, hd=hd: e.scalar_tensor_tensor(out=osb[:, hd * 128:(hd + 1) * 128], in0=src, scalar=rs, in1=self.gb[:, hd * 128:(hd + 1) * 128], op0=ALU.mult, op1=ALU.mult),
                             reads=[r_rs, self.r_gb], writes=[r_osb, self.r_ps[po[e_]]])
                P.op('dve', lambda e, n=n: e.tensor_tensor(out=og[:, :], in0=osb[:, :], in1=sr[:, n, :], op=ALU.mult), reads=[r_osb, r_sr[n]], writes=[r_og])
                pt2 = self.getps()
                ptw = self.ps[pt2][:, :].bitcast(BF16)

                def tro(e, ptw=ptw):
                    ins = None
                    for j in range(4):
                        ins = e.transpose(out=ptw[:, j * 128:(j + 1) * 128], in_=og[:, j * 128:(j + 1) * 128], identity=self.ident)
                    return ins
                P.op('pe', tro, reads=[r_og], writes=[self.r_ps[pt2]])
                P.op('act', lambda e, n=n, ptw=ptw: e.copy(out=oglaT[:, 0:4, n * 128:(n + 1) * 128], in_=ptw[:, 0:512].rearrange('p (a b) -> p a b', b=128)),
                     reads=[self.r_ps[pt2]], writes=[r_ogla])
            for hf in range(2):
                pp = self.getps()
                P.op('pe', lambda e, hf=hf, pp=pp, n=n: e.matmul(self.ps[pp][:, 0:256], lhsT=kd[:, hf * 128:(hf + 1) * 128], rhs=V[:, n, hf * 256:(hf + 1) * 256], start=True, stop=True),
                     reads=[r_kd, r_V[n]], writes=[self.r_ps[pp]])
                cdh, r_cdh = cds[hf]
                for e_ in range(2):
                    P.op('dve', lambda e, hf=hf, e_=e_, pp=pp, cdh=cdh: e.scalar_tensor_tensor(out=S[hf][e_ * 64:(e_ + 1) * 64, :], in0=S[hf][e_ * 64:(e_ + 1) * 64, :], scalar=cdh[e_ * 64:(e_ + 1) * 64, :],
                                                                                             in1=self.ps[pp][e_ * 64:(e_ + 1) * 64, e_ * 128:(e_ + 1) * 128], op0=ALU.mult, op1=ALU.add),
                         reads=[self.r_ps[pp], r_cdh, r_S[hf]] + ([r_Sb[hf]] if mode == 'B' else []), writes=[r_S[hf]])
                if mode == 'A':
                    P.op('dve', lambda e, hf=hf, cdh=cdh: e.tensor_tensor(out=Dt[:, hf:hf + 1], in0=Dt[:, hf:hf + 1], in1=cdh, op=ALU.mult), reads=[r_cdh, r_D], writes=[r_D])
        if mode == 'A':
            for hf in range(2):
                P.dma('sp', xs_out[:, hf * 128:(hf + 1) * 128], S[hf][:, :], reads=[r_S[hf]], writes=[self.r_xsd], key='xsS%d' % hf)
            P.dma('sp', xs_out[:, 256:258], Dt[:, 0:2], reads=[r_D], writes=[self.r_xsd], key='xsD')
        P.barrier()

    def phase_A_exchange(self, l, xk_out, xs_out):
        P = self.P
        X = self.RX
        X.reset()
        ckT = v3(X.alloc(4 * T * 2), T)
        r_ck = Res()
        self.latent(l, 'wkv', 'g_mla_kv_norm', ckT, r_ck, X)
        for kc in range(4):
            P.dma('sp', xk_out[kc], ckT[:, kc, :], reads=[r_ck], writes=[self.r_xkd], key='xk%d' % kc)
        rope = X.alloc(2 * T * 4, F32)
        r_rope = Res()
        P.dma('sp', rope[0:64, :], self.d_rope, writes=[r_rope], key='rope')
        kpe = X.alloc(T * 2)
        r_kpe = Res()
        t1 = X.alloc(2048, F32)
        t2 = X.alloc(2048, F32)
        r_t1, r_t2 = Res(), Res()
        hold = {}

        def epi_kr(ti, mi, tg, pi):
            if mi == 0:
                hold[tg] = pi
                P.op('dve', lambda e: e.tensor_tensor(out=t1[0:64, :], in0=self.ps[pi][0:64, :], in1=rope[0:64, tg * 512:(tg + 1) * 512], op=ALU.mult),
                     reads=[self.r_ps[pi], r_rope], writes=[r_t1])
            else:
                P.op('dve', lambda e: e.tensor_tensor(out=t2[0:64, :], in0=self.ps[pi][0:64, :], in1=rope[0:64, T + tg * 512:T + (tg + 1) * 512], op=ALU.mult),
                     reads=[self.r_ps[pi], r_rope], writes=[r_t2])
                P.op('dve', lambda e: e.tensor_tensor(out=kpe[0:64, tg * 512:(tg + 1) * 512], in0=t1[0:64, :], in1=t2[0:64, :], op=ALU.add),
                     reads=[r_t1, r_t2], writes=[r_kpe])
        for tg in range(2):
            self.lin_fm(self.uT, lambda tg_: self.r_uT[tg_ * 4:(tg_ + 1) * 4], KC, self.dram['L%d_wkr' % l], 128, [(0, 64), (64, 64)], epi_kr, tgs=(tg,))
        P.dma('sp', xk_out[4][0:64, :], kpe[0:64, :], reads=[r_kpe], writes=[self.r_xkd], key='xk4')
        zt = X.alloc(64, F32)
        cgs = X.alloc(2048, F32)
        zb = X.alloc(1024)
        r_zt, r_cgs, r_zb = Res(), Res(), Res()
        wc = self.dram['L%d_wconv' % l]
        pcg, phn = self.getps(), self.getps()
        for part, base, pp in (('cg', 2, pcg), ('hin', 4, phn)):
            for k in range(2):
                wt, r_wt = self.wload(wc[base + k])
                wv = v3(wt[:, 0:KC * 256], 256)

                def mm(e, wv=wv, k=k, pp=pp):
                    ins = None
                    for kc in range(KC):
                        ins = e.matmul(self.ps[pp][0:2, k * 256:(k + 1) * 256], lhsT=self.uT[:, kc, T - 2:T], rhs=wv[:, kc, :], start=(kc == 0), stop=(kc == KC - 1))
                    return ins
                P.op('pe', mm, reads=[r_wt] + self.r_uT[4:8], writes=[self.r_ps[pp]])
        P.op('act', lambda e: e.copy(out=cgs[0:2, :], in_=self.ps[pcg][0:2, :]), reads=[self.r_ps[pcg]], writes=[r_cgs])
        P.op('dve', lambda e: e.tensor_tensor(out=zb[0:2, :], in0=cgs[0:2, :], in1=self.ps[phn][0:2, :], op=ALU.mult), reads=[self.r_ps[phn], r_cgs], writes=[r_zb])
        ptz = self.getps()
        ptzv = self.ps[ptz][:, :].bitcast(BF16)

        def trz(e):
            ins = None
            for ch in range(4):
                ins = e.transpose(out=ptzv[:, ch * 2:ch * 2 + 2], in_=zb[0:2, ch * 128:(ch + 1) * 128], identity=self.ident[0:2, 0:2])
            return ins
        P.op('pe', trz, reads=[r_zb], writes=[self.r_ps[ptz]])
        P.op('act', lambda e: e.copy(out=zt[:, 0:8], in_=ptzv[:, 0:8]), reads=[self.r_ps[ptz]], writes=[r_zt])
        P.dma('sp', xs_out[:, 258:266], zt[:, 0:8], reads=[r_zt], writes=[self.r_xsd], key='xsz')
        P.barrier()
        X.reset()
        self.gla(l, 'A', X, xs_out=xs_out)
        X.reset()

    def mla(self, l, S2, X, gk, ownk, omlaT, r_omla):
        P = self.P
        cqT = v3(S2.alloc(4 * T * 2), T)
        r_cq = Res()
        self.latent(l, 'wq', 'g_mla_q_norm', cqT, r_cq, X)
        rope = S2.alloc(2 * T * 4, F32)
        r_rope = Res()
        P.dma('sp', rope[0:64, :], self.d_rope, writes=[r_rope], key='rope')
        segbuf = [X.alloc(XK_COLS * 2) for _ in range(2)]
        r_segb = [Res(), Res()]
        segi = [0]
        KT = [X.alloc(T * 2) for _ in range(5)]
        VV = [v3(X.alloc(T * 2), 128) for _ in range(5)]
        KPE = [S2.alloc(T * 2) for _ in range(5)]
        r_K = [Res() for _ in range(5)]
        r_Vv = [Res() for _ in range(5)]
        r_KPE = [Res() for _ in range(5)]
        qT = [X.alloc(T * 2) for _ in range(2)]
        qpe = [X.alloc(T * 2) for _ in range(2)]
        r_q = [Res(), Res()]
        r_qpe = [Res(), Res()]
        PT = [X.alloc(1024) for _ in range(5)]
        r_PT = [Res() for _ in range(5)]
        t1 = X.alloc(2048, F32)
        t2 = X.alloc(2048, F32)
        r_t1, r_t2 = Res(), Res()
        rsb = X.alloc(2048, F32)
        r_rsb = Res()
        wm = self.dram['L%d_wmla' % l]
        pti = 0
        for hd in range(8):
            hb = hd % 2
            wt, r_wt = self.wload(wm[hd])
            wv = v3(wt[:, 0:4 * 512], 512)
            self.ps_set = [0, 1, 2, 3]
            for tg in range(2):
                pi = self.getps()

                def mmq(e, tg=tg, pi=pi, wv=wv):
                    ins = None
                    for kc in range(4):
                        ins = e.matmul(self.ps[pi][:, :], lhsT=wv[:, kc, 0:128], rhs=cqT[:, kc, tg * 512:(tg + 1) * 512], start=(kc == 0), stop=(kc == 3))
                    return ins
                P.op('pe', mmq, reads=[r_wt, r_cq], writes=[self.r_ps[pi]])
                P.op('act', lambda e, tg=tg, pi=pi, hb=hb: e.copy(out=qT[hb][:, tg * 512:(tg + 1) * 512], in_=self.ps[pi][:, :]), reads=[self.r_ps[pi]], writes=[r_q[hb]])
                for which in range(2):
                    pj = self.getps()

                    def mmr(e, tg=tg, pj=pj, wv=wv, which=which):
                        ins = None
                        for kc in range(4):
                            ins = e.matmul(self.ps[pj][0:64, :], lhsT=wv[:, kc, 128 + which * 64:192 + which * 64], rhs=cqT[:, kc, tg * 512:(tg + 1) * 512], start=(kc == 0), stop=(kc == 3))
                        return ins
                    P.op('pe', mmr, reads=[r_wt, r_cq], writes=[self.r_ps[pj]])
                    tt, r_tt = (t1, r_t1) if which == 0 else (t2, r_t2)
                    P.op('dve', lambda e, tg=tg, pj=pj, which=which, tt=tt: e.tensor_tensor(out=tt[0:64, :], in0=self.ps[pj][0:64, :], in1=rope[0:64, which * T + tg * 512:which * T + (tg + 1) * 512], op=ALU.mult),
                         reads=[self.r_ps[pj], r_rope], writes=[r_tt])
                P.op('dve', lambda e, tg=tg, hb=hb: e.tensor_tensor(out=qpe[hb][0:64, tg * 512:(tg + 1) * 512], in0=t1[0:64, :], in1=t2[0:64, :], op=ALU.add),
                     reads=[r_t1, r_t2], writes=[r_qpe[hb]])
            for s in range(4):
                sb_ = segi[0] % 2
                segi[0] += 1
                for j5 in range(5):
                    srcd = ownk[j5] if s == 0 else gk[j5][s - 1]
                    P.dma('sp', segbuf[sb_][:, j5 * 1024:(j5 + 1) * 1024], srcd, reads=[self.r_gk, self.r_xkd], writes=[r_segb[sb_]], key='seg%d' % sb_)
                if hd == 0:
                    P.op('pool', lambda e, s=s, sb_=sb_: e.tensor_copy(out=KPE[s][0:64, :], in_=segbuf[sb_][0:64, 4096:5120]), reads=[r_segb[sb_]], writes=[r_KPE[s]])
                sv = v3(segbuf[sb_][:, 0:4096], T)
                r_sg_ = r_segb[sb_]
                for ktg in range(2):
                    pi = self.getps()

                    def mmk(e, sv=sv, ktg=ktg, pi=pi, wv=wv):
                        ins = None
                        for kc in range(4):
                            ins = e.matmul(self.ps[pi][:, :], lhsT=wv[:, kc, 256:384], rhs=sv[:, kc, ktg * 512:(ktg + 1) * 512], start=(kc == 0), stop=(kc == 3))
                        return ins
                    P.op('pe', mmk, reads=[r_wt, r_sg_], writes=[self.r_ps[pi]])
                    eng = 'act' if ktg == 0 else 'dve'
                    P.op(eng, lambda e, s=s, ktg=ktg, pi=pi, eng=eng: (e.copy if eng == 'act' else e.tensor_copy)(out=KT[s][:, ktg * 512:(ktg + 1) * 512], in_=self.ps[pi][:, :]),
                         reads=[self.r_ps[pi]], writes=[r_K[s]])
                for kb4 in range(2):
                    pi = self.getps()

                    def mmv(e, sv=sv, kb4=kb4, pi=pi, wv=wv):
                        ins = None
                        for j in range(4):
                            kb = kb4 * 4 + j
                            for kc in range(4):
                                ins = e.matmul(self.ps[pi][:, j * 128:(j + 1) * 128], lhsT=sv[:, kc, kb * 128:(kb + 1) * 128], rhs=wv[:, kc, 384:512], start=(kc == 0), stop=(kc == 3))
                        return ins
                    P.op('pe', mmv, reads=[r_wt, r_sg_], writes=[self.r_ps[pi]])
                    eng = 'dve' if kb4 == 0 else 'act'
                    P.op(eng, lambda e, s=s, kb4=kb4, pi=pi, eng=eng: (e.copy if eng == 'act' else e.tensor_copy)(out=VV[s][:, kb4 * 4:(kb4 + 1) * 4, :], in_=self.ps[pi][:, :].rearrange('p (a b) -> p a b', b=128)),
                         reads=[self.r_ps[pi]], writes=[r_Vv[s]])
            for tg in range(2):
                units = []
                for s in range(4):
                    for kb in range(8):
                        if s == 0 and kb > 4 * tg + 3:
                            continue
                        r = (kb - 4 * tg) if (s == 0 and kb >= 4 * tg) else 0
                        units.append((s, kb, r, s == 0 and kb >= 4 * tg))
                self.ps_set = [0, 1, 2, 3, 4, 5]
                sp_of = {}

                def emit_scores(u):
                    s, kb, r, diag = units[u]
                    pi = self.getps()
                    sp_of[u] = pi
                    c0 = r * 128

                    def mms(e, s=s, kb=kb, c0=c0, pi=pi, tg=tg, hb=hb):
                        e.matmul(self.ps[pi][:, c0:512], lhsT=KT[s][:, kb * 128:(kb + 1) * 128], rhs=qT[hb][:, tg * 512 + c0:(tg + 1) * 512], start=True, stop=False)
                        return e.matmul(self.ps[pi][:, c0:512], lhsT=KPE[s][0:64, kb * 128:(kb + 1) * 128], rhs=qpe[hb][0:64, tg * 512 + c0:(tg + 1) * 512], start=False, stop=True)
                    P.op('pe', mms, reads=[r_K[s], r_KPE[s], r_q[hb], r_qpe[hb]], writes=[self.r_ps[pi]])

                def emit_pv(u, pt, first, last):
                    s, kb, r, diag = units[u]
                    c0 = r * 128

                    def mmp(e, s=s, kb=kb, c0=c0, pt=pt, first=first, last=last, hb=hb):
                        e.matmul(self.ps[6][:, c0:512], lhsT=VV[s][:, kb, :], rhs=PT[pt][:, c0:512], start=first, stop=last)
                        return e.matmul(self.ps[7][:, c0:512], lhsT=self.ones, rhs=PT[pt][:, c0:512], start=first, stop=last)
                    P.op('pe', mmp, reads=[r_Vv[s], r_PT[pt]], writes=[self.r_ps[6], self.r_ps[7]])
                LA = 4
                for u0 in range(min(LA, len(units))):
                    emit_scores(u0)
                for u in range(len(units)):
                    s, kb, r, diag = units[u]
                    c0 = r * 128
                    pi = sp_of[u]
                    pt = pti % 5
                    pti += 1
                    if s == 0:
                        P.op('act', lambda e, pi=pi, pt=pt, c0=c0: e.activation(out=PT[pt][:, c0:512], in_=self.ps[pi][:, c0:512], func=AF.Exp, scale=SCALE_MLA),
                             reads=[self.r_ps[pi]], writes=[r_PT[pt]])
                    else:
                        P.op('act', lambda e, pi=pi, pt=pt, s=s: e.activation(out=PT[pt][:, :], in_=self.ps[pi][:, :], func=AF.Exp, scale=SCALE_MLA, bias=self.ctl[:, s - 1:s]),
                             reads=[self.r_ps[pi]], writes=[r_PT[pt]])
                    if diag:
                        P.op('pool', lambda e, pt=pt, c0=c0: e.tensor_tensor(out=PT[pt][:, c0:c0 + 128], in0=PT[pt][:, c0:c0 + 128], in1=self.tri, op=ALU.mult),
                             reads=[r_PT[pt]], writes=[r_PT[pt]])
                    if u + LA < len(units):
                        emit_scores(u + LA)
                    emit_pv(u, pt, u == 0, u == len(units) - 1)
                P.op('dve', lambda e: e.reciprocal(out=rsb[:, :], in_=self.ps[7][:, :]), reads=[self.r_ps[7]], writes=[r_rsb])
                P.op('dve', lambda e, tg=tg, hd=hd: e.tensor_tensor(out=omlaT[:, hd, tg * 512:(tg + 1) * 512], in0=self.ps[6][:, :], in1=rsb[:, :], op=ALU.mult),
                     reads=[self.r_ps[6], r_rsb], writes=[r_omla])
        self.ps_set = list(range(8))
        P.barrier()

    def conv(self, l, X, gs, r_gs, oconvT, r_oconv):
        P = self.P
        P.dma('sp', self.convw[:, 0:12], self.dram['L%d_convw' % l], writes=[self.r_lc], key='lc')
        z = X.alloc((T + 2) * 4 + 56, F32)
        y = X.alloc(T * 4, F32)
        cg = X.alloc(T * 4, F32)
        halo = X.alloc(64, F32)
        r_z, r_y, r_cg, r_halo = Res(), Res(), Res(), Res()
        for i in range(3):
            if i == 0:
                P.op('dve', lambda e, i=i: e.tensor_scalar(out=halo[:, 0:8], in0=gs[:, i, 258:266], scalar1=self.ctl[:, 12 + i:13 + i], scalar2=None, op0=ALU.mult), reads=[r_gs], writes=[r_halo])
            else:
                P.op('dve', lambda e, i=i: e.scalar_tensor_tensor(out=halo[:, 0:8], in0=gs[:, i, 258:266], scalar=self.ctl[:, 12 + i:13 + i], in1=halo[:, 0:8], op0=ALU.mult, op1=ALU.add),
                     reads=[r_gs, r_halo], writes=[r_halo])
        wc = self.dram['L%d_wconv' % l]
        for k in range(2):
            wts = [self.wload(wc[base + k]) for base in (2, 4, 0)]
            for sub in range(2):
                ch = 2 * k + sub
                for tg in range(2):
                    def mmfor(wt, r_wt, sub=sub, tg=tg):
                        pi = self.getps()
                        wv = v3(wt[:, 0:KC * 256], 256)

                        def mm(e, wv=wv, pi=pi, sub=sub, tg=tg):
                            ins = None
                            for kc in range(KC):
                                ins = e.matmul(self.ps[pi][:, :], lhsT=wv[:, kc, sub * 128:(sub + 1) * 128], rhs=self.uT[:, kc, tg * 512:(tg + 1) * 512], start=(kc == 0), stop=(kc == KC - 1))
                            return ins
                        P.op('pe', mm, reads=[r_wt] + self.r_uT[tg * 4:(tg + 1) * 4], writes=[self.r_ps[pi]])
                        return pi
                    pc = mmfor(*wts[0])
                    P.op('act', lambda e, pc=pc, tg=tg: e.copy(out=cg[:, tg * 512:(tg + 1) * 512], in_=self.ps[pc][:, :]), reads=[self.r_ps[pc]], writes=[r_cg])
                    ph = mmfor(*wts[1])
                    P.op('dve', lambda e, ph=ph, tg=tg: e.tensor_tensor(out=z[:, 2 + tg * 512:2 + (tg + 1) * 512], in0=cg[:, tg * 512:(tg + 1) * 512], in1=self.ps[ph][:, :], op=ALU.mult),
                         reads=[self.r_ps[ph], r_cg], writes=[r_z])
                P.op('dve', lambda e, ch=ch: e.tensor_copy(out=z[:, 0:2], in_=halo[:, ch * 2:ch * 2 + 2]), reads=[r_halo], writes=[r_z])
                cw = self.convw
                P.op('dve', lambda e, ch=ch: e.tensor_scalar(out=y[:, :], in0=z[:, 2:T + 2], scalar1=cw[:, ch * 3 + 2:ch * 3 + 3], scalar2=None, op0=ALU.mult), reads=[r_z, self.r_lc], writes=[r_y])
                P.op('dve', lambda e, ch=ch: e.scalar_tensor_tensor(out=y[:, :], in0=z[:, 1:T + 1], scalar=cw[:, ch * 3 + 1:ch * 3 + 2], in1=y[:, :], op0=ALU.mult, op1=ALU.add), reads=[r_z, r_y], writes=[r_y])
                P.op('dve', lambda e, ch=ch: e.scalar_tensor_tensor(out=y[:, :], in0=z[:, 0:T], scalar=cw[:, ch * 3:ch * 3 + 1], in1=y[:, :], op0=ALU.mult, op1=ALU.add), reads=[r_z, r_y], writes=[r_y])
                for tg in range(2):
                    pb = mmfor(*wts[2], sub=sub, tg=tg)
                    P.op('dve', lambda e, pb=pb, tg=tg, ch=ch: e.tensor_tensor(out=oconvT[:, ch, tg * 512:(tg + 1) * 512], in0=y[:, tg * 512:(tg + 1) * 512], in1=self.ps[pb][:, :], op=ALU.mult),
                         reads=[self.r_ps[pb], r_y], writes=[r_oconv])
        P.barrier()

    def merge(self, l, X, omlaT, r_omla, oglaT, r_ogla, oconvT, r_oconv, mT, r_mT):
        P = self.P
        P.dma('sp', self.bgate[:, 0:48], self.dram['L%d_bgate' % l], writes=[self.r_lc], key='lc')
        sig = [X.alloc(2048, F32) for _ in range(2)]
        acc = [X.alloc(2048, F32) for _ in range(2)]
        r_sig = [Res(), Res()]
        r_acc = [Res(), Res()]
        wmg = self.dram['L%d_wmerge' % l]
        srcs = [(omlaT, r_omla, 8), (oglaT, r_ogla, 4), (oconvT, r_oconv, 4)]
        k = 0
        for ncx in range(16):
            w0, r_w0 = self.wload(wmg[ncx * 2])
            w1, r_w1 = self.wload(wmg[ncx * 2 + 1])
            v0 = v3(w0[:, 0:4096], 128)
            v1 = v3(w1[:, 0:4096], 128)
            gate_w = [(v0, 0, r_w0), (v0, 16, r_w0), (v1, 0, r_w1)]
            proj_off = [16, 24, 28]
            for tg in range(2):
                a = k % 2
                k += 1
                for br in range(3):
                    gv, goff, r_g = gate_w[br]
                    pg = self.getps()

                    def mmg(e, gv=gv, goff=goff, pg=pg, tg=tg):
                        ins = None
                        for kc in range(KC):
                            ins = e.matmul(self.ps[pg][:, :], lhsT=gv[:, goff + kc, :], rhs=self.uT[:, kc, tg * 512:(tg + 1) * 512], start=(kc == 0), stop=(kc == KC - 1))
                        return ins
                    P.op('pe', mmg, reads=[r_g] + self.r_uT[tg * 4:(tg + 1) * 4], writes=[self.r_ps[pg]])
                    sb = br % 2
                    P.op('act', lambda e, pg=pg, sb=sb, br=br, ncx=ncx: e.activation(out=sig[sb][:, :], in_=self.ps[pg][:, :], func=AF.Sigmoid, bias=self.bgate[:, br * 16 + ncx:br * 16 + ncx + 1]),
                         reads=[self.r_ps[pg], self.r_lc], writes=[r_sig[sb]])
                    src, r_src, nk = srcs[br]
                    py = self.getps()

                    def mmy(e, src=src, nk=nk, po=proj_off[br], py=py, tg=tg, v1=v1):
                        ins = None
                        for kk in range(nk):
                            ins = e.matmul(self.ps[py][:, :], lhsT=v1[:, po + kk, :], rhs=src[:, kk, tg * 512:(tg + 1) * 512], start=(kk == 0), stop=(kk == nk - 1))
                        return ins
                    P.op('pe', mmy, reads=[r_w1, r_src], writes=[self.r_ps[py]])
                    if br == 0:
                        P.op('dve', lambda e, py=py, sb=sb, a=a: e.tensor_tensor(out=acc[a][:, :], in0=sig[sb][:, :], in1=self.ps[py][:, :], op=ALU.mult),
                             reads=[self.r_ps[py], r_sig[sb]], writes=[r_acc[a]])
                    else:
                        P.op('dve', lambda e, py=py, sb=sb: e.tensor_tensor(out=sig[sb][:, :], in0=sig[sb][:, :], in1=self.ps[py][:, :], op=ALU.mult),
                             reads=[self.r_ps[py], r_sig[sb]], writes=[r_sig[sb]])
                        if br == 1:
                            P.op('dve', lambda e, sb=sb, a=a: e.tensor_tensor(out=acc[a][:, :], in0=acc[a][:, :], in1=sig[sb][:, :], op=ALU.add), reads=[r_sig[sb], r_acc[a]], writes=[r_acc[a]])
                        else:
                            P.op('dve', lambda e, sb=sb, a=a, ncx=ncx, tg=tg: e.tensor_tensor(out=mT[:, ncx, tg * 512:(tg + 1) * 512], in0=acc[a][:, :], in1=sig[sb][:, :], op=ALU.add),
                                 reads=[r_sig[sb], r_acc[a]], writes=[r_mT[tg]])
        P.barrier()

    def proj_residual(self, xT, r_x, wtiles):
        P = self.P

        def epi(ti, tb, pi):
            P.op('dve', lambda e: e.tensor_tensor(out=self.h[:, tb, ti * 256:(ti + 1) * 256], in0=self.ps[pi][:, 0:256], in1=self.h[:, tb, ti * 256:(ti + 1) * 256], op=ALU.add),
                 reads=[self.r_ps[pi], self.r_h[tb]], writes=[self.r_h[tb]])
        self.lin_tok(xT, r_x, KC, wtiles, 256, epi)
        P.barrier()

    def xattn(self, l, mem_d):
        P = self.P
        X = self.RX
        X.reset()
        self.norm_h(self.dram['L%d_g_xattn_norm' % l])
        memT = v3(X.alloc(KC * MEM * 2), MEM)
        r_memT = [Res(), Res()]
        KTm = v3(X.alloc(KC * MEM * 2), MEM)
        r_KTm = Res()
        Vm = v3(X.alloc(2 * D * 2), D)
        r_Vm = [Res(), Res()]
        save = X.off
        mm_ = v3(X.alloc(2 * D * 4, F32), D)
        r_mm = [Res(), Res()]
        for mb in range(2):
            P.dma('sp', mm_[:, mb, :], mem_d[mb * 128:(mb + 1) * 128, :], writes=[r_mm[mb]], key='mem%d' % mb)
        ub = [X.alloc(4096), X.alloc(4096)]
        r_ub = [Res(), Res()]
        junk = X.alloc(4096)
        self.load_gain(self.dram['L%d_g_mem_norm' % l])
        self.norm_T(lambda tb: mm_[:, tb, :], lambda tb: r_mm[tb], D, memT, lambda tb: r_memT[tb], 2, ub, r_ub, junk)
        P.barrier()
        X.off = save
        qx = v3(X.alloc(KC * T * 2), T)
        r_qx = [[Res() for _ in range(2)] for _ in range(KC)]
        PTm = [X.alloc(1024) for _ in range(4)]
        r_PTm = [Res() for _ in range(4)]
        rsb = X.alloc(2048, F32)
        r_rsb = Res()

        def epi_k(ti, mi, tg, pi):
            ch = ti * 2 + mi
            P.op('act' if ch % 2 == 0 else 'dve', lambda e: (e.copy if ch % 2 == 0 else e.tensor_copy)(out=KTm[:, ch, :], in_=self.ps[pi][:, 0:256]),
                 reads=[self.r_ps[pi]], writes=[r_KTm])
        self.lin_fm(memT, lambda tg: r_memT, KC, self.dram['L%d_xk' % l], 256, [(0, 128), (128, 128)], epi_k, tgs=(0,), ncol_T=256)

        def epi_v(ti, tb, pi):
            P.op('act' if tb == 0 else 'dve', lambda e: (e.copy if tb == 0 else e.tensor_copy)(out=Vm[:, tb, ti * 256:(ti + 1) * 256], in_=self.ps[pi][:, 0:256]),
                 reads=[self.r_ps[pi]], writes=[r_Vm[tb]])
        self.lin_tok(memT, lambda tb: [r_memT[tb]], KC, self.dram['L%d_xv' % l], 256, epi_v, ntb=2)

        def epi_q(ti, mi, tg, pi):
            ch = ti * 2 + mi
            P.op('act' if tg == 0 else 'dve', lambda e: (e.copy if tg == 0 else e.tensor_copy)(out=qx[:, ch, tg * 512:(tg + 1) * 512], in_=self.ps[pi][:, :]),
                 reads=[self.r_ps[pi]], writes=[r_qx[ch][tg]])
        self.lin_fm(self.uT, lambda tg: self.r_uT[tg * 4:(tg + 1) * 4], KC, self.dram['L%d_xq' % l], 256, [(0, 128), (128, 128)], epi_q)
        k = 0
        for hx in range(4):
            for tg in range(2):
                pts = []
                for mb in range(2):
                    pi = self.getps()

                    def mms(e, mb=mb, pi=pi, hx=hx, tg=tg):
                        ins = None
                        for dc in range(4):
                            ins = e.matmul(self.ps[pi][:, :], lhsT=KTm[:, hx * 4 + dc, mb * 128:(mb + 1) * 128], rhs=qx[:, hx * 4 + dc, tg * 512:(tg + 1) * 512], start=(dc == 0), stop=(dc == 3))
                        return ins
                    P.op('pe', mms, reads=[r_KTm] + [r_qx[hx * 4 + dc][tg] for dc in range(4)], writes=[self.r_ps[pi]])
                    pt = k % 4
                    k += 1
                    pts.append(pt)
                    P.op('act', lambda e, pi=pi, pt=pt: e.activation(out=PTm[pt][:, :], in_=self.ps[pi][:, :], func=AF.Exp, scale=SCALE_X), reads=[self.r_ps[pi]], writes=[r_PTm[pt]])
                psu = self.getps()

                def mmsum(e, psu=psu, pts=pts):
                    e.matmul(self.ps[psu][:, :], lhsT=self.ones, rhs=PTm[pts[0]][:, :], start=True, stop=False)
                    return e.matmul(self.ps[psu][:, :], lhsT=self.ones, rhs=PTm[pts[1]][:, :], start=False, stop=True)
                P.op('pe', mmsum, reads=[r_PTm[pts[0]], r_PTm[pts[1]]], writes=[self.r_ps[psu]])
                P.op('dve', lambda e, psu=psu: e.reciprocal(out=rsb[:, :], in_=self.ps[psu][:, :]), reads=[self.r_ps[psu]], writes=[r_rsb])
                for dvc in range(4):
                    po = self.getps()

                    def mmo(e, po=po, pts=pts, hx=hx, dvc=dvc):
                        e.matmul(self.ps[po][:, :], lhsT=Vm[:, 0, hx * 512 + dvc * 128:hx * 512 + (dvc + 1) * 128], rhs=PTm[pts[0]][:, :], start=True, stop=False)
                        return e.matmul(self.ps[po][:, :], lhsT=Vm[:, 1, hx * 512 + dvc * 128:hx * 512 + (dvc + 1) * 128], rhs=PTm[pts[1]][:, :], start=False, stop=True)
                    P.op('pe', mmo, reads=r_Vm + [r_PTm[pts[0]], r_PTm[pts[1]]], writes=[self.r_ps[po]])
                    ch = hx * 4 + dvc
                    P.op('dve', lambda e, po=po, ch=ch, tg=tg: e.tensor_tensor(out=qx[:, ch, tg * 512:(tg + 1) * 512], in0=self.ps[po][:, :], in1=rsb[:, :], op=ALU.mult),
                         reads=[self.r_ps[po], r_rsb], writes=[r_qx[ch][tg]])
        P.barrier()
        allq = [r_qx[ch][tg] for ch in range(KC) for tg in range(2)]
        self.proj_residual(qx, lambda tb: allq, self.dram['L%d_xo' % l])
        X.reset()

    def phase_B_mixer(self, l, gk, ownk, gs_d, hspill):
        P = self.P
        X = self.RX
        X.reset()
        S2 = Region(self.arena, 0, 65536)
        omlaT = v3(S2.alloc(8 * T * 2), T)
        oglaT = v3(S2.alloc(4 * T * 2), T)
        oconvT = v3(S2.alloc(4 * T * 2), T)
        gs = v3(S2.alloc(4 * XS_COLS * 4 + 32, F32)[:, 0:4 * XS_COLS], XS_COLS)
        r_omla, r_ogla, r_oconv, r_gs = Res(), Res(), Res(), Res()
        for i in range(4):
            P.dma('sp', gs[:, i, :], gs_d[i], reads=[self.r_gsd], writes=[r_gs], key='gs')
        s2save = S2.off
        if 'skip_mla' not in self.dbg:
            self.mla(l, S2, X, gk, ownk, omlaT, r_omla)
        if 'omla' in self.dbg:
            self.dump('omla', omlaT[:, :, :], [r_omla], BF16)
        if 'stop_mla' in self.dbg:
            return True
        X.reset()
        S2.off = s2save
        self.gla(l, 'B', X, gs=gs, r_gs=r_gs, oglaT=oglaT, r_ogla=r_ogla)
        if 'ogla' in self.dbg:
            self.dump('ogla', oglaT[:, :, :], [r_ogla], BF16)
        if 'stop_gla' in self.dbg:
            return True
        X.reset()
        self.conv(l, X, gs, r_gs, oconvT, r_oconv)
        if 'oconv' in self.dbg:
            self.dump('oconv', oconvT[:, :, :], [r_oconv], BF16)
        if 'stop_conv' in self.dbg:
            return True
        X.reset()
        mT = v3(X.alloc(KC * T * 2), T)
        r_mT = [Res(), Res()]
        self.merge(l, X, omlaT, r_omla, oglaT, r_ogla, oconvT, r_oconv, mT, r_mT)
        self.load_h(hspill)
        self.proj_residual(mT, lambda tb: r_mT, self.dram['L%d_wout' % l])
        X.reset()

    def load_h(self, src):
        for tb in range(NTB):
            self.P.dma('sp', self.h[:, tb, :], src[tb * 128:(tb + 1) * 128, :], reads=[self.r_hsp], writes=[self.r_h[tb]], key='h%d' % tb)

    def store_h(self, dst, key='hs'):
        for tb in range(NTB):
            self.P.dma('sp', dst[tb * 128:(tb + 1) * 128, :], self.h[:, tb, :], reads=[self.r_h[tb]], writes=[self.r_hsp], key=key)

    def final_norm(self, g_d, y_d):
        P = self.P
        X = self.RX
        X.reset()
        self.load_gain(g_d)
        ob = [X.alloc(8192, F32), X.alloc(8192, F32)]
        r_ob = [Res(), Res()]
        junk = X.alloc(4096)
        r_junk = Res()
        for tb in range(NTB):
            ss, r_ss = self.scal()
            rs, r_rs = self.scal()
            b = tb % 2
            P.op('act', lambda e, tb=tb, ss=ss: e.activation(out=junk[:, :], in_=self.h[:, tb, :], func=AF.Square, accum_out=ss), reads=[self.r_h[tb]], writes=[r_junk, r_ss])
            P.op('dve', lambda e, ss=ss, rs=rs: e.tensor_scalar(out=rs, in0=ss, scalar1=1.0 / D, scalar2=EPS, op0=ALU.mult, op1=ALU.add), reads=[r_ss], writes=[r_rs])
            P.op('act', lambda e, rs=rs: e.activation(out=rs, in_=rs, func=AF.Sqrt), reads=[r_rs], writes=[r_rs])
            P.op('dve', lambda e, rs=rs: e.reciprocal(out=rs, in_=rs), reads=[r_rs], writes=[r_rs])
            P.op('dve', lambda e, tb=tb, b=b, rs=rs: e.scalar_tensor_tensor(out=ob[b][:, :], in0=self.h[:, tb, :], scalar=rs, in1=self.gb[:, :], op0=ALU.mult, op1=ALU.mult),
                 reads=[self.r_h[tb], r_rs, self.r_gb], writes=[r_ob[b]])
            P.dma('sp', y_d[tb * 128:(tb + 1) * 128, :], ob[b][:, :], reads=[r_ob[b]], writes=[r_ob[b]], key='y%d' % b)
        P.barrier()


def build_program(stages, fused=False, dbg=None):
    nc = bass.Bass("TRN2", target_bir_lowering=False)
    layers = sorted({int(s[1]) for s in stages})
    st = ExitStack()
    with st:
        B = Builder(nc, st, layers, SHAPES, dbg)
        P = B.P
        hin = nc.dram_tensor('hin', [T, D], F32, kind='ExternalInput').ap()
        hout = nc.dram_tensor('hout', [T, D], F32, kind='ExternalOutput').ap()
        mem = None
        if any(s[0] == 'B' for s in stages):
            mem = nc.dram_tensor('mem', [MEM, D], F32, kind='ExternalInput').ap()
        ends_with_A = stages[-1][0] == 'A'
        xk_out = xs_out = None
        if not fused:
            if ends_with_A:
                xk_out = nc.dram_tensor('xk', [128, XK_COLS], BF16, kind='ExternalOutput').ap()
                xs_out = nc.dram_tensor('xs', [128, XS_COLS], F32, kind='ExternalOutput').ap()
            if stages[0][0] == 'B':
                gk_ = nc.dram_tensor('gk', [4, 128, XK_COLS], BF16, kind='ExternalInput').ap()
                ownk_ = nc.dram_tensor('ownk', [128, XK_COLS], BF16, kind='ExternalInput').ap()
                gk = [gk_[:, :, j * 1024:(j + 1) * 1024] for j in range(5)]
                ownk = [ownk_[:, j * 1024:(j + 1) * 1024] for j in range(5)]
                gs_d = nc.dram_tensor('gs', [4, 128, XS_COLS], F32, kind='ExternalInput').ap()
        RG = [[0, 1, 2, 3], [4, 5, 6, 7]]
        if fused:
            hsp = nc.dram_tensor('hspill', [T, D], F32).ap()
            xk_i = {l: [nc.dram_tensor('xk%d_%d' % (l, j), [128, 1024], BF16) for j in range(5)] for l in layers}
            gk_i = {l: [nc.dram_tensor('gk%d_%d' % (l, j), [4 * 128, 1024], BF16) for j in range(5)] for l in layers}
            xs_i = {l: nc.dram_tensor('xs%d' % l, [128, XS_COLS], F32) for l in layers}
            gs_i = {l: nc.dram_tensor('gs%d' % l, [4 * 128, XS_COLS], F32) for l in layers}
        hsrc = hin
        first = True
        for sname in stages:
            l = int(sname[1])
            if sname[0] == 'A':
                if first:
                    B.load_h(hin)
                B.ffn(l, 1)
                B.norm_h(B.dram['L%d_g_mix_norm' % l])
                if 'h_ffn1' in B.dbg:
                    B.dump('h_ffn1', B.h[:, :, :], B.r_h)
                if fused:
                    def issue_xk(l=l):
                        for a, b in zip(xk_i[l], gk_i[l]):
                            def ccf(e, a=a, b=b):
                                return e.collective_compute("AllGather", ALU.bypass, replica_groups=RG, ins=[a.ap()], outs=[b.ap()])
                            P.custom('pool', ccf, 1, reads=[B.r_xkd], writes=[B.r_gk], key='ccx')
                    B.cc_after_xk = issue_xk
                    B.store_h(hsp)
                    B.phase_A_exchange(l, [t.ap() for t in xk_i[l]], xs_i[l].ap())
                    hsrc = hsp

                    def ccs(e, a=xs_i[l], b=gs_i[l]):
                        return e.collective_compute("AllGather", ALU.bypass, replica_groups=RG, ins=[a.ap()], outs=[b.ap()])
                    P.custom('pool', ccs, 1, reads=[B.r_xsd], writes=[B.r_gsd], key='ccs')
                    gk = [t.ap().rearrange('(s p) c -> s p c', p=128) for t in gk_i[l]]
                    gs_d = gs_i[l].ap().rearrange('(s p) c -> s p c', p=128)
                    ownk = [t.ap() for t in xk_i[l]]
                else:
                    B.phase_A_exchange(l, [xk_out[:, j * 1024:(j + 1) * 1024] for j in range(5)], xs_out)
                    B.store_h(hout)
                    hsrc = hout
            else:
                if first:
                    B.load_h(hin)
                    B.norm_h(B.dram['L%d_g_mix_norm' % l])
                if B.phase_B_mixer(l, gk, ownk, gs_d, hsrc):
                    break
                if 'h_mix' in B.dbg:
                    B.dump('h_mix', B.h[:, :, :], B.r_h)
                if 'stop_mix' in B.dbg:
                    break
                B.xattn(l, mem)
                if 'h_x' in B.dbg:
                    B.dump('h_x', B.h[:, :, :], B.r_h)
                if 'stop_x' in B.dbg:
                    break
                B.ffn(l, 2)
                if l == DEPTH - 1:
                    gfin = nc.dram_tensor('gfinal', [1, D], F32, kind='ExternalInput').ap()
                    B.final_norm(gfin, hout)
                elif sname == stages[-1]:
                    B.store_h(hout)
            first = False
        P.barrier(full=True)
        P.emit()
    return nc


def _input_names(nc):
    names = []
    for alloc in nc.allocations:
        if isinstance(alloc, mybir.MemoryLocationSet) and alloc.kind == 'ExternalInput':
            names.append(alloc.memorylocations[0].name)
    return names


_PROGS = {}


def _run(stages, per_core, dbg=None, fused=False):
    key = (tuple(stages), tuple(sorted(dbg)) if dbg else None, fused)
    if key not in _PROGS:
        _PROGS[key] = build_program(stages, fused=fused, dbg=dbg)
    nc = _PROGS[key]
    names = _input_names(nc)
    in_maps = [{n: pc[n] for n in names if n in pc} for pc in per_core]
    res = run_bass_kernel_spmd(nc, in_maps, core_ids=list(range(NCORES)))
    return res.results


def prepare(inputs):
    inp = {k: np.asarray(v) for k, v in inputs.items()}
    x = np.ascontiguousarray(inp['x'], dtype=np.float32).reshape(NCORES, T, D)
    mem = np.ascontiguousarray(inp['mem'], dtype=np.float32)
    base = {'cbf': const_bf(), 'cf32': const_f32(), 'gfinal': np.ascontiguousarray(inp['final_norm'][None, :], dtype=np.float32)}
    for l in range(DEPTH):
        for k, v in layer_weights(inp, l).items():
            base['L%d_%s' % (l, k)] = v
    per_core = []
    for c in range(NCORES):
        ctl, rope = core_tables(c)
        d = dict(base)
        d.update({'ctl': ctl, 'rope': rope, 'mem': mem[c // 4], 'hin': x[c]})
        per_core.append(d)
    return per_core


def kernel(**inputs):
    per_core = prepare(inputs)

    def exchange(res):
        for c in range(NCORES):
            b0 = (c // 4) * 4
            per_core[c]['hin'] = res[c]['hout']
            per_core[c]['gk'] = np.ascontiguousarray(np.stack([res[b0 + i]['xk'] for i in range(4)]))
            per_core[c]['gs'] = np.ascontiguousarray(np.stack([res[b0 + i]['xs'] for i in range(4)]))
            per_core[c]['ownk'] = res[c]['xk']

    if FUSED:
        r = _run(['A0', 'B0', 'A1', 'B1'], per_core, fused=True)
    else:
        r = _run(['A0'], per_core)
        exchange(r)
        r = _run(['B0', 'A1'], per_core)
        exchange(r)
        r = _run(['B1'], per_core)
    y = np.stack([r[c]['hout'] for c in range(NCORES)]).reshape(2, 4 * T, D)
    return y.astype(np.float32)
```

```python
import numpy as np
import ml_dtypes
from contextlib import ExitStack
import concourse.bass as bass
import concourse.mybir as mybir
from concourse.bass_utils import run_bass_kernel_spmd

F32 = mybir.dt.float32
BF16 = mybir.dt.bfloat16
AF = mybir.ActivationFunctionType
ALU = mybir.AluOpType

NCORES = 8
T = 1024
D = 2048
FF = 5632
KC = 16
NTB = 8
NHC = 44
HG = 4
CPG = 11
EPS = 1e-6
DEPTH = 2
MEM = 256
SCALE_MLA = 192 ** -0.5
SCALE_X = 512 ** -0.5
XK_COLS = 5120
XS_COLS = 266
NEG = -30000.0
FUSED = True

ENGS = ['pe', 'act', 'dve', 'pool', 'sp']


class Res:
    __slots__ = ('w', 'r')

    def __init__(self):
        self.w = None
        self.r = []


class Prog:
    def __init__(self, nc, stack):
        self.nc = nc
        self.stack = stack
        self.streams = {e: [] for e in ENGS}
        self.cnt = {e: 0 for e in ENGS}
        self.sem = {e: stack.enter_context(nc.semaphore('s_' + e)) for e in ENGS}
        self.waited = {e: {} for e in ENGS}
        self.semobj = {}
        self.dma_sems = {}
        self.pool_pending = None

    def _waits(self, eng, reads, writes):
        need = {}
        for r in reads:
            if r.w is not None:
                s, v = r.w
                if need.get(s, 0) < v:
                    need[s] = v
        for w in writes:
            if w.w is not None:
                s, v = w.w
                if need.get(s, 0) < v:
                    need[s] = v
            for (s, v) in w.r:
                if need.get(s, 0) < v:
                    need[s] = v
        out = []
        me = id(self.sem[eng])
        for s, v in need.items():
            if s == me and eng == 'pe':
                continue
            if self.waited[eng].get(s, 0) < v:
                self.waited[eng][s] = v
                out.append((self.semobj[s], v))
        return out

    def _tok(self, semh, v):
        self.semobj[id(semh)] = semh
        return (id(semh), v)

    def _mark(self, tok, reads, writes):
        for r in reads:
            r.r.append(tok)
        for w in writes:
            w.w = tok
            w.r = []

    @staticmethod
    def _flat(lst):
        out = []
        for x in lst:
            if isinstance(x, (list, tuple)):
                out.extend(Prog._flat(x))
            else:
                out.append(x)
        return out

    def op(self, eng, fn, reads=(), writes=()):
        reads, writes = self._flat(reads), self._flat(writes)
        waits = self._waits(eng, reads, writes)
        if eng == 'pool' and self.pool_pending:
            for s, v in self.pool_pending:
                if s != id(self.sem['pool']) and self.waited['pool'].get(s, 0) < v:
                    self.waited['pool'][s] = v
                    waits.append((self.semobj[s], v))
            self.pool_pending = None
        self.cnt[eng] += 1
        semh = self.sem[eng]
        tok = self._tok(semh, self.cnt[eng])
        self.streams[eng].append((waits, fn, (semh, 1)))
        self._mark(tok, reads, writes)
        return tok

    def dma(self, q, out, in_, reads=(), writes=(), key='d'):
        reads, writes = self._flat(reads), self._flat(writes)
        waits = self._waits(q, reads, writes)
        if key not in self.dma_sems:
            self.dma_sems[key] = [self.stack.enter_context(self.nc.semaphore('d_' + key)), 0]
        ent = self.dma_sems[key]
        ent[1] += 16
        tok = self._tok(ent[0], ent[1])

        def fn(e, out=out, in_=in_):
            return e.dma_start(out=out, in_=in_)
        self.streams[q].append((waits, fn, (ent[0], 16)))
        self._mark(tok, reads, writes)
        return tok

    def custom(self, q, fn, inc, reads=(), writes=(), key='cc'):
        reads, writes = self._flat(reads), self._flat(writes)
        waits = self._waits(q, reads, writes)
        if key not in self.dma_sems:
            self.dma_sems[key] = [self.stack.enter_context(self.nc.semaphore('d_' + key)), 0]
        ent = self.dma_sems[key]
        ent[1] += inc
        tok = self._tok(ent[0], ent[1])
        self.streams[q].append((waits, fn, (ent[0], inc)))
        self._mark(tok, reads, writes)
        return tok

    def barrier(self, full=False):
        toks = [(id(self.sem[e]), self.cnt[e]) for e in ENGS if self.cnt[e] > 0]
        for k, ent in self.dma_sems.items():
            toks.append((id(ent[0]), ent[1]))
        for e in ENGS:
            if e == 'pool' and not full:
                self.pool_pending = toks
                continue
            out = []
            for s, v in toks:
                if s == id(self.sem[e]) and e == 'pe':
                    continue
                if self.waited[e].get(s, 0) < v:
                    self.waited[e][s] = v
                    out.append((self.semobj[s], v))
            if out:
                self.streams[e].append((out, None, None))

    def emit(self):
        nc = self.nc
        streams = self.streams

        def run(e, lst):
            for waits, fn, inc in lst:
                for s, v in waits:
                    e.wait_ge(s, v)
                if fn is not None:
                    ins = fn(e)
                    if inc is not None:
                        ins.then_inc(inc[0], inc[1])

        with nc.Block() as block:
            @block.tensor
            def _(e):
                run(e, streams['pe'])

            @block.scalar
            def _(e):
                run(e, streams['act'])

            @block.vector
            def _(e):
                run(e, streams['dve'])

            @block.gpsimd
            def _(e):
                run(e, streams['pool'])

            @block.sync
            def _(e):
                run(e, streams['sp'])


class Region:
    def __init__(self, arena, lo, hi):
        self.arena, self.lo, self.hi, self.off = arena, lo, hi, lo

    def reset(self):
        self.off = self.lo

    def alloc(self, nbytes, dtype=BF16):
        nbytes = (nbytes + 63) // 64 * 64
        assert self.off + nbytes <= self.hi, ('region overflow', self.off, nbytes, self.hi)
        a = self.arena[:, self.off // 2:(self.off + nbytes) // 2]
        self.off += nbytes
        if dtype != BF16:
            a = a.bitcast(dtype)
        return a


def v3(ap, b):
    return ap.rearrange('p (a b) -> p a b', b=b)


def tile_w(W, ncols):
    K, N = W.shape
    kc = K // 128
    t = W.reshape(kc, 128, N // ncols, ncols).transpose(2, 1, 0, 3)
    return np.ascontiguousarray(t).reshape(N // ncols, 128, kc * ncols)


def prep_ffn_w1(w_in):
    g = w_in[:, :FF].reshape(KC, 128, NHC, 128)
    u = w_in[:, FF:].reshape(KC, 128, NHC, 128)
    t = np.stack([g, u], axis=3).transpose(2, 1, 0, 3, 4)
    return np.ascontiguousarray(t).reshape(NHC, 128, KC * 256)


def prep_ffn_w2(w_out):
    t = w_out.reshape(HG, CPG, 128, 4, 512).transpose(0, 3, 2, 1, 4)
    return np.ascontiguousarray(t).reshape(HG * 4, 128, CPG * 512)


def layer_weights(inp, l):
    W = {}
    win = inp['mix_w_in'][l]
    W['f1w1'] = prep_ffn_w1(inp['ffn1_w_in'][l])
    W['f1w2'] = prep_ffn_w2(inp['ffn1_w_out'][l])
    W['f2w1'] = prep_ffn_w1(inp['ffn2_w_in'][l])
    W['f2w2'] = prep_ffn_w2(inp['ffn2_w_out'][l])
    W['wq'] = tile_w(win[:, 0:512], 256)
    W['wkv'] = tile_w(win[:, 512:1024], 256)
    kr = win[:, 1024:1088]
    W['wkr'] = tile_w(np.concatenate([kr, kr[:, 32:], kr[:, :32]], axis=1), 128)
    W['wgla'] = tile_w(win[:, 1088:2624], 256)
    W['walow'] = tile_w(win[:, 2624:2640], 16)
    W['wconv'] = tile_w(win[:, 2640:4176], 256)
    gates = win[:, 4176:]
    mt = np.zeros((16, 2, 128, 32, 128), np.float32)
    gm = gates[:, 0:2048].reshape(16, 128, 16, 128)
    gg = gates[:, 2048:4096].reshape(16, 128, 16, 128)
    gc = gates[:, 4096:6144].reshape(16, 128, 16, 128)
    pm = inp['mla_w_proj'][l].reshape(8, 128, 16, 128)
    pg = inp['gla_w_proj'][l].reshape(4, 128, 16, 128)
    pc = inp['conv_w_proj'][l].reshape(4, 128, 16, 128)
    mt[:, 0, :, 0:16] = gm.transpose(2, 1, 0, 3)
    mt[:, 0, :, 16:32] = gg.transpose(2, 1, 0, 3)
    mt[:, 1, :, 0:16] = gc.transpose(2, 1, 0, 3)
    mt[:, 1, :, 16:24] = pm.transpose(2, 1, 0, 3)
    mt[:, 1, :, 24:28] = pg.transpose(2, 1, 0, 3)
    mt[:, 1, :, 28:32] = pc.transpose(2, 1, 0, 3)
    W['wmerge'] = mt.reshape(32, 128, 32 * 128)
    W['wout'] = tile_w(inp['mix_w_out'][l], 256)
    W['xq'] = tile_w(inp['xattn_w_q'][l], 256)
    W['xk'] = tile_w(inp['xattn_w_kv'][l][:, :D], 256)
    W['xv'] = tile_w(inp['xattn_w_kv'][l][:, D:], 256)
    W['xo'] = tile_w(inp['xattn_w_o'][l], 256)
    uq = inp['mla_w_uq'][l].reshape(512, 8, 192)
    ukv = inp['mla_w_ukv'][l].reshape(512, 8, 256)
    hh = np.concatenate([uq[:, :, 0:128], uq[:, :, 128:192], uq[:, :, 160:192], uq[:, :, 128:160], ukv], axis=2)
    hh = hh.reshape(4, 128, 8, 512).transpose(2, 1, 0, 3)
    W['wmla'] = np.ascontiguousarray(hh).reshape(8, 128, 4 * 512)
    W['wa2'] = np.ascontiguousarray(inp['gla_w_a2'][l])
    W['ba'] = np.ascontiguousarray(inp['gla_b_a'][l][None, :])
    for nm in ('ffn1_norm', 'mix_norm', 'xattn_norm', 'mem_norm', 'ffn2_norm', 'mla_q_norm', 'mla_kv_norm', 'gla_norm'):
        W['g_' + nm] = np.ascontiguousarray(inp[nm][l][None, :])
    W['convw'] = np.ascontiguousarray(inp['conv_w'][l][:, 0, :].reshape(3, 4, 128).transpose(2, 1, 0)).reshape(128, 12)
    W['bgate'] = np.ascontiguousarray(inp['mix_b_gate'][l].reshape(3, 16, 128).transpose(2, 0, 1)).reshape(128, 48)
    return {k: np.ascontiguousarray(v, dtype=np.float32) for k, v in W.items()}


def const_bf():
    c = np.zeros((128, 384), np.float32)
    c[:, 0:128] = np.eye(128)
    c[:, 128:256] = 1.0
    c[:, 256:384] = np.triu(np.ones((128, 128)))
    return c.astype(ml_dtypes.bfloat16)


def const_f32():
    c = np.zeros((128, 384), np.float32)
    tri = np.triu(np.ones((128, 128), np.float32))
    c[:, 0:128] = -tri / 16.0
    c[:, 128:256] = -(1.0 - tri) / 16.0
    c[:, 256:384] = -1.0 / 16.0
    return c


def core_tables(c):
    j = c % 4
    ctl = np.zeros((128, 16), np.float32)
    for i in range(4):
        ctl[:, i] = 0.0 if i < j else NEG
        ctl[:, 4 + i] = 1.0 if i < j else 0.0
        ctl[:, 8 + i] = 0.0 if i < j else 1.0
        ctl[:, 12 + i] = 1.0 if i == j - 1 else 0.0
    inv_freq = (1.0 / (np.float32(10000.0) ** (np.arange(0, 64, 2, dtype=np.float32) / np.float32(64)))).astype(np.float32)
    pos = (np.arange(T, dtype=np.float32) + np.float32(j * T))
    ang = (pos[:, None] * inv_freq[None, :]).astype(np.float32)
    cos, sin = np.cos(ang).astype(np.float32).T, np.sin(ang).astype(np.float32).T
    rope = np.zeros((64, 2 * T), np.float32)
    rope[0:32, 0:T] = cos
    rope[32:64, 0:T] = cos
    rope[0:32, T:] = -sin
    rope[32:64, T:] = sin
    return ctl, rope


LAYER_KEYS = ['f1w1', 'f1w2', 'f2w1', 'f2w2', 'wq', 'wkv', 'wkr', 'wgla', 'walow', 'wconv', 'wmerge', 'wout', 'xq', 'xk', 'xv',
              'xo', 'wmla', 'wa2', 'ba', 'g_ffn1_norm', 'g_mix_norm', 'g_xattn_norm', 'g_mem_norm', 'g_ffn2_norm',
              'g_mla_q_norm', 'g_mla_kv_norm', 'g_gla_norm', 'convw', 'bgate']


class LazyDram(dict):
    def __init__(self, nc, shapes):
        super().__init__()
        self.nc, self.shapes = nc, shapes

    def __missing__(self, name):
        k = name.split('_', 1)[1]
        ap = self.nc.dram_tensor(name, list(self.shapes[k]), F32, kind='ExternalInput').ap()
        self[name] = ap
        return ap


SHAPES = {'f1w1': (44, 128, 4096), 'f2w1': (44, 128, 4096), 'f1w2': (16, 128, 5632), 'f2w2': (16, 128, 5632),
          'wq': (2, 128, 4096), 'wkv': (2, 128, 4096), 'wkr': (1, 128, 2048), 'wgla': (6, 128, 4096), 'walow': (1, 128, 256),
          'wconv': (6, 128, 4096), 'wmerge': (32, 128, 4096), 'wout': (8, 128, 4096), 'xq': (8, 128, 4096), 'xk': (8, 128, 4096),
          'xv': (8, 128, 4096), 'xo': (8, 128, 4096), 'wmla': (8, 128, 2048), 'wa2': (16, 256), 'ba': (1, 256),
          'g_ffn1_norm': (1, D), 'g_mix_norm': (1, D), 'g_xattn_norm': (1, D), 'g_mem_norm': (1, D), 'g_ffn2_norm': (1, D),
          'g_mla_q_norm': (1, 512), 'g_mla_kv_norm': (1, 512), 'g_gla_norm': (1, 512), 'convw': (128, 12), 'bgate': (128, 48)}


class Builder:
    def __init__(self, nc, st, layers, shapes, dbg=None):
        self.nc, self.st = nc, st
        self.P = Prog(nc, st)
        self.dbg = dbg or {}
        self.dram = LazyDram(nc, shapes)
        self.d_cbf = nc.dram_tensor('cbf', [128, 384], BF16, kind='ExternalInput').ap()
        self.d_cf32 = nc.dram_tensor('cf32', [128, 384], F32, kind='ExternalInput').ap()
        self.d_ctl = nc.dram_tensor('ctl', [128, 16], F32, kind='ExternalInput').ap()
        self.d_rope = nc.dram_tensor('rope', [64, 2 * T], F32, kind='ExternalInput').ap()
        TOTAL = 207 * 1024
        self.arena = st.enter_context(nc.sbuf_tensor('arena', [128, TOTAL // 2], BF16))
        P = self.P
        self.RH = Region(self.arena, 0, 65536)
        self.RU = Region(self.arena, 65536, 98304)
        self.RW = Region(self.arena, 98304, 98304 + 36864)
        self.RC = Region(self.arena, 135168, 135168 + 13312)
        self.RX = Region(self.arena, 148480, TOTAL)
        self.h = v3(self.RH.alloc(65536, F32), D)
        self.uT = v3(self.RU.alloc(32768), T)
        self.wsl = [self.RW.alloc(12288) for _ in range(3)]
        self.r_w = [Res() for _ in range(3)]
        self.wi = 0
        self.gb = self.RC.alloc(8192, F32)
        self.cbf = self.RC.alloc(768)
        self.cf32 = self.RC.alloc(1536, F32)
        self.ctl = self.RC.alloc(64, F32)
        self.small = self.RC.alloc(1024, F32)
        self.convw = self.RC.alloc(64, F32)
        self.bgate = self.RC.alloc(192, F32)
        self.wa2 = self.RC.alloc(512)
        self.ba = self.RC.alloc(512)
        self.ident = self.cbf[:, 0:128]
        self.ones = self.cbf[:, 128:256]
        self.tri = self.cbf[:, 256:384]
        self.r_gb = Res()
        self.r_hsp = Res()
        self.r_gk = Res()
        self.r_gsd = Res()
        self.r_xkd = Res()
        self.r_xsd = Res()
        self.cc_after_xk = None
        self.r_c = Res()
        self.r_lc = Res()
        self.r_small = [Res() for _ in range(256)]
        self.si = 0
        self.r_h = [Res() for _ in range(NTB)]
        self.r_uT = [[Res() for _ in range(4)] for _ in range(NTB)]
        self.ps = [st.enter_context(nc.psum_tensor('ps%d' % i, [128, 512], F32)) for i in range(8)]
        self.r_ps = [Res() for _ in range(8)]
        self.ps_set = list(range(8))
        self.psi = 0
        P.dma('sp', self.cbf[:, :], self.d_cbf, writes=[self.r_c], key='c')
        P.dma('sp', self.cf32[:, :], self.d_cf32, writes=[self.r_c], key='c')
        P.dma('sp', self.ctl[:, :], self.d_ctl, writes=[self.r_c], key='c')
        P.barrier()
        self.ndump = 0

    def getps(self):
        i = self.ps_set[self.psi % len(self.ps_set)]
        self.psi += 1
        return i

    def scal(self):
        i = self.si % 256
        self.si += 1
        return self.small[:, i:i + 1], self.r_small[i]

    def wload(self, src):
        i = self.wi % 3
        self.wi += 1
        n = src.shape[1]
        assert n * 2 <= 12288
        self.P.dma('pool', self.wsl[i][:, 0:n], src, writes=[self.r_w[i]], key='w%d' % i)
        return self.wsl[i], self.r_w[i]

    def dump(self, name, ap, reads, dtype=F32):
        d = self.nc.dram_tensor('dbg_' + name, list(ap.shape), dtype, kind='ExternalOutput').ap()
        self.P.dma('sp', d, ap, reads=reads, key='dbg')

    def load_gain(self, src, n=D):
        self.P.dma('sp', self.gb[:, 0:n], src[0, :].partition_broadcast(128), writes=[self.r_gb], key='gb')

    def scal_block(self, n):
        if self.si % 256 + n > 256:
            self.si += 256 - self.si % 256
        i = self.si % 256
        self.si += n
        return self.small[:, i:i + n], [self.r_small[i + k] for k in range(n)]

    def rstd_batch(self, ssq, r_ssq, n, F):
        P = self.P
        P.op('dve', lambda e: e.tensor_scalar(out=ssq, in0=ssq, scalar1=1.0 / F, scalar2=EPS, op0=ALU.mult, op1=ALU.add), reads=[], writes=r_ssq)
        P.op('act', lambda e: e.activation(out=ssq, in_=ssq, func=AF.Sqrt), reads=[], writes=r_ssq)
        P.op('dve', lambda e: e.reciprocal(out=ssq, in_=ssq), reads=[], writes=r_ssq)

    def norm_T(self, src_fn, src_res_fn, F, dstT, dst_res_fn, ntb, ub, r_ub, junk):
        P = self.P
        nk = F // 128
        ssq, r_ssq = self.scal_block(ntb)
        for tb in range(ntb):
            P.op('act', lambda e, tb=tb: e.activation(out=junk[:, 0:F], in_=src_fn(tb), func=AF.Square, accum_out=ssq[:, tb:tb + 1]),
                 reads=[src_res_fn(tb)], writes=[r_ssq[tb]])
        self.rstd_batch(ssq, r_ssq, ntb, F)
        nb = len(ub)
        for tb in range(ntb):
            b = tb % nb
            P.op('dve', lambda e, tb=tb, b=b: e.scalar_tensor_tensor(out=ub[b][:, 0:F], in0=src_fn(tb), scalar=ssq[:, tb:tb + 1], in1=self.gb[:, 0:F], op0=ALU.mult, op1=ALU.mult),
                 reads=[src_res_fn(tb), r_ssq[tb], self.r_gb], writes=[r_ub[b]])
            for k4 in range(nk // 4):
                pi = self.getps()
                pt = self.ps[pi][:, :].bitcast(BF16)

                def tr(e, b=b, k4=k4, pt=pt):
                    ins = None
                    for j in range(4):
                        kc = k4 * 4 + j
                        ins = e.transpose(out=pt[:, j * 128:(j + 1) * 128], in_=ub[b][:, kc * 128:(kc + 1) * 128], identity=self.ident)
                    return ins
                P.op('pe', tr, reads=[r_ub[b]], writes=[self.r_ps[pi]])
                eng = 'act' if k4 % 2 == 0 else 'dve'

                def cp(e, tb=tb, k4=k4, pt=pt, eng=eng):
                    s = pt[:, 0:512].rearrange('p (a b) -> p a b', b=128)
                    d = dstT[:, k4 * 4:(k4 + 1) * 4, tb * 128:(tb + 1) * 128]
                    return e.copy(out=d, in_=s) if eng == 'act' else e.tensor_copy(out=d, in_=s)
                dr = dst_res_fn(tb)
                P.op(eng, cp, reads=[self.r_ps[pi]], writes=[dr[k4] if isinstance(dr, list) else dr])

    def norm_h(self, gain_ap, dstT=None, dst_res=None):
        X = self.RX
        save = X.off
        ub = [X.alloc(4096), X.alloc(4096), X.alloc(4096)]
        r_ub = [Res(), Res(), Res()]
        junk = X.alloc(4096)
        self.load_gain(gain_ap)
        dstT = self.uT if dstT is None else dstT
        dst_res = self.r_uT if dst_res is None else dst_res
        self.norm_T(lambda tb: self.h[:, tb, :], lambda tb: self.r_h[tb], D, dstT, lambda tb: dst_res[tb], NTB, ub, r_ub, junk)
        self.P.barrier()
        X.off = save

    def lin_fm(self, xT, r_x, nkc, wtiles, ncols, M_list, epi, tgs=(0, 1), ncol_T=512):
        P = self.P
        for ti in range(wtiles.shape[0]):
            wt, r_wt = self.wload(wtiles[ti])
            wv = v3(wt[:, 0:nkc * ncols], ncols)
            for mi, (off, M) in enumerate(M_list):
                for tg in tgs:
                    pi = self.getps()

                    def mm(e, wv=wv, off=off, M=M, tg=tg, pi=pi):
                        ins = None
                        for kc in range(nkc):
                            ins = e.matmul(self.ps[pi][0:M, 0:ncol_T], lhsT=wv[:, kc, off:off + M], rhs=xT[:, kc, tg * ncol_T:(tg + 1) * ncol_T],
                                           start=(kc == 0), stop=(kc == nkc - 1))
                        return ins
                    P.op('pe', mm, reads=[r_wt] + list(r_x(tg)), writes=[self.r_ps[pi]])
                    epi(ti, mi, tg, pi)

    def lin_tok(self, xT, r_x, nkc, wtiles, ncols, epi, ntb=NTB):
        P = self.P
        for ti in range(wtiles.shape[0]):
            wt, r_wt = self.wload(wtiles[ti])
            wv = v3(wt[:, 0:nkc * ncols], ncols)
            for tb in range(ntb):
                pi = self.getps()

                def mm(e, wv=wv, tb=tb, pi=pi):
                    ins = None
                    for kc in range(nkc):
                        ins = e.matmul(self.ps[pi][:, 0:ncols], lhsT=xT[:, kc, tb * 128:(tb + 1) * 128], rhs=wv[:, kc, :],
                                       start=(kc == 0), stop=(kc == nkc - 1))
                    return ins
                P.op('pe', mm, reads=[r_wt] + list(r_x(tb)), writes=[self.r_ps[pi]])
                epi(ti, tb, pi)

    def ffn(self, l, which):
        P = self.P
        X = self.RX
        self.norm_h(self.dram['L%d_g_ffn%d_norm' % (l, which)])
        X.reset()
        actT = [v3(X.alloc(CPG * T * 2), T) for _ in range(2)]
        sg = [X.alloc(2048, F32) for _ in range(2)]
        r_act = [[Res() for _ in range(CPG)] for _ in range(2)]
        r_sg = [Res(), Res()]
        w1 = self.dram['L%d_f%dw1' % (l, which)]
        w2 = self.dram['L%d_f%dw2' % (l, which)]
        uT = self.uT
        for g in range(HG):
            ab = g % 2
            for cl in range(CPG):
                c = g * CPG + cl
                wt, r_wt = self.wload(w1[c])
                wv = v3(wt[:, 0:KC * 256], 256)
                for tg in range(2):
                    pg, pu = self.getps(), self.getps()

                    def mm(e, wv=wv, tg=tg, pg=pg, pu=pu):
                        ins = None
                        for half, pp in ((0, pg), (1, pu)):
                            for kc in range(KC):
                                ins = e.matmul(self.ps[pp][:, :], lhsT=wv[:, kc, half * 128:(half + 1) * 128],
                                               rhs=uT[:, kc, tg * 512:(tg + 1) * 512], start=(kc == 0), stop=(kc == KC - 1))
                        return ins
                    P.op('pe', mm, reads=[r_wt] + self.r_uT[tg * 4:(tg + 1) * 4], writes=[self.r_ps[pg], self.r_ps[pu]])
                    sb = (cl * 2 + tg) % 2
                    P.op('act', lambda e, pg=pg, sb=sb: e.activation(out=sg[sb][:, :], in_=self.ps[pg][:, :], func=AF.Silu),
                         reads=[self.r_ps[pg]], writes=[r_sg[sb]])
                    P.op('dve', lambda e, pu=pu, sb=sb, ab=ab, cl=cl, tg=tg: e.tensor_tensor(out=actT[ab][:, cl, tg * 512:(tg + 1) * 512], in0=sg[sb][:, :], in1=self.ps[pu][:, :], op=ALU.mult),
                         reads=[self.r_ps[pu], r_sg[sb]], writes=[r_act[ab][cl]])
            for ng in range(4):
                wt, r_wt = self.wload(w2[g * 4 + ng])
                wv = v3(wt[:, 0:CPG * 512], 512)
                for tb in range(NTB):
                    po = self.getps()

                    def mm2(e, wv=wv, tb=tb, po=po, ab=ab):
                        ins = None
                        for cl in range(CPG):
                            ins = e.matmul(self.ps[po][:, :], lhsT=actT[ab][:, cl, tb * 128:(tb + 1) * 128], rhs=wv[:, cl, :],
                                           start=(cl == 0), stop=(cl == CPG - 1))
                        return ins
                    P.op('pe', mm2, reads=[r_wt] + r_act[ab], writes=[self.r_ps[po]])
                    P.op('dve', lambda e, po=po, tb=tb, ng=ng: e.scalar_tensor_tensor(out=self.h[:, tb, ng * 512:(ng + 1) * 512], in0=self.ps[po][:, :], scalar=0.5,
                                                                                    in1=self.h[:, tb, ng * 512:(ng + 1) * 512], op0=ALU.mult, op1=ALU.add),
                         reads=[self.r_ps[po], self.r_h[tb]], writes=[self.r_h[tb]])
        P.barrier()
        X.reset()

    def latent(self, l, wkey, gkey, dstT, r_dst, X):
        P = self.P
        save = X.off
        ub = [X.alloc(1024), X.alloc(1024)]
        r_ub = [Res(), Res()]
        lat = [X.alloc(2048, F32) for _ in range(8)]
        r_lat = [Res() for _ in range(8)]
        junk = X.alloc(1024)
        self.load_gain(self.dram['L%d_%s' % (l, gkey)], 512)
        wt = self.dram['L%d_%s' % (l, wkey)]
        w0, r_w0 = self.wload(wt[0])
        w1, r_w1 = self.wload(wt[1])
        wv = [v3(w0[:, 0:KC * 256], 256), v3(w1[:, 0:KC * 256], 256)]
        for half4 in range(1):
            ssq, r_ssq = self.scal_block(8)
            for i in range(8):
                tb = i
                pi = self.getps()

                def mm(e, tb=tb, pi=pi):
                    ins = None
                    for half in range(2):
                        for kc in range(KC):
                            ins = e.matmul(self.ps[pi][:, half * 256:(half + 1) * 256], lhsT=self.uT[:, kc, tb * 128:(tb + 1) * 128], rhs=wv[half][:, kc, :],
                                           start=(kc == 0), stop=(kc == KC - 1))
                    return ins
                P.op('pe', mm, reads=[r_w0, r_w1, self.r_uT[tb]], writes=[self.r_ps[pi]])
                P.op('act', lambda e, i=i, pi=pi, ssq=ssq: e.activation(out=junk[:, 0:512], in_=self.ps[pi][:, :], func=AF.Square, accum_out=ssq[:, i:i + 1]),
                     reads=[], writes=[self.r_ps[pi], r_ssq[i]])
                P.op('dve', lambda e, i=i, pi=pi: e.tensor_copy(out=lat[i][:, :], in_=self.ps[pi][:, :]), reads=[], writes=[self.r_ps[pi], r_lat[i]])
            self.rstd_batch(ssq, r_ssq, 8, 512)
            for i in range(8):
                tb = i
                b = i % 2
                P.op('dve', lambda e, i=i, b=b, ssq=ssq: e.scalar_tensor_tensor(out=ub[b][:, 0:512], in0=lat[i][:, :], scalar=ssq[:, i:i + 1], in1=self.gb[:, 0:512], op0=ALU.mult, op1=ALU.mult),
                     reads=[r_lat[i], r_ssq[i], self.r_gb], writes=[r_ub[b]])
                p2 = self.getps()
                pt = self.ps[p2][:, :].bitcast(BF16)

                def tr(e, b=b, pt=pt):
                    ins = None
                    for j in range(4):
                        ins = e.transpose(out=pt[:, j * 128:(j + 1) * 128], in_=ub[b][:, j * 128:(j + 1) * 128], identity=self.ident)
                    return ins
                P.op('pe', tr, reads=[r_ub[b]], writes=[self.r_ps[p2]])
                P.op('act', lambda e, tb=tb, pt=pt: e.copy(out=dstT[:, 0:4, tb * 128:(tb + 1) * 128], in_=pt[:, 0:512].rearrange('p (a b) -> p a b', b=128)),
                     reads=[self.r_ps[p2]], writes=[r_dst])
        P.barrier()
        X.off = save

    def gla(self, l, mode, X, xs_out=None, gs=None, r_gs=None, oglaT=None, r_ogla=None):
        P = self.P
        uT = self.uT
        M1, M2, M3 = self.cf32[:, 0:128], self.cf32[:, 128:256], self.cf32[:, 256:384]
        alT = X.alloc(2048)
        r_al = Res()
        P.dma('pool', self.wa2[0:16, 0:256], self.dram['L%d_wa2' % l], writes=[self.r_lc], key='lc')
        P.dma('pool', self.ba[0:1, 0:256], self.dram['L%d_ba' % l], writes=[self.r_lc], key='lc')

        def epi_al(ti, mi, tg, pi):
            P.op('act', lambda e: e.copy(out=alT[0:16, tg * 512:(tg + 1) * 512], in_=self.ps[pi][0:16, :]), reads=[self.r_ps[pi]], writes=[r_al])
        self.lin_fm(uT, lambda tg: self.r_uT[tg * 4:(tg + 1) * 4], KC, self.dram['L%d_walow' % l], 16, [(0, 16)], epi_al)
        qk = v3(X.alloc(NTB * 512 * 4, F32), 512)
        V = v3(X.alloc(NTB * 512 * 2), 512)
        r_qk = [Res() for _ in range(NTB)]
        r_V = [Res() for _ in range(NTB)]
        if mode == 'B':
            sr = v3(X.alloc(NTB * 512 * 2), 512)
            r_sr = [Res() for _ in range(NTB)]

        def epi_g(ti, tb, pi):
            if ti < 2:
                P.op('act', lambda e: e.copy(out=qk[:, tb, ti * 256:(ti + 1) * 256], in_=self.ps[pi][:, 0:256]), reads=[self.r_ps[pi]], writes=[r_qk[tb]])
            elif ti < 4:
                P.op('dve', lambda e: e.tensor_copy(out=V[:, tb, (ti - 2) * 256:(ti - 1) * 256], in_=self.ps[pi][:, 0:256]), reads=[self.r_ps[pi]], writes=[r_V[tb]])
            elif mode == 'B':
                P.op('act', lambda e: e.activation(out=sr[:, tb, (ti - 4) * 256:(ti - 3) * 256], in_=self.ps[pi][:, 0:256], func=AF.Silu), reads=[self.r_ps[pi]], writes=[r_sr[tb]])
        wg = self.dram['L%d_wgla' % l]
        self.lin_tok(uT, lambda tb: [self.r_uT[tb]], KC, wg if mode == 'B' else wg[0:4], 256, epi_g)
        S = [X.alloc(512, F32) for _ in range(2)]
        Sb = [X.alloc(256) for _ in range(2)]
        r_S = [Res(), Res()]
        r_Sb = [Res(), Res()]
        Dt = X.alloc(64, F32)
        r_D = Res()
        lsp = X.alloc(1024, F32)
        r_lsp = Res()
        ex = [X.alloc(1024, F32) for _ in range(3)]
        r_ex = [Res() for _ in range(3)]
        kd = X.alloc(512)
        r_kd = Res()
        if mode == 'B':
            qt = X.alloc(512)
            kt = X.alloc(512)
            r_qt, r_kt = Res(), Res()
            qT = [X.alloc(256) for _ in range(2)]
            kT = [X.alloc(256) for _ in range(2)]
            r_qT, r_kT = [Res(), Res()], [Res(), Res()]
            AT = [X.alloc(512) for _ in range(2)]
            r_AT = [Res(), Res()]
            og = X.alloc(1024)
            r_og = Res()
            osb = X.alloc(2048, F32)
            gjunk = X.alloc(256)
            r_osb = Res()
            self.load_gain(self.dram['L%d_g_gla_norm' % l], 512)
        if mode == 'A':
            for hf in range(2):
                P.op('pool', lambda e, hf=hf: e.memset(S[hf][:, :], 0.0), writes=[r_S[hf]])
            P.op('pool', lambda e: e.memset(Dt[:, :], 1.0), writes=[r_D])
        else:
            tmpL = X.alloc(512, F32)
            r_tmpL = Res()
            coef, r_coef = self.scal()
            for hf in range(2):
                P.op('pool', lambda e, hf=hf: e.memset(S[hf][:, :], 0.0), writes=[r_S[hf]])
                for i in range(3):
                    P.op('dve', lambda e, hf=hf, i=i: e.tensor_scalar(out=coef, in0=gs[:, i, 256 + hf:257 + hf], scalar1=self.ctl[:, 4 + i:5 + i], scalar2=self.ctl[:, 8 + i:9 + i], op0=ALU.mult, op1=ALU.add),
                         reads=[r_gs], writes=[r_coef])
                    P.op('dve', lambda e, hf=hf, i=i: e.tensor_scalar(out=tmpL[:, :], in0=gs[:, i, hf * 128:(hf + 1) * 128], scalar1=self.ctl[:, 4 + i:5 + i], scalar2=None, op0=ALU.mult),
                         reads=[r_gs], writes=[r_tmpL])
                    P.op('dve', lambda e, hf=hf: e.scalar_tensor_tensor(out=S[hf][:, :], in0=S[hf][:, :], scalar=coef, in1=tmpL[:, :], op0=ALU.mult, op1=ALU.add),
                         reads=[r_coef, r_tmpL, r_S[hf]], writes=[r_S[hf]])
        if mode == 'A' and self.cc_after_xk is not None:
            self.cc_after_xk()
        for n in range(NTB):
            px = self.getps()

            def mmx(e, n=n, px=px):
                e.matmul(self.ps[px][:, 0:256], lhsT=alT[0:16, n * 128:(n + 1) * 128], rhs=self.wa2[0:16, 0:256], start=True, stop=False)
                return e.matmul(self.ps[px][:, 0:256], lhsT=self.ones[0:1, 0:128], rhs=self.ba[0:1, 0:256], start=False, stop=True)
            P.op('pe', mmx, reads=[r_al, self.r_lc], writes=[self.r_ps[px]])
            P.op('act', lambda e, px=px: e.activation(out=lsp[:, :], in_=self.ps[px][:, 0:256], func=AF.Exp, scale=-1.0), reads=[self.r_ps[px]], writes=[r_lsp])
            P.op('act', lambda e: e.activation(out=lsp[:, :], in_=lsp[:, :], func=AF.Ln, bias=1.0), reads=[r_lsp], writes=[r_lsp])
            pb = self.getps()

            def mmb(e, pb=pb):
                e.matmul(self.ps[pb][:, 0:256], lhsT=M1, rhs=lsp[:, :], start=True, stop=True)
                return e.matmul(self.ps[pb][:, 256:512], lhsT=M2, rhs=lsp[:, :], start=True, stop=True)
            P.op('pe', mmb, reads=[r_lsp], writes=[self.r_ps[pb]])
            pc = self.getps()

            def mmc(e, pc=pc):
                e.matmul(self.ps[pc][:, 0:2], lhsT=lsp[:, 0:128], rhs=M3[:, 0:2], start=True, stop=True)
                return e.matmul(self.ps[pc][:, 2:4], lhsT=lsp[:, 128:256], rhs=M3[:, 0:2], start=True, stop=True)
            P.op('pe', mmc, reads=[r_lsp], writes=[self.r_ps[pc]])
            cd, r_cd = self.scal()
            cd2, r_cd2 = self.scal()
            P.op('act', lambda e, pc=pc, cd=cd: e.activation(out=cd, in_=self.ps[pc][:, 0:1], func=AF.Exp), reads=[self.r_ps[pc]], writes=[r_cd])
            P.op('act', lambda e, pc=pc, cd2=cd2: e.activation(out=cd2, in_=self.ps[pc][:, 2:3], func=AF.Exp), reads=[self.r_ps[pc]], writes=[r_cd2])
            cds = [(cd, r_cd), (cd2, r_cd2)]
            P.op('act', lambda e, pb=pb: e.activation(out=ex[2][:, :], in_=self.ps[pb][:, 256:512], func=AF.Exp), reads=[self.r_ps[pb]], writes=[r_ex[2]])
            P.op('dve', lambda e, n=n: e.tensor_tensor(out=kd[:, :], in0=qk[:, n, 256:512], in1=ex[2][:, :], op=ALU.mult), reads=[r_qk[n], r_ex[2]], writes=[r_kd])
            if mode == 'B':
                P.op('act', lambda e, pb=pb: e.activation(out=ex[0][:, :], in_=self.ps[pb][:, 0:256], func=AF.Exp), reads=[self.r_ps[pb]], writes=[r_ex[0]])
                P.op('act', lambda e, pb=pb: e.activation(out=ex[1][:, :], in_=self.ps[pb][:, 0:256], func=AF.Exp, scale=-1.0), reads=[self.r_ps[pb]], writes=[r_ex[1]])
                P.op('dve', lambda e, n=n: e.scalar_tensor_tensor(out=qt[:, :], in0=qk[:, n, 0:256], scalar=0.125, in1=ex[0][:, :], op0=ALU.mult, op1=ALU.mult),
                     reads=[r_qk[n], r_ex[0]], writes=[r_qt])
                P.op('dve', lambda e, n=n: e.tensor_tensor(out=kt[:, :], in0=qk[:, n, 256:512], in1=ex[1][:, :], op=ALU.mult), reads=[r_qk[n], r_ex[1]], writes=[r_kt])
                ptq = self.getps()
                ptv = self.ps[ptq][:, :].bitcast(BF16)

                def trq(e, ptv=ptv):
                    e.transpose(out=ptv[:, 0:128], in_=qt[:, 0:128], identity=self.ident)
                    e.transpose(out=ptv[:, 128:256], in_=qt[:, 128:256], identity=self.ident)
                    e.transpose(out=ptv[:, 256:384], in_=kt[:, 0:128], identity=self.ident)
                    return e.transpose(out=ptv[:, 384:512], in_=kt[:, 128:256], identity=self.ident)
                P.op('pe', trq, reads=[r_qt, r_kt], writes=[self.r_ps[ptq]])
                for hf in range(2):
                    P.op('act', lambda e, hf=hf, ptv=ptv: e.copy(out=qT[hf][:, :], in_=ptv[:, hf * 128:(hf + 1) * 128]), reads=[self.r_ps[ptq]], writes=[r_qT[hf]])
                    P.op('act', lambda e, hf=hf, ptv=ptv: e.copy(out=kT[hf][:, :], in_=ptv[:, 256 + hf * 128:256 + (hf + 1) * 128]), reads=[self.r_ps[ptq]], writes=[r_kT[hf]])
                pa = [self.getps(), self.getps()]
                for e_ in range(2):
                    def mma(e, e_=e_, pa=pa):
                        ins = None
                        for hf in range(2):
                            ins = e.matmul(self.ps[pa[e_]][:, hf * 128:(hf + 1) * 128], lhsT=kT[hf][e_ * 64:(e_ + 1) * 64, :], rhs=qT[hf][e_ * 64:(e_ + 1) * 64, :],
                                           start=True, stop=True)
                        return ins
                    P.op('pe', mma, reads=r_qT + r_kT, writes=[self.r_ps[pa[e_]]])
                    for hf in range(2):
                        P.op('dve', lambda e, e_=e_, hf=hf, pa=pa: e.tensor_tensor(out=AT[e_][:, hf * 128:(hf + 1) * 128], in0=self.ps[pa[e_]][:, hf * 128:(hf + 1) * 128], in1=self.tri, op=ALU.mult),
                             reads=[self.r_ps[pa[e_]]], writes=[r_AT[e_]])
                for hf in range(2):
                    P.op('act', lambda e, hf=hf: e.copy(out=Sb[hf][:, :], in_=S[hf][:, :]), reads=[r_S[hf]], writes=[r_Sb[hf]])
                po = [self.getps(), self.getps()]
                for e_ in range(2):
                    def mmo(e, e_=e_, po=po, n=n):
                        ins = None
                        for hf in range(2):
                            hd = 2 * hf + e_
                            e.matmul(self.ps[po[e_]][:, hf * 128:(hf + 1) * 128], lhsT=AT[e_][:, hf * 128:(hf + 1) * 128], rhs=V[:, n, hd * 128:(hd + 1) * 128], start=True, stop=False)
                            ins = e.matmul(self.ps[po[e_]][:, hf * 128:(hf + 1) * 128], lhsT=qT[hf][e_ * 64:(e_ + 1) * 64, :], rhs=Sb[hf][e_ * 64:(e_ + 1) * 64, :], start=False, stop=True)
                        return ins
                    P.op('pe', mmo, reads=[r_AT[e_], r_V[n]] + r_qT + r_Sb, writes=[self.r_ps[po[e_]]])
                ssq4, r_ssq4 = self.scal_block(4)
                for e_ in range(2):
                    for hf in range(2):
                        hd = 2 * hf + e_
                        src = self.ps[po[e_]][:, hf * 128:(hf + 1) * 128]
                        P.op('act', lambda e, src=src, hd=hd, ssq4=ssq4: e.activation(out=gjunk[:, 0:128], in_=src, func=AF.Square, accum_out=ssq4[:, hd:hd + 1]),
                             reads=[], writes=[r_ssq4[hd], self.r_ps[po[e_]]])
                self.rstd_batch(ssq4, r_ssq4, 4, 128)
                for e_ in range(2):
                    for hf in range(2):
                        hd = 2 * hf + e_
                        src = self.ps[po[e_]][:, hf * 128:(hf + 1) * 128]
                        P.op('dve', lambda e, src=src, hd=hd, ssq4=ssq4: e.scalar_tensor_tensor(out=osb[:, hd * 128:(hd + 1) * 128], in0=src, scalar=ssq4[:, hd:hd + 1], in1=self.gb[:, hd * 128:(hd + 1) * 128], op0=ALU.mult, op1=ALU.mult),
                             reads=[r_ssq4[hd], self.r_gb], writes=[r_osb, self.r_ps[po[e_]]])
                P.op('dve', lambda e, n=n: e.tensor_tensor(out=og[:, :], in0=osb[:, :], in1=sr[:, n, :], op=ALU.mult), reads=[r_osb, r_sr[n]], writes=[r_og])
                pt2 = self.getps()
                ptw = self.ps[pt2][:, :].bitcast(BF16)

                def tro(e, ptw=ptw):
                    ins = None
                    for j in range(4):
                        ins = e.transpose(out=ptw[:, j * 128:(j + 1) * 128], in_=og[:, j * 128:(j + 1) * 128], identity=self.ident)
                    return ins
                P.op('pe', tro, reads=[r_og], writes=[self.r_ps[pt2]])
                P.op('act', lambda e, n=n, ptw=ptw: e.copy(out=oglaT[:, 0:4, n * 128:(n + 1) * 128], in_=ptw[:, 0:512].rearrange('p (a b) -> p a b', b=128)),
                     reads=[self.r_ps[pt2]], writes=[r_ogla])
            for hf in range(2):
                pp = self.getps()
                P.op('pe', lambda e, hf=hf, pp=pp, n=n: e.matmul(self.ps[pp][:, 0:256], lhsT=kd[:, hf * 128:(hf + 1) * 128], rhs=V[:, n, hf * 256:(hf + 1) * 256], start=True, stop=True),
                     reads=[r_kd, r_V[n]], writes=[self.r_ps[pp]])
                cdh, r_cdh = cds[hf]
                for e_ in range(2):
                    P.op('dve', lambda e, hf=hf, e_=e_, pp=pp, cdh=cdh: e.scalar_tensor_tensor(out=S[hf][e_ * 64:(e_ + 1) * 64, :], in0=S[hf][e_ * 64:(e_ + 1) * 64, :], scalar=cdh[e_ * 64:(e_ + 1) * 64, :],
                                                                                             in1=self.ps[pp][e_ * 64:(e_ + 1) * 64, e_ * 128:(e_ + 1) * 128], op0=ALU.mult, op1=ALU.add),
                         reads=[self.r_ps[pp], r_cdh, r_S[hf]] + ([r_Sb[hf]] if mode == 'B' else []), writes=[r_S[hf]])
                if mode == 'A':
                    P.op('dve', lambda e, hf=hf, cdh=cdh: e.tensor_tensor(out=Dt[:, hf:hf + 1], in0=Dt[:, hf:hf + 1], in1=cdh, op=ALU.mult), reads=[r_cdh, r_D], writes=[r_D])
        if mode == 'A':
            for hf in range(2):
                P.dma('sp', xs_out[:, hf * 128:(hf + 1) * 128], S[hf][:, :], reads=[r_S[hf]], writes=[self.r_xsd], key='xsS%d' % hf)
            P.dma('sp', xs_out[:, 256:258], Dt[:, 0:2], reads=[r_D], writes=[self.r_xsd], key='xsD')
        P.barrier()

    def phase_A_exchange(self, l, xk_out, xs_out):
        P = self.P
        X = self.RX
        X.reset()
        ckT = v3(X.alloc(4 * T * 2), T)
        r_ck = Res()
        self.latent(l, 'wkv', 'g_mla_kv_norm', ckT, r_ck, X)
        for kc in range(4):
            P.dma('sp', xk_out[kc], ckT[:, kc, :], reads=[r_ck], writes=[self.r_xkd], key='xk%d' % kc)
        rope = X.alloc(2 * T * 4, F32)
        r_rope = Res()
        P.dma('sp', rope[0:64, :], self.d_rope, writes=[r_rope], key='rope')
        kpe = X.alloc(T * 2)
        r_kpe = Res()
        t1 = X.alloc(2048, F32)
        t2 = X.alloc(2048, F32)
        r_t1, r_t2 = Res(), Res()
        hold = {}

        def epi_kr(ti, mi, tg, pi):
            if mi == 0:
                hold[tg] = pi
                P.op('dve', lambda e: e.tensor_tensor(out=t1[0:64, :], in0=self.ps[pi][0:64, :], in1=rope[0:64, tg * 512:(tg + 1) * 512], op=ALU.mult),
                     reads=[self.r_ps[pi], r_rope], writes=[r_t1])
            else:
                P.op('dve', lambda e: e.tensor_tensor(out=t2[0:64, :], in0=self.ps[pi][0:64, :], in1=rope[0:64, T + tg * 512:T + (tg + 1) * 512], op=ALU.mult),
                     reads=[self.r_ps[pi], r_rope], writes=[r_t2])
                P.op('dve', lambda e: e.tensor_tensor(out=kpe[0:64, tg * 512:(tg + 1) * 512], in0=t1[0:64, :], in1=t2[0:64, :], op=ALU.add),
                     reads=[r_t1, r_t2], writes=[r_kpe])
        for tg in range(2):
            self.lin_fm(self.uT, lambda tg_: self.r_uT[tg_ * 4:(tg_ + 1) * 4], KC, self.dram['L%d_wkr' % l], 128, [(0, 64), (64, 64)], epi_kr, tgs=(tg,))
        P.dma('sp', xk_out[4][0:64, :], kpe[0:64, :], reads=[r_kpe], writes=[self.r_xkd], key='xk4')
        zt = X.alloc(64, F32)
        cgs = X.alloc(2048, F32)
        zb = X.alloc(1024)
        r_zt, r_cgs, r_zb = Res(), Res(), Res()
        wc = self.dram['L%d_wconv' % l]
        pcg, phn = self.getps(), self.getps()
        for part, base, pp in (('cg', 2, pcg), ('hin', 4, phn)):
            for k in range(2):
                wt, r_wt = self.wload(wc[base + k])
                wv = v3(wt[:, 0:KC * 256], 256)

                def mm(e, wv=wv, k=k, pp=pp):
                    ins = None
                    for kc in range(KC):
                        ins = e.matmul(self.ps[pp][0:2, k * 256:(k + 1) * 256], lhsT=self.uT[:, kc, T - 2:T], rhs=wv[:, kc, :], start=(kc == 0), stop=(kc == KC - 1))
                    return ins
                P.op('pe', mm, reads=[r_wt] + self.r_uT[4:8], writes=[self.r_ps[pp]])
        P.op('act', lambda e: e.copy(out=cgs[0:2, :], in_=self.ps[pcg][0:2, :]), reads=[self.r_ps[pcg]], writes=[r_cgs])
        P.op('dve', lambda e: e.tensor_tensor(out=zb[0:2, :], in0=cgs[0:2, :], in1=self.ps[phn][0:2, :], op=ALU.mult), reads=[self.r_ps[phn], r_cgs], writes=[r_zb])
        ptz = self.getps()
        ptzv = self.ps[ptz][:, :].bitcast(BF16)

        def trz(e):
            ins = None
            for ch in range(4):
                ins = e.transpose(out=ptzv[:, ch * 2:ch * 2 + 2], in_=zb[0:2, ch * 128:(ch + 1) * 128], identity=self.ident[0:2, 0:2])
            return ins
        P.op('pe', trz, reads=[r_zb], writes=[self.r_ps[ptz]])
        P.op('act', lambda e: e.copy(out=zt[:, 0:8], in_=ptzv[:, 0:8]), reads=[self.r_ps[ptz]], writes=[r_zt])
        P.dma('sp', xs_out[:, 258:266], zt[:, 0:8], reads=[r_zt], writes=[self.r_xsd], key='xsz')
        P.barrier()
        X.reset()
        self.gla(l, 'A', X, xs_out=xs_out)
        X.reset()

    def mla(self, l, S2, X, gk, ownk, omlaT, r_omla):
        P = self.P
        cqT = v3(S2.alloc(4 * T * 2), T)
        r_cq = Res()
        self.latent(l, 'wq', 'g_mla_q_norm', cqT, r_cq, X)
        rope = S2.alloc(2 * T * 4, F32)
        r_rope = Res()
        P.dma('sp', rope[0:64, :], self.d_rope, writes=[r_rope], key='rope')
        segbuf = [X.alloc(XK_COLS * 2) for _ in range(2)]
        r_segb = [Res(), Res()]
        segi = [0]
        KT = [X.alloc(T * 2) for _ in range(5)]
        VV = [v3(X.alloc(T * 2), 128) for _ in range(5)]
        KPE = [S2.alloc(T * 2) for _ in range(5)]
        r_K = [[Res(), Res()] for _ in range(5)]
        r_Vv = [[Res(), Res()] for _ in range(5)]
        r_KPE = [Res() for _ in range(5)]
        qT = [X.alloc(T * 2) for _ in range(2)]
        qpe = [X.alloc(T * 2) for _ in range(2)]
        r_q = [Res(), Res()]
        r_qpe = [Res(), Res()]
        PT = [X.alloc(1024) for _ in range(5)]
        r_PT = [Res() for _ in range(5)]
        t1 = X.alloc(2048, F32)
        t2 = X.alloc(2048, F32)
        r_t1, r_t2 = Res(), Res()
        rsb = X.alloc(2048, F32)
        r_rsb = Res()
        wm = self.dram['L%d_wmla' % l]
        pti = 0
        for hd in range(8):
            hb = hd % 2
            wt, r_wt = self.wload(wm[hd])
            wv = v3(wt[:, 0:4 * 512], 512)
            self.ps_set = [0, 1, 2, 3]
            for tg in range(2):
                pi = self.getps()

                def mmq(e, tg=tg, pi=pi, wv=wv):
                    ins = None
                    for kc in range(4):
                        ins = e.matmul(self.ps[pi][:, :], lhsT=wv[:, kc, 0:128], rhs=cqT[:, kc, tg * 512:(tg + 1) * 512], start=(kc == 0), stop=(kc == 3))
                    return ins
                P.op('pe', mmq, reads=[r_wt, r_cq], writes=[self.r_ps[pi]])
                P.op('act', lambda e, tg=tg, pi=pi, hb=hb: e.copy(out=qT[hb][:, tg * 512:(tg + 1) * 512], in_=self.ps[pi][:, :]), reads=[self.r_ps[pi]], writes=[r_q[hb]])
                for which in range(2):
                    pj = self.getps()

                    def mmr(e, tg=tg, pj=pj, wv=wv, which=which):
                        ins = None
                        for kc in range(4):
                            ins = e.matmul(self.ps[pj][0:64, :], lhsT=wv[:, kc, 128 + which * 64:192 + which * 64], rhs=cqT[:, kc, tg * 512:(tg + 1) * 512], start=(kc == 0), stop=(kc == 3))
                        return ins
                    P.op('pe', mmr, reads=[r_wt, r_cq], writes=[self.r_ps[pj]])
                    tt, r_tt = (t1, r_t1) if which == 0 else (t2, r_t2)
                    P.op('dve', lambda e, tg=tg, pj=pj, which=which, tt=tt: e.tensor_tensor(out=tt[0:64, :], in0=self.ps[pj][0:64, :], in1=rope[0:64, which * T + tg * 512:which * T + (tg + 1) * 512], op=ALU.mult),
                         reads=[self.r_ps[pj], r_rope], writes=[r_tt])
                P.op('dve', lambda e, tg=tg, hb=hb: e.tensor_tensor(out=qpe[hb][0:64, tg * 512:(tg + 1) * 512], in0=t1[0:64, :], in1=t2[0:64, :], op=ALU.add),
                     reads=[r_t1, r_t2], writes=[r_qpe[hb]])
            for s in range(4):
                sb_ = segi[0] % 2
                segi[0] += 1
                for j5 in range(5):
                    srcd = ownk[j5] if s == 0 else gk[j5][s - 1]
                    P.dma('sp', segbuf[sb_][:, j5 * 1024:(j5 + 1) * 1024], srcd, reads=[self.r_gk, self.r_xkd], writes=[r_segb[sb_]], key='seg%d' % sb_)
                if hd == 0:
                    P.op('pool', lambda e, s=s, sb_=sb_: e.tensor_copy(out=KPE[s][0:64, :], in_=segbuf[sb_][0:64, 4096:5120]), reads=[r_segb[sb_]], writes=[r_KPE[s]])
                sv = v3(segbuf[sb_][:, 0:4096], T)
                r_sg_ = r_segb[sb_]
                for ktg in range(2):
                    pi = self.getps()

                    def mmk(e, sv=sv, ktg=ktg, pi=pi, wv=wv):
                        ins = None
                        for kc in range(4):
                            ins = e.matmul(self.ps[pi][:, :], lhsT=wv[:, kc, 256:384], rhs=sv[:, kc, ktg * 512:(ktg + 1) * 512], start=(kc == 0), stop=(kc == 3))
                        return ins
                    P.op('pe', mmk, reads=[r_wt, r_sg_], writes=[self.r_ps[pi]])
                    eng = 'act' if ktg == 0 else 'dve'
                    P.op(eng, lambda e, s=s, ktg=ktg, pi=pi, eng=eng: (e.copy if eng == 'act' else e.tensor_copy)(out=KT[s][:, ktg * 512:(ktg + 1) * 512], in_=self.ps[pi][:, :]),
                         reads=[self.r_ps[pi]], writes=[r_K[s][ktg]])
                for kb4 in range(2):
                    pi = self.getps()

                    def mmv(e, sv=sv, kb4=kb4, pi=pi, wv=wv):
                        ins = None
                        for j in range(4):
                            kb = kb4 * 4 + j
                            for kc in range(4):
                                ins = e.matmul(self.ps[pi][:, j * 128:(j + 1) * 128], lhsT=sv[:, kc, kb * 128:(kb + 1) * 128], rhs=wv[:, kc, 384:512], start=(kc == 0), stop=(kc == 3))
                        return ins
                    P.op('pe', mmv, reads=[r_wt, r_sg_], writes=[self.r_ps[pi]])
                    eng = 'dve' if kb4 == 0 else 'act'
                    P.op(eng, lambda e, s=s, kb4=kb4, pi=pi, eng=eng: (e.copy if eng == 'act' else e.tensor_copy)(out=VV[s][:, kb4 * 4:(kb4 + 1) * 4, :], in_=self.ps[pi][:, :].rearrange('p (a b) -> p a b', b=128)),
                         reads=[self.r_ps[pi]], writes=[r_Vv[s][kb4]])
            for tg in range(2):
                units = []
                for s in range(4):
                    for kb in range(8):
                        if s == 0 and kb > 4 * tg + 3:
                            continue
                        r = (kb - 4 * tg) if (s == 0 and kb >= 4 * tg) else 0
                        units.append((s, kb, r, s == 0 and kb >= 4 * tg))
                self.ps_set = [0, 1, 2, 3, 4, 5]
                sp_of = {}

                def emit_scores(u):
                    s, kb, r, diag = units[u]
                    pi = self.getps()
                    sp_of[u] = pi
                    c0 = r * 128

                    def mms(e, s=s, kb=kb, c0=c0, pi=pi, tg=tg, hb=hb):
                        e.matmul(self.ps[pi][:, c0:512], lhsT=KT[s][:, kb * 128:(kb + 1) * 128], rhs=qT[hb][:, tg * 512 + c0:(tg + 1) * 512], start=True, stop=False)
                        return e.matmul(self.ps[pi][:, c0:512], lhsT=KPE[s][0:64, kb * 128:(kb + 1) * 128], rhs=qpe[hb][0:64, tg * 512 + c0:(tg + 1) * 512], start=False, stop=True)
                    P.op('pe', mms, reads=[r_K[s], r_KPE[s], r_q[hb], r_qpe[hb]], writes=[self.r_ps[pi]])

                def emit_pv(u, pt, first, last):
                    s, kb, r, diag = units[u]
                    c0 = r * 128

                    def mmp(e, s=s, kb=kb, c0=c0, pt=pt, first=first, last=last, hb=hb):
                        e.matmul(self.ps[6][:, c0:512], lhsT=VV[s][:, kb, :], rhs=PT[pt][:, c0:512], start=first, stop=last)
                        return e.matmul(self.ps[7][:, c0:512], lhsT=self.ones, rhs=PT[pt][:, c0:512], start=first, stop=last)
                    P.op('pe', mmp, reads=[r_Vv[s], r_PT[pt]], writes=[self.r_ps[6], self.r_ps[7]])
                LA = 4
                for u0 in range(min(LA, len(units))):
                    emit_scores(u0)
                for u in range(len(units)):
                    s, kb, r, diag = units[u]
                    c0 = r * 128
                    pi = sp_of[u]
                    pt = pti % 5
                    pti += 1
                    if s == 0:
                        P.op('act', lambda e, pi=pi, pt=pt, c0=c0: e.activation(out=PT[pt][:, c0:512], in_=self.ps[pi][:, c0:512], func=AF.Exp, scale=SCALE_MLA),
                             reads=[self.r_ps[pi]], writes=[r_PT[pt]])
                    else:
                        P.op('act', lambda e, pi=pi, pt=pt, s=s: e.activation(out=PT[pt][:, :], in_=self.ps[pi][:, :], func=AF.Exp, scale=SCALE_MLA, bias=self.ctl[:, s - 1:s]),
                             reads=[self.r_ps[pi]], writes=[r_PT[pt]])
                    if diag:
                        P.op('pool', lambda e, pt=pt, c0=c0: e.tensor_tensor(out=PT[pt][:, c0:c0 + 128], in0=PT[pt][:, c0:c0 + 128], in1=self.tri, op=ALU.mult),
                             reads=[r_PT[pt]], writes=[r_PT[pt]])
                    if u + LA < len(units):
                        emit_scores(u + LA)
                    emit_pv(u, pt, u == 0, u == len(units) - 1)
                P.op('dve', lambda e: e.reciprocal(out=rsb[:, :], in_=self.ps[7][:, :]), reads=[self.r_ps[7]], writes=[r_rsb])
                P.op('dve', lambda e, tg=tg, hd=hd: e.tensor_tensor(out=omlaT[:, hd, tg * 512:(tg + 1) * 512], in0=self.ps[6][:, :], in1=rsb[:, :], op=ALU.mult),
                     reads=[self.r_ps[6], r_rsb], writes=[r_omla])
        self.ps_set = list(range(8))
        P.barrier()

    def conv(self, l, X, gs, r_gs, oconvT, r_oconv):
        P = self.P
        P.dma('sp', self.convw[:, 0:12], self.dram['L%d_convw' % l], writes=[self.r_lc], key='lc')
        z = X.alloc((T + 2) * 4 + 56, F32)
        y = X.alloc(T * 4, F32)
        cg = X.alloc(T * 4, F32)
        halo = X.alloc(64, F32)
        r_z, r_y, r_cg, r_halo = Res(), Res(), Res(), Res()
        for i in range(3):
            if i == 0:
                P.op('dve', lambda e, i=i: e.tensor_scalar(out=halo[:, 0:8], in0=gs[:, i, 258:266], scalar1=self.ctl[:, 12 + i:13 + i], scalar2=None, op0=ALU.mult), reads=[r_gs], writes=[r_halo])
            else:
                P.op('dve', lambda e, i=i: e.scalar_tensor_tensor(out=halo[:, 0:8], in0=gs[:, i, 258:266], scalar=self.ctl[:, 12 + i:13 + i], in1=halo[:, 0:8], op0=ALU.mult, op1=ALU.add),
                     reads=[r_gs, r_halo], writes=[r_halo])
        wc = self.dram['L%d_wconv' % l]
        for k in range(2):
            wts = [self.wload(wc[base + k]) for base in (2, 4, 0)]
            for sub in range(2):
                ch = 2 * k + sub
                for tg in range(2):
                    def mmfor(wt, r_wt, sub=sub, tg=tg):
                        pi = self.getps()
                        wv = v3(wt[:, 0:KC * 256], 256)

                        def mm(e, wv=wv, pi=pi, sub=sub, tg=tg):
                            ins = None
                            for kc in range(KC):
                                ins = e.matmul(self.ps[pi][:, :], lhsT=wv[:, kc, sub * 128:(sub + 1) * 128], rhs=self.uT[:, kc, tg * 512:(tg + 1) * 512], start=(kc == 0), stop=(kc == KC - 1))
                            return ins
                        P.op('pe', mm, reads=[r_wt] + self.r_uT[tg * 4:(tg + 1) * 4], writes=[self.r_ps[pi]])
                        return pi
                    pc = mmfor(*wts[0])
                    P.op('act', lambda e, pc=pc, tg=tg: e.copy(out=cg[:, tg * 512:(tg + 1) * 512], in_=self.ps[pc][:, :]), reads=[self.r_ps[pc]], writes=[r_cg])
                    ph = mmfor(*wts[1])
                    P.op('dve', lambda e, ph=ph, tg=tg: e.tensor_tensor(out=z[:, 2 + tg * 512:2 + (tg + 1) * 512], in0=cg[:, tg * 512:(tg + 1) * 512], in1=self.ps[ph][:, :], op=ALU.mult),
                         reads=[self.r_ps[ph], r_cg], writes=[r_z])
                P.op('dve', lambda e, ch=ch: e.tensor_copy(out=z[:, 0:2], in_=halo[:, ch * 2:ch * 2 + 2]), reads=[r_halo], writes=[r_z])
                cw = self.convw
                P.op('dve', lambda e, ch=ch: e.tensor_scalar(out=y[:, :], in0=z[:, 2:T + 2], scalar1=cw[:, ch * 3 + 2:ch * 3 + 3], scalar2=None, op0=ALU.mult), reads=[r_z, self.r_lc], writes=[r_y])
                P.op('dve', lambda e, ch=ch: e.scalar_tensor_tensor(out=y[:, :], in0=z[:, 1:T + 1], scalar=cw[:, ch * 3 + 1:ch * 3 + 2], in1=y[:, :], op0=ALU.mult, op1=ALU.add), reads=[r_z, r_y], writes=[r_y])
                P.op('dve', lambda e, ch=ch: e.scalar_tensor_tensor(out=y[:, :], in0=z[:, 0:T], scalar=cw[:, ch * 3:ch * 3 + 1], in1=y[:, :], op0=ALU.mult, op1=ALU.add), reads=[r_z, r_y], writes=[r_y])
                for tg in range(2):
                    pb = mmfor(*wts[2], sub=sub, tg=tg)
                    P.op('dve', lambda e, pb=pb, tg=tg, ch=ch: e.tensor_tensor(out=oconvT[:, ch, tg * 512:(tg + 1) * 512], in0=y[:, tg * 512:(tg + 1) * 512], in1=self.ps[pb][:, :], op=ALU.mult),
                         reads=[self.r_ps[pb], r_y], writes=[r_oconv])
        P.barrier()

    def merge(self, l, X, omlaT, r_omla, oglaT, r_ogla, oconvT, r_oconv, mT, r_mT):
        P = self.P
        P.dma('sp', self.bgate[:, 0:48], self.dram['L%d_bgate' % l], writes=[self.r_lc], key='lc')
        sig = [X.alloc(2048, F32) for _ in range(2)]
        acc = [X.alloc(2048, F32) for _ in range(2)]
        r_sig = [Res(), Res()]
        r_acc = [Res(), Res()]
        wmg = self.dram['L%d_wmerge' % l]
        srcs = [(omlaT, r_omla, 8), (oglaT, r_ogla, 4), (oconvT, r_oconv, 4)]
        k = 0
        for ncx in range(16):
            w0, r_w0 = self.wload(wmg[ncx * 2])
            w1, r_w1 = self.wload(wmg[ncx * 2 + 1])
            v0 = v3(w0[:, 0:4096], 128)
            v1 = v3(w1[:, 0:4096], 128)
            gate_w = [(v0, 0, r_w0), (v0, 16, r_w0), (v1, 0, r_w1)]
            proj_off = [16, 24, 28]
            for tg in range(2):
                a = k % 2
                k += 1
                for br in range(3):
                    gv, goff, r_g = gate_w[br]
                    pg = self.getps()

                    def mmg(e, gv=gv, goff=goff, pg=pg, tg=tg):
                        ins = None
                        for kc in range(KC):
                            ins = e.matmul(self.ps[pg][:, :], lhsT=gv[:, goff + kc, :], rhs=self.uT[:, kc, tg * 512:(tg + 1) * 512], start=(kc == 0), stop=(kc == KC - 1))
                        return ins
                    P.op('pe', mmg, reads=[r_g] + self.r_uT[tg * 4:(tg + 1) * 4], writes=[self.r_ps[pg]])
                    sb = br % 2
                    P.op('act', lambda e, pg=pg, sb=sb, br=br, ncx=ncx: e.activation(out=sig[sb][:, :], in_=self.ps[pg][:, :], func=AF.Sigmoid, bias=self.bgate[:, br * 16 + ncx:br * 16 + ncx + 1]),
                         reads=[self.r_ps[pg], self.r_lc], writes=[r_sig[sb]])
                    src, r_src, nk = srcs[br]
                    py = self.getps()

                    def mmy(e, src=src, nk=nk, po=proj_off[br], py=py, tg=tg, v1=v1):
                        ins = None
                        for kk in range(nk):
                            ins = e.matmul(self.ps[py][:, :], lhsT=v1[:, po + kk, :], rhs=src[:, kk, tg * 512:(tg + 1) * 512], start=(kk == 0), stop=(kk == nk - 1))
                        return ins
                    P.op('pe', mmy, reads=[r_w1, r_src], writes=[self.r_ps[py]])
                    if br == 0:
                        P.op('dve', lambda e, py=py, sb=sb, a=a: e.tensor_tensor(out=acc[a][:, :], in0=sig[sb][:, :], in1=self.ps[py][:, :], op=ALU.mult),
                             reads=[self.r_ps[py], r_sig[sb]], writes=[r_acc[a]])
                    else:
                        P.op('dve', lambda e, py=py, sb=sb: e.tensor_tensor(out=sig[sb][:, :], in0=sig[sb][:, :], in1=self.ps[py][:, :], op=ALU.mult),
                             reads=[self.r_ps[py], r_sig[sb]], writes=[r_sig[sb]])
                        if br == 1:
                            P.op('dve', lambda e, sb=sb, a=a: e.tensor_tensor(out=acc[a][:, :], in0=acc[a][:, :], in1=sig[sb][:, :], op=ALU.add), reads=[r_sig[sb], r_acc[a]], writes=[r_acc[a]])
                        else:
                            P.op('dve', lambda e, sb=sb, a=a, ncx=ncx, tg=tg: e.tensor_tensor(out=mT[:, ncx, tg * 512:(tg + 1) * 512], in0=acc[a][:, :], in1=sig[sb][:, :], op=ALU.add),
                                 reads=[r_sig[sb], r_acc[a]], writes=[r_mT[tg]])
        P.barrier()

    def proj_residual(self, xT, r_x, wtiles):
        P = self.P

        def epi(ti, tb, pi):
            P.op('dve', lambda e: e.tensor_tensor(out=self.h[:, tb, ti * 256:(ti + 1) * 256], in0=self.ps[pi][:, 0:256], in1=self.h[:, tb, ti * 256:(ti + 1) * 256], op=ALU.add),
                 reads=[self.r_ps[pi], self.r_h[tb]], writes=[self.r_h[tb]])
        self.lin_tok(xT, r_x, KC, wtiles, 256, epi)
        P.barrier()

    def xattn(self, l, mem_d):
        P = self.P
        X = self.RX
        X.reset()
        self.norm_h(self.dram['L%d_g_xattn_norm' % l])
        memT = v3(X.alloc(KC * MEM * 2), MEM)
        r_memT = [[Res() for _ in range(4)] for _ in range(2)]
        KTm = v3(X.alloc(KC * MEM * 2), MEM)
        r_KTm = [Res() for _ in range(KC)]
        Vm = v3(X.alloc(2 * D * 2), D)
        r_Vm = [Res(), Res()]
        save = X.off
        mm_ = v3(X.alloc(2 * D * 4, F32), D)
        r_mm = [Res(), Res()]
        for mb in range(2):
            P.dma('sp', mm_[:, mb, :], mem_d[mb * 128:(mb + 1) * 128, :], writes=[r_mm[mb]], key='mem%d' % mb)
        ub = [X.alloc(4096), X.alloc(4096)]
        r_ub = [Res(), Res()]
        junk = X.alloc(4096)
        self.load_gain(self.dram['L%d_g_mem_norm' % l])
        self.norm_T(lambda tb: mm_[:, tb, :], lambda tb: r_mm[tb], D, memT, lambda tb: r_memT[tb], 2, ub, r_ub, junk)
        P.barrier()
        X.off = save
        qx = v3(X.alloc(KC * T * 2), T)
        r_qx = [[Res() for _ in range(2)] for _ in range(KC)]
        PTm = [X.alloc(1024) for _ in range(4)]
        r_PTm = [Res() for _ in range(4)]
        rsb = X.alloc(2048, F32)
        r_rsb = Res()

        def epi_k(ti, mi, tg, pi):
            ch = ti * 2 + mi
            P.op('act' if ch % 2 == 0 else 'dve', lambda e: (e.copy if ch % 2 == 0 else e.tensor_copy)(out=KTm[:, ch, :], in_=self.ps[pi][:, 0:256]),
                 reads=[self.r_ps[pi]], writes=[r_KTm[ch]])
        self.lin_fm(memT, lambda tg: r_memT, KC, self.dram['L%d_xk' % l], 256, [(0, 128), (128, 128)], epi_k, tgs=(0,), ncol_T=256)

        def epi_v(ti, tb, pi):
            P.op('act' if tb == 0 else 'dve', lambda e: (e.copy if tb == 0 else e.tensor_copy)(out=Vm[:, tb, ti * 256:(ti + 1) * 256], in_=self.ps[pi][:, 0:256]),
                 reads=[self.r_ps[pi]], writes=[r_Vm[tb]])
        self.lin_tok(memT, lambda tb: [r_memT[tb]], KC, self.dram['L%d_xv' % l], 256, epi_v, ntb=2)

        def epi_q(ti, mi, tg, pi):
            ch = ti * 2 + mi
            P.op('act' if tg == 0 else 'dve', lambda e: (e.copy if tg == 0 else e.tensor_copy)(out=qx[:, ch, tg * 512:(tg + 1) * 512], in_=self.ps[pi][:, :]),
                 reads=[self.r_ps[pi]], writes=[r_qx[ch][tg]])
        self.lin_fm(self.uT, lambda tg: self.r_uT[tg * 4:(tg + 1) * 4], KC, self.dram['L%d_xq' % l], 256, [(0, 128), (128, 128)], epi_q)
        k = 0
        for hx in range(4):
            for tg in range(2):
                pts = []
                for mb in range(2):
                    pi = self.getps()

                    def mms(e, mb=mb, pi=pi, hx=hx, tg=tg):
                        ins = None
                        for dc in range(4):
                            ins = e.matmul(self.ps[pi][:, :], lhsT=KTm[:, hx * 4 + dc, mb * 128:(mb + 1) * 128], rhs=qx[:, hx * 4 + dc, tg * 512:(tg + 1) * 512], start=(dc == 0), stop=(dc == 3))
                        return ins
                    P.op('pe', mms, reads=[r_KTm[hx * 4 + dc] for dc in range(4)] + [r_qx[hx * 4 + dc][tg] for dc in range(4)], writes=[self.r_ps[pi]])
                    pt = k % 4
                    k += 1
                    pts.append(pt)
                    P.op('act', lambda e, pi=pi, pt=pt: e.activation(out=PTm[pt][:, :], in_=self.ps[pi][:, :], func=AF.Exp, scale=SCALE_X), reads=[self.r_ps[pi]], writes=[r_PTm[pt]])
                psu = self.getps()

                def mmsum(e, psu=psu, pts=pts):
                    e.matmul(self.ps[psu][:, :], lhsT=self.ones, rhs=PTm[pts[0]][:, :], start=True, stop=False)
                    return e.matmul(self.ps[psu][:, :], lhsT=self.ones, rhs=PTm[pts[1]][:, :], start=False, stop=True)
                P.op('pe', mmsum, reads=[r_PTm[pts[0]], r_PTm[pts[1]]], writes=[self.r_ps[psu]])
                P.op('dve', lambda e, psu=psu: e.reciprocal(out=rsb[:, :], in_=self.ps[psu][:, :]), reads=[self.r_ps[psu]], writes=[r_rsb])
                for dvc in range(4):
                    po = self.getps()

                    def mmo(e, po=po, pts=pts, hx=hx, dvc=dvc):
                        e.matmul(self.ps[po][:, :], lhsT=Vm[:, 0, hx * 512 + dvc * 128:hx * 512 + (dvc + 1) * 128], rhs=PTm[pts[0]][:, :], start=True, stop=False)
                        return e.matmul(self.ps[po][:, :], lhsT=Vm[:, 1, hx * 512 + dvc * 128:hx * 512 + (dvc + 1) * 128], rhs=PTm[pts[1]][:, :], start=False, stop=True)
                    P.op('pe', mmo, reads=r_Vm + [r_PTm[pts[0]], r_PTm[pts[1]]], writes=[self.r_ps[po]])
                    ch = hx * 4 + dvc
                    P.op('dve', lambda e, po=po, ch=ch, tg=tg: e.tensor_tensor(out=qx[:, ch, tg * 512:(tg + 1) * 512], in0=self.ps[po][:, :], in1=rsb[:, :], op=ALU.mult),
                         reads=[self.r_ps[po], r_rsb], writes=[r_qx[ch][tg]])
        P.barrier()
        allq = [r_qx[ch][tg] for ch in range(KC) for tg in range(2)]
        self.proj_residual(qx, lambda tb: allq, self.dram['L%d_xo' % l])
        X.reset()

    def phase_B_mixer(self, l, gk, ownk, gs_d, hspill):
        P = self.P
        X = self.RX
        X.reset()
        S2 = Region(self.arena, 0, 65536)
        omlaT = v3(S2.alloc(8 * T * 2), T)
        oglaT = v3(S2.alloc(4 * T * 2), T)
        oconvT = v3(S2.alloc(4 * T * 2), T)
        gs = v3(S2.alloc(4 * XS_COLS * 4 + 32, F32)[:, 0:4 * XS_COLS], XS_COLS)
        r_omla, r_ogla, r_oconv, r_gs = Res(), Res(), Res(), Res()
        for i in range(4):
            P.dma('sp', gs[:, i, :], gs_d[i], reads=[self.r_gsd], writes=[r_gs], key='gs')
        s2save = S2.off
        if 'skip_mla' not in self.dbg:
            self.mla(l, S2, X, gk, ownk, omlaT, r_omla)
        if 'omla' in self.dbg:
            self.dump('omla', omlaT[:, :, :], [r_omla], BF16)
        if 'stop_mla' in self.dbg:
            return True
        X.reset()
        S2.off = s2save
        self.gla(l, 'B', X, gs=gs, r_gs=r_gs, oglaT=oglaT, r_ogla=r_ogla)
        if 'ogla' in self.dbg:
            self.dump('ogla', oglaT[:, :, :], [r_ogla], BF16)
        if 'stop_gla' in self.dbg:
            return True
        X.reset()
        self.conv(l, X, gs, r_gs, oconvT, r_oconv)
        if 'oconv' in self.dbg:
            self.dump('oconv', oconvT[:, :, :], [r_oconv], BF16)
        if 'stop_conv' in self.dbg:
            return True
        X.reset()
        mT = v3(X.alloc(KC * T * 2), T)
        r_mT = [Res(), Res()]
        self.merge(l, X, omlaT, r_omla, oglaT, r_ogla, oconvT, r_oconv, mT, r_mT)
        self.load_h(hspill)
        self.proj_residual(mT, lambda tb: r_mT, self.dram['L%d_wout' % l])
        X.reset()

    def load_h(self, src):
        for tb in range(NTB):
            self.P.dma('sp', self.h[:, tb, :], src[tb * 128:(tb + 1) * 128, :], reads=[self.r_hsp], writes=[self.r_h[tb]], key='h%d' % tb)

    def store_h(self, dst, key='hs'):
        for tb in range(NTB):
            self.P.dma('sp', dst[tb * 128:(tb + 1) * 128, :], self.h[:, tb, :], reads=[self.r_h[tb]], writes=[self.r_hsp], key=key)

    def final_norm(self, g_d, y_d):
        P = self.P
        X = self.RX
        X.reset()
        self.load_gain(g_d)
        ob = [X.alloc(8192, F32), X.alloc(8192, F32)]
        r_ob = [Res(), Res()]
        junk = X.alloc(4096)
        r_junk = Res()
        for tb in range(NTB):
            ss, r_ss = self.scal()
            rs, r_rs = self.scal()
            b = tb % 2
            P.op('act', lambda e, tb=tb, ss=ss: e.activation(out=junk[:, :], in_=self.h[:, tb, :], func=AF.Square, accum_out=ss), reads=[self.r_h[tb]], writes=[r_junk, r_ss])
            P.op('dve', lambda e, ss=ss, rs=rs: e.tensor_scalar(out=rs, in0=ss, scalar1=1.0 / D, scalar2=EPS, op0=ALU.mult, op1=ALU.add), reads=[r_ss], writes=[r_rs])
            P.op('act', lambda e, rs=rs: e.activation(out=rs, in_=rs, func=AF.Sqrt), reads=[r_rs], writes=[r_rs])
            P.op('dve', lambda e, rs=rs: e.reciprocal(out=rs, in_=rs), reads=[r_rs], writes=[r_rs])
            P.op('dve', lambda e, tb=tb, b=b, rs=rs: e.scalar_tensor_tensor(out=ob[b][:, :], in0=self.h[:, tb, :], scalar=rs, in1=self.gb[:, :], op0=ALU.mult, op1=ALU.mult),
                 reads=[self.r_h[tb], r_rs, self.r_gb], writes=[r_ob[b]])
            P.dma('sp', y_d[tb * 128:(tb + 1) * 128, :], ob[b][:, :], reads=[r_ob[b]], writes=[r_ob[b]], key='y%d' % b)
        P.barrier()


def build_program(stages, fused=False, dbg=None):
    nc = bass.Bass("TRN2", target_bir_lowering=False)
    layers = sorted({int(s[1]) for s in stages})
    st = ExitStack()
    with st:
        B = Builder(nc, st, layers, SHAPES, dbg)
        P = B.P
        hin = nc.dram_tensor('hin', [T, D], F32, kind='ExternalInput').ap()
        hout = nc.dram_tensor('hout', [T, D], F32, kind='ExternalOutput').ap()
        mem = None
        if any(s[0] == 'B' for s in stages):
            mem = nc.dram_tensor('mem', [MEM, D], F32, kind='ExternalInput').ap()
        ends_with_A = stages[-1][0] == 'A'
        xk_out = xs_out = None
        if not fused:
            if ends_with_A:
                xk_out = nc.dram_tensor('xk', [128, XK_COLS], BF16, kind='ExternalOutput').ap()
                xs_out = nc.dram_tensor('xs', [128, XS_COLS], F32, kind='ExternalOutput').ap()
            if stages[0][0] == 'B':
                gk_ = nc.dram_tensor('gk', [4, 128, XK_COLS], BF16, kind='ExternalInput').ap()
                ownk_ = nc.dram_tensor('ownk', [128, XK_COLS], BF16, kind='ExternalInput').ap()
                gk = [gk_[:, :, j * 1024:(j + 1) * 1024] for j in range(5)]
                ownk = [ownk_[:, j * 1024:(j + 1) * 1024] for j in range(5)]
                gs_d = nc.dram_tensor('gs', [4, 128, XS_COLS], F32, kind='ExternalInput').ap()
        RG = [[0, 1, 2, 3], [4, 5, 6, 7]]
        if fused:
            hsp = nc.dram_tensor('hspill', [T, D], F32).ap()
            xk_i = {l: [nc.dram_tensor('xk%d_%d' % (l, j), [128, 1024], BF16) for j in range(5)] for l in layers}
            gk_i = {l: [nc.dram_tensor('gk%d_%d' % (l, j), [4 * 128, 1024], BF16) for j in range(5)] for l in layers}
            xs_i = {l: nc.dram_tensor('xs%d' % l, [128, XS_COLS], F32) for l in layers}
            gs_i = {l: nc.dram_tensor('gs%d' % l, [4 * 128, XS_COLS], F32) for l in layers}
        hsrc = hin
        first = True
        for sname in stages:
            l = int(sname[1])
            if sname[0] == 'A':
                if first:
                    B.load_h(hin)
                B.ffn(l, 1)
                B.norm_h(B.dram['L%d_g_mix_norm' % l])
                if 'h_ffn1' in B.dbg:
                    B.dump('h_ffn1', B.h[:, :, :], B.r_h)
                if fused:
                    def issue_xk(l=l):
                        for a, b in zip(xk_i[l], gk_i[l]):
                            def ccf(e, a=a, b=b):
                                return e.collective_compute("AllGather", ALU.bypass, replica_groups=RG, ins=[a.ap()], outs=[b.ap()])
                            P.custom('pool', ccf, 1, reads=[B.r_xkd], writes=[B.r_gk], key='ccx')
                    B.cc_after_xk = issue_xk
                    B.store_h(hsp)
                    B.phase_A_exchange(l, [t.ap() for t in xk_i[l]], xs_i[l].ap())
                    hsrc = hsp

                    def ccs(e, a=xs_i[l], b=gs_i[l]):
                        return e.collective_compute("AllGather", ALU.bypass, replica_groups=RG, ins=[a.ap()], outs=[b.ap()])
                    P.custom('pool', ccs, 1, reads=[B.r_xsd], writes=[B.r_gsd], key='ccs')
                    gk = [t.ap().rearrange('(s p) c -> s p c', p=128) for t in gk_i[l]]
                    gs_d = gs_i[l].ap().rearrange('(s p) c -> s p c', p=128)
                    ownk = [t.ap() for t in xk_i[l]]
                else:
                    B.phase_A_exchange(l, [xk_out[:, j * 1024:(j + 1) * 1024] for j in range(5)], xs_out)
                    B.store_h(hout)
                    hsrc = hout
            else:
                if first:
                    B.load_h(hin)
                    B.norm_h(B.dram['L%d_g_mix_norm' % l])
                if B.phase_B_mixer(l, gk, ownk, gs_d, hsrc):
                    break
                if 'h_mix' in B.dbg:
                    B.dump('h_mix', B.h[:, :, :], B.r_h)
                if 'stop_mix' in B.dbg:
                    break
                B.xattn(l, mem)
                if 'h_x' in B.dbg:
                    B.dump('h_x', B.h[:, :, :], B.r_h)
                if 'stop_x' in B.dbg:
                    break
                B.ffn(l, 2)
                if l == DEPTH - 1:
                    gfin = nc.dram_tensor('gfinal', [1, D], F32, kind='ExternalInput').ap()
                    B.final_norm(gfin, hout)
                elif sname == stages[-1]:
                    B.store_h(hout)
            first = False
        P.barrier(full=True)
        P.emit()
    return nc


def _input_names(nc):
    names = []
    for alloc in nc.allocations:
        if isinstance(alloc, mybir.MemoryLocationSet) and alloc.kind == 'ExternalInput':
            names.append(alloc.memorylocations[0].name)
    return names


_PROGS = {}


def _run(stages, per_core, dbg=None, fused=False):
    key = (tuple(stages), tuple(sorted(dbg)) if dbg else None, fused)
    if key not in _PROGS:
        _PROGS[key] = build_program(stages, fused=fused, dbg=dbg)
    nc = _PROGS[key]
    names = _input_names(nc)
    in_maps = [{n: pc[n] for n in names if n in pc} for pc in per_core]
    res = run_bass_kernel_spmd(nc, in_maps, core_ids=list(range(NCORES)))
    return res.results


def prepare(inputs):
    inp = {k: np.asarray(v) for k, v in inputs.items()}
    x = np.ascontiguousarray(inp['x'], dtype=np.float32).reshape(NCORES, T, D)
    mem = np.ascontiguousarray(inp['mem'], dtype=np.float32)
    base = {'cbf': const_bf(), 'cf32': const_f32(), 'gfinal': np.ascontiguousarray(inp['final_norm'][None, :], dtype=np.float32)}
    for l in range(DEPTH):
        for k, v in layer_weights(inp, l).items():
            base['L%d_%s' % (l, k)] = v
    per_core = []
    for c in range(NCORES):
        ctl, rope = core_tables(c)
        d = dict(base)
        d.update({'ctl': ctl, 'rope': rope, 'mem': mem[c // 4], 'hin': x[c]})
        per_core.append(d)
    return per_core


def kernel(**inputs):
    per_core = prepare(inputs)

    def exchange(res):
        for c in range(NCORES):
            b0 = (c // 4) * 4
            per_core[c]['hin'] = res[c]['hout']
            per_core[c]['gk'] = np.ascontiguousarray(np.stack([res[b0 + i]['xk'] for i in range(4)]))
            per_core[c]['gs'] = np.ascontiguousarray(np.stack([res[b0 + i]['xs'] for i in range(4)]))
            per_core[c]['ownk'] = res[c]['xk']

    if FUSED:
        r = _run(['A0', 'B0', 'A1', 'B1'], per_core, fused=True)
    else:
        r = _run(['A0'], per_core)
        exchange(r)
        r = _run(['B0', 'A1'], per_core)
        exchange(r)
        r = _run(['B1'], per_core)
    y = np.stack([r[c]['hout'] for c in range(NCORES)]).reshape(2, 4 * T, D)
    return y.astype(np.float32)
```

```python
import numpy as np
import ml_dtypes
from contextlib import ExitStack
import concourse.bass as bass
import concourse.mybir as mybir
from concourse.bass_utils import run_bass_kernel_spmd

F32 = mybir.dt.float32
BF16 = mybir.dt.bfloat16
AF = mybir.ActivationFunctionType
ALU = mybir.AluOpType

NCORES = 8
T = 1024
D = 2048
FF = 5632
KC = 16
NTB = 8
NHC = 44
HG = 4
CPG = 11
EPS = 1e-6
DEPTH = 2
MEM = 256
SCALE_MLA = 192 ** -0.5
SCALE_X = 512 ** -0.5
XK_COLS = 5120
XS_COLS = 266
NEG = -30000.0
FUSED = True

ENGS = ['pe', 'act', 'dve', 'pool', 'sp']


class Res:
    __slots__ = ('w', 'r')

    def __init__(self):
        self.w = None
        self.r = []


class Prog:
    def __init__(self, nc, stack):
        self.nc = nc
        self.stack = stack
        self.streams = {e: [] for e in ENGS}
        self.cnt = {e: 0 for e in ENGS}
        self.sem = {e: stack.enter_context(nc.semaphore('s_' + e)) for e in ENGS}
        self.waited = {e: {} for e in ENGS}
        self.semobj = {}
        self.dma_sems = {}
        self.pool_pending = None

    def _waits(self, eng, reads, writes):
        need = {}
        for r in reads:
            if r.w is not None:
                s, v = r.w
                if need.get(s, 0) < v:
                    need[s] = v
        for w in writes:
            if w.w is not None:
                s, v = w.w
                if need.get(s, 0) < v:
                    need[s] = v
            for (s, v) in w.r:
                if need.get(s, 0) < v:
                    need[s] = v
        out = []
        me = id(self.sem[eng])
        for s, v in need.items():
            if s == me and eng == 'pe':
                continue
            if self.waited[eng].get(s, 0) < v:
                self.waited[eng][s] = v
                out.append((self.semobj[s], v))
        return out

    def _tok(self, semh, v):
        self.semobj[id(semh)] = semh
        return (id(semh), v)

    def _mark(self, tok, reads, writes):
        for r in reads:
            r.r.append(tok)
        for w in writes:
            w.w = tok
            w.r = []

    @staticmethod
    def _flat(lst):
        out = []
        for x in lst:
            if isinstance(x, (list, tuple)):
                out.extend(Prog._flat(x))
            else:
                out.append(x)
        return out

    def op(self, eng, fn, reads=(), writes=()):
        reads, writes = self._flat(reads), self._flat(writes)
        waits = self._waits(eng, reads, writes)
        if eng == 'pool' and self.pool_pending:
            for s, v in self.pool_pending:
                if s != id(self.sem['pool']) and self.waited['pool'].get(s, 0) < v:
                    self.waited['pool'][s] = v
                    waits.append((self.semobj[s], v))
            self.pool_pending = None
        self.cnt[eng] += 1
        semh = self.sem[eng]
        tok = self._tok(semh, self.cnt[eng])
        self.streams[eng].append((waits, fn, (semh, 1)))
        self._mark(tok, reads, writes)
        return tok

    def dma(self, q, out, in_, reads=(), writes=(), key='d'):
        reads, writes = self._flat(reads), self._flat(writes)
        waits = self._waits(q, reads, writes)
        if key not in self.dma_sems:
            self.dma_sems[key] = [self.stack.enter_context(self.nc.semaphore('d_' + key)), 0]
        ent = self.dma_sems[key]
        ent[1] += 16
        tok = self._tok(ent[0], ent[1])

        def fn(e, out=out, in_=in_):
            return e.dma_start(out=out, in_=in_)
        self.streams[q].append((waits, fn, (ent[0], 16)))
        self._mark(tok, reads, writes)
        return tok

    def custom(self, q, fn, inc, reads=(), writes=(), key='cc'):
        reads, writes = self._flat(reads), self._flat(writes)
        waits = self._waits(q, reads, writes)
        if key not in self.dma_sems:
            self.dma_sems[key] = [self.stack.enter_context(self.nc.semaphore('d_' + key)), 0]
        ent = self.dma_sems[key]
        ent[1] += inc
        tok = self._tok(ent[0], ent[1])
        self.streams[q].append((waits, fn, (ent[0], inc)))
        self._mark(tok, reads, writes)
        return tok

    def barrier(self, full=False):
        toks = [(id(self.sem[e]), self.cnt[e]) for e in ENGS if self.cnt[e] > 0]
        for k, ent in self.dma_sems.items():
            toks.append((id(ent[0]), ent[1]))
        for e in ENGS:
            if e == 'pool' and not full:
                self.pool_pending = toks
                continue
            out = []
            for s, v in toks:
                if s == id(self.sem[e]) and e == 'pe':
                    continue
                if self.waited[e].get(s, 0) < v:
                    self.waited[e][s] = v
                    out.append((self.semobj[s], v))
            if out:
                self.streams[e].append((out, None, None))

    def emit(self):
        nc = self.nc
        streams = self.streams

        def run(e, lst):
            for waits, fn, inc in lst:
                for s, v in waits:
                    e.wait_ge(s, v)
                if fn is not None:
                    ins = fn(e)
                    if inc is not None:
                        ins.then_inc(inc[0], inc[1])

        with nc.Block() as block:
            @block.tensor
            def _(e):
                run(e, streams['pe'])

            @block.scalar
            def _(e):
                run(e, streams['act'])

            @block.vector
            def _(e):
                run(e, streams['dve'])

            @block.gpsimd
            def _(e):
                run(e, streams['pool'])

            @block.sync
            def _(e):
                run(e, streams['sp'])


class Region:
    def __init__(self, arena, lo, hi):
        self.arena, self.lo, self.hi, self.off = arena, lo, hi, lo

    def reset(self):
        self.off = self.lo

    def alloc(self, nbytes, dtype=BF16):
        nbytes = (nbytes + 63) // 64 * 64
        assert self.off + nbytes <= self.hi, ('region overflow', self.off, nbytes, self.hi)
        a = self.arena[:, self.off // 2:(self.off + nbytes) // 2]
        self.off += nbytes
        if dtype != BF16:
            a = a.bitcast(dtype)
        return a


def v3(ap, b):
    return ap.rearrange('p (a b) -> p a b', b=b)


def tile_w(W, ncols):
    K, N = W.shape
    kc = K // 128
    t = W.reshape(kc, 128, N // ncols, ncols).transpose(2, 1, 0, 3)
    return np.ascontiguousarray(t).reshape(N // ncols, 128, kc * ncols)


def prep_ffn_w1(w_in):
    g = w_in[:, :FF].reshape(KC, 128, NHC, 128)
    u = w_in[:, FF:].reshape(KC, 128, NHC, 128)
    t = np.stack([g, u], axis=3).transpose(2, 1, 0, 3, 4)
    return np.ascontiguousarray(t).reshape(NHC, 128, KC * 256)


def prep_ffn_w2(w_out):
    t = w_out.reshape(HG, CPG, 128, 4, 512).transpose(0, 3, 2, 1, 4)
    return np.ascontiguousarray(t).reshape(HG * 4, 128, CPG * 512)


def layer_weights(inp, l):
    W = {}
    win = inp['mix_w_in'][l]
    W['f1w1'] = prep_ffn_w1(inp['ffn1_w_in'][l])
    W['f1w2'] = prep_ffn_w2(inp['ffn1_w_out'][l])
    W['f2w1'] = prep_ffn_w1(inp['ffn2_w_in'][l])
    W['f2w2'] = prep_ffn_w2(inp['ffn2_w_out'][l])
    W['wq'] = tile_w(win[:, 0:512], 256)
    W['wkv'] = tile_w(win[:, 512:1024], 256)
    kr = win[:, 1024:1088]
    W['wkr'] = tile_w(np.concatenate([kr, kr[:, 32:], kr[:, :32]], axis=1), 128)
    W['wgla'] = tile_w(win[:, 1088:2624], 256)
    W['walow'] = tile_w(win[:, 2624:2640], 16)
    W['wconv'] = tile_w(win[:, 2640:4176], 256)
    gates = win[:, 4176:]
    mt = np.zeros((16, 2, 128, 32, 128), np.float32)
    gm = gates[:, 0:2048].reshape(16, 128, 16, 128)
    gg = gates[:, 2048:4096].reshape(16, 128, 16, 128)
    gc = gates[:, 4096:6144].reshape(16, 128, 16, 128)
    pm = inp['mla_w_proj'][l].reshape(8, 128, 16, 128)
    pg = inp['gla_w_proj'][l].reshape(4, 128, 16, 128)
    pc = inp['conv_w_proj'][l].reshape(4, 128, 16, 128)
    mt[:, 0, :, 0:16] = gm.transpose(2, 1, 0, 3)
    mt[:, 0, :, 16:32] = gg.transpose(2, 1, 0, 3)
    mt[:, 1, :, 0:16] = gc.transpose(2, 1, 0, 3)
    mt[:, 1, :, 16:24] = pm.transpose(2, 1, 0, 3)
    mt[:, 1, :, 24:28] = pg.transpose(2, 1, 0, 3)
    mt[:, 1, :, 28:32] = pc.transpose(2, 1, 0, 3)
    W['wmerge'] = mt.reshape(32, 128, 32 * 128)
    W['wout'] = tile_w(inp['mix_w_out'][l], 256)
    W['xq'] = tile_w(inp['xattn_w_q'][l], 256)
    W['xk'] = tile_w(inp['xattn_w_kv'][l][:, :D], 256)
    W['xv'] = tile_w(inp['xattn_w_kv'][l][:, D:], 256)
    W['xo'] = tile_w(inp['xattn_w_o'][l], 256)
    uq = inp['mla_w_uq'][l].reshape(512, 8, 192)
    ukv = inp['mla_w_ukv'][l].reshape(512, 8, 256)
    hh = np.concatenate([uq[:, :, 0:128], uq[:, :, 128:192], uq[:, :, 160:192], uq[:, :, 128:160], ukv], axis=2)
    hh = hh.reshape(4, 128, 8, 512).transpose(2, 1, 0, 3)
    W['wmla'] = np.ascontiguousarray(hh).reshape(8, 128, 4 * 512)
    W['wa2'] = np.ascontiguousarray(inp['gla_w_a2'][l])
    W['ba'] = np.ascontiguousarray(inp['gla_b_a'][l][None, :])
    for nm in ('ffn1_norm', 'mix_norm', 'xattn_norm', 'mem_norm', 'ffn2_norm', 'mla_q_norm', 'mla_kv_norm', 'gla_norm'):
        W['g_' + nm] = np.ascontiguousarray(inp[nm][l][None, :])
    W['convw'] = np.ascontiguousarray(inp['conv_w'][l][:, 0, :].reshape(3, 4, 128).transpose(2, 1, 0)).reshape(128, 12)
    W['bgate'] = np.ascontiguousarray(inp['mix_b_gate'][l].reshape(3, 16, 128).transpose(2, 0, 1)).reshape(128, 48)
    return {k: np.ascontiguousarray(v, dtype=np.float32) for k, v in W.items()}


def const_bf():
    c = np.zeros((128, 384), np.float32)
    c[:, 0:128] = np.eye(128)
    c[:, 128:256] = 1.0
    c[:, 256:384] = np.triu(np.ones((128, 128)))
    return c.astype(ml_dtypes.bfloat16)


def const_f32():
    c = np.zeros((128, 384), np.float32)
    tri = np.triu(np.ones((128, 128), np.float32))
    c[:, 0:128] = -tri / 16.0
    c[:, 128:256] = -(1.0 - tri) / 16.0
    c[:, 256:384] = -1.0 / 16.0
    return c


def core_tables(c):
    j = c % 4
    ctl = np.zeros((128, 16), np.float32)
    for i in range(4):
        ctl[:, i] = 0.0 if i < j else NEG
        ctl[:, 4 + i] = 1.0 if i < j else 0.0
        ctl[:, 8 + i] = 0.0 if i < j else 1.0
        ctl[:, 12 + i] = 1.0 if i == j - 1 else 0.0
    inv_freq = (1.0 / (np.float32(10000.0) ** (np.arange(0, 64, 2, dtype=np.float32) / np.float32(64)))).astype(np.float32)
    pos = (np.arange(T, dtype=np.float32) + np.float32(j * T))
    ang = (pos[:, None] * inv_freq[None, :]).astype(np.float32)
    cos, sin = np.cos(ang).astype(np.float32).T, np.sin(ang).astype(np.float32).T
    rope = np.zeros((64, 2 * T), np.float32)
    rope[0:32, 0:T] = cos
    rope[32:64, 0:T] = cos
    rope[0:32, T:] = -sin
    rope[32:64, T:] = sin
    return ctl, rope


LAYER_KEYS = ['f1w1', 'f1w2', 'f2w1', 'f2w2', 'wq', 'wkv', 'wkr', 'wgla', 'walow', 'wconv', 'wmerge', 'wout', 'xq', 'xk', 'xv',
              'xo', 'wmla', 'wa2', 'ba', 'g_ffn1_norm', 'g_mix_norm', 'g_xattn_norm', 'g_mem_norm', 'g_ffn2_norm',
              'g_mla_q_norm', 'g_mla_kv_norm', 'g_gla_norm', 'convw', 'bgate']


class LazyDram(dict):
    def __init__(self, nc, shapes):
        super().__init__()
        self.nc, self.shapes = nc, shapes

    def __missing__(self, name):
        k = name.split('_', 1)[1]
        ap = self.nc.dram_tensor(name, list(self.shapes[k]), F32, kind='ExternalInput').ap()
        self[name] = ap
        return ap


SHAPES = {'f1w1': (44, 128, 4096), 'f2w1': (44, 128, 4096), 'f1w2': (16, 128, 5632), 'f2w2': (16, 128, 5632),
          'wq': (2, 128, 4096), 'wkv': (2, 128, 4096), 'wkr': (1, 128, 2048), 'wgla': (6, 128, 4096), 'walow': (1, 128, 256),
          'wconv': (6, 128, 4096), 'wmerge': (32, 128, 4096), 'wout': (8, 128, 4096), 'xq': (8, 128, 4096), 'xk': (8, 128, 4096),
          'xv': (8, 128, 4096), 'xo': (8, 128, 4096), 'wmla': (8, 128, 2048), 'wa2': (16, 256), 'ba': (1, 256),
          'g_ffn1_norm': (1, D), 'g_mix_norm': (1, D), 'g_xattn_norm': (1, D), 'g_mem_norm': (1, D), 'g_ffn2_norm': (1, D),
          'g_mla_q_norm': (1, 512), 'g_mla_kv_norm': (1, 512), 'g_gla_norm': (1, 512), 'convw': (128, 12), 'bgate': (128, 48)}


class Builder:
    def __init__(self, nc, st, layers, shapes, dbg=None):
        self.nc, self.st = nc, st
        self.P = Prog(nc, st)
        self.dbg = dbg or {}
        self.dram = LazyDram(nc, shapes)
        self.d_cbf = nc.dram_tensor('cbf', [128, 384], BF16, kind='ExternalInput').ap()
        self.d_cf32 = nc.dram_tensor('cf32', [128, 384], F32, kind='ExternalInput').ap()
        self.d_ctl = nc.dram_tensor('ctl', [128, 16], F32, kind='ExternalInput').ap()
        self.d_rope = nc.dram_tensor('rope', [64, 2 * T], F32, kind='ExternalInput').ap()
        TOTAL = 207 * 1024
        self.arena = st.enter_context(nc.sbuf_tensor('arena', [128, TOTAL // 2], BF16))
        P = self.P
        self.RH = Region(self.arena, 0, 65536)
        self.RU = Region(self.arena, 65536, 98304)
        self.RW = Region(self.arena, 98304, 98304 + 36864)
        self.RC = Region(self.arena, 135168, 135168 + 13312)
        self.RX = Region(self.arena, 148480, TOTAL)
        self.h = v3(self.RH.alloc(65536, F32), D)
        self.uT = v3(self.RU.alloc(32768), T)
        self.wsl = [self.RW.alloc(12288) for _ in range(3)]
        self.r_w = [Res() for _ in range(3)]
        self.wi = 0
        self.gb = self.RC.alloc(8192, F32)
        self.cbf = self.RC.alloc(768)
        self.cf32 = self.RC.alloc(1536, F32)
        self.ctl = self.RC.alloc(64, F32)
        self.small = self.RC.alloc(1024, F32)
        self.convw = self.RC.alloc(64, F32)
        self.bgate = self.RC.alloc(192, F32)
        self.wa2 = self.RC.alloc(512)
        self.ba = self.RC.alloc(512)
        self.ident = self.cbf[:, 0:128]
        self.ones = self.cbf[:, 128:256]
        self.tri = self.cbf[:, 256:384]
        self.r_gb = Res()
        self.r_hsp = Res()
        self.r_gk = Res()
        self.r_gsd = Res()
        self.r_xkd = Res()
        self.r_xsd = Res()
        self.cc_after_xk = None
        self.r_c = Res()
        self.r_lc = Res()
        self.r_small = [Res() for _ in range(256)]
        self.si = 0
        self.r_h = [Res() for _ in range(NTB)]
        self.r_uT = [[Res() for _ in range(4)] for _ in range(NTB)]
        self.ps = [st.enter_context(nc.psum_tensor('ps%d' % i, [128, 512], F32)) for i in range(8)]
        self.r_ps = [Res() for _ in range(8)]
        self.ps_set = list(range(8))
        self.psi = 0
        P.dma('sp', self.cbf[:, :], self.d_cbf, writes=[self.r_c], key='c')
        P.dma('sp', self.cf32[:, :], self.d_cf32, writes=[self.r_c], key='c')
        P.dma('sp', self.ctl[:, :], self.d_ctl, writes=[self.r_c], key='c')
        P.barrier()
        self.ndump = 0

    def getps(self):
        i = self.ps_set[self.psi % len(self.ps_set)]
        self.psi += 1
        return i

    def scal(self):
        i = self.si % 256
        self.si += 1
        return self.small[:, i:i + 1], self.r_small[i]

    def wload(self, src):
        i = self.wi % 3
        self.wi += 1
        n = src.shape[1]
        assert n * 2 <= 12288
        self.P.dma('pool', self.wsl[i][:, 0:n], src, writes=[self.r_w[i]], key='w%d' % i)
        return self.wsl[i], self.r_w[i]

    def dump(self, name, ap, reads, dtype=F32):
        d = self.nc.dram_tensor('dbg_' + name, list(ap.shape), dtype, kind='ExternalOutput').ap()
        self.P.dma('sp', d, ap, reads=reads, key='dbg')

    def load_gain(self, src, n=D):
        self.P.dma('sp', self.gb[:, 0:n], src[0, :].partition_broadcast(128), writes=[self.r_gb], key='gb')

    def scal_block(self, n):
        if self.si % 256 + n > 256:
            self.si += 256 - self.si % 256
        i = self.si % 256
        self.si += n
        return self.small[:, i:i + n], [self.r_small[i + k] for k in range(n)]

    def rstd_batch(self, ssq, r_ssq, n, F):
        P = self.P
        P.op('dve', lambda e: e.tensor_scalar(out=ssq, in0=ssq, scalar1=1.0 / F, scalar2=EPS, op0=ALU.mult, op1=ALU.add), reads=[], writes=r_ssq)
        P.op('act', lambda e: e.activation(out=ssq, in_=ssq, func=AF.Sqrt), reads=[], writes=r_ssq)
        P.op('dve', lambda e: e.reciprocal(out=ssq, in_=ssq), reads=[], writes=r_ssq)

    def norm_T(self, src_fn, src_res_fn, F, dstT, dst_res_fn, ntb, ub, r_ub, junk):
        P = self.P
        nk = F // 128
        ssq, r_ssq = self.scal_block(ntb)
        for tb in range(ntb):
            P.op('act', lambda e, tb=tb: e.activation(out=junk[:, 0:F], in_=src_fn(tb), func=AF.Square, accum_out=ssq[:, tb:tb + 1]),
                 reads=[src_res_fn(tb)], writes=[r_ssq[tb]])
        self.rstd_batch(ssq, r_ssq, ntb, F)
        nb = len(ub)
        for tb in range(ntb):
            b = tb % nb
            P.op('dve', lambda e, tb=tb, b=b: e.scalar_tensor_tensor(out=ub[b][:, 0:F], in0=src_fn(tb), scalar=ssq[:, tb:tb + 1], in1=self.gb[:, 0:F], op0=ALU.mult, op1=ALU.mult),
                 reads=[src_res_fn(tb), r_ssq[tb], self.r_gb], writes=[r_ub[b]])
            for k4 in range(nk // 4):
                pi = self.getps()
                pt = self.ps[pi][:, :].bitcast(BF16)

                def tr(e, b=b, k4=k4, pt=pt):
                    ins = None
                    for j in range(4):
                        kc = k4 * 4 + j
                        ins = e.transpose(out=pt[:, j * 128:(j + 1) * 128], in_=ub[b][:, kc * 128:(kc + 1) * 128], identity=self.ident)
                    return ins
                P.op('pe', tr, reads=[r_ub[b]], writes=[self.r_ps[pi]])
                eng = 'act' if k4 % 2 == 0 else 'dve'

                def cp(e, tb=tb, k4=k4, pt=pt, eng=eng):
                    s = pt[:, 0:512].rearrange('p (a b) -> p a b', b=128)
                    d = dstT[:, k4 * 4:(k4 + 1) * 4, tb * 128:(tb + 1) * 128]
                    return e.copy(out=d, in_=s) if eng == 'act' else e.tensor_copy(out=d, in_=s)
                dr = dst_res_fn(tb)
                P.op(eng, cp, reads=[self.r_ps[pi]], writes=[dr[k4] if isinstance(dr, list) else dr])

    def norm_h(self, gain_ap, dstT=None, dst_res=None):
        X = self.RX
        save = X.off
        ub = [X.alloc(4096), X.alloc(4096), X.alloc(4096)]
        r_ub = [Res(), Res(), Res()]
        junk = X.alloc(4096)
        self.load_gain(gain_ap)
        dstT = self.uT if dstT is None else dstT
        dst_res = self.r_uT if dst_res is None else dst_res
        self.norm_T(lambda tb: self.h[:, tb, :], lambda tb: self.r_h[tb], D, dstT, lambda tb: dst_res[tb], NTB, ub, r_ub, junk)
        self.P.barrier()
        X.off = save

    def lin_fm(self, xT, r_x, nkc, wtiles, ncols, M_list, epi, tgs=(0, 1), ncol_T=512):
        P = self.P
        for ti in range(wtiles.shape[0]):
            wt, r_wt = self.wload(wtiles[ti])
            wv = v3(wt[:, 0:nkc * ncols], ncols)
            for mi, (off, M) in enumerate(M_list):
                for tg in tgs:
                    pi = self.getps()

                    def mm(e, wv=wv, off=off, M=M, tg=tg, pi=pi):
                        ins = None
                        for kc in range(nkc):
                            ins = e.matmul(self.ps[pi][0:M, 0:ncol_T], lhsT=wv[:, kc, off:off + M], rhs=xT[:, kc, tg * ncol_T:(tg + 1) * ncol_T],
                                           start=(kc == 0), stop=(kc == nkc - 1))
                        return ins
                    P.op('pe', mm, reads=[r_wt] + list(r_x(tg)), writes=[self.r_ps[pi]])
                    epi(ti, mi, tg, pi)

    def lin_tok(self, xT, r_x, nkc, wtiles, ncols, epi, ntb=NTB):
        P = self.P
        for ti in range(wtiles.shape[0]):
            wt, r_wt = self.wload(wtiles[ti])
            wv = v3(wt[:, 0:nkc * ncols], ncols)
            for tb in range(ntb):
                pi = self.getps()

                def mm(e, wv=wv, tb=tb, pi=pi):
                    ins = None
                    for kc in range(nkc):
                        ins = e.matmul(self.ps[pi][:, 0:ncols], lhsT=xT[:, kc, tb * 128:(tb + 1) * 128], rhs=wv[:, kc, :],
                                       start=(kc == 0), stop=(kc == nkc - 1))
                    return ins
                P.op('pe', mm, reads=[r_wt] + list(r_x(tb)), writes=[self.r_ps[pi]])
                epi(ti, tb, pi)

    def ffn(self, l, which):
        P = self.P
        X = self.RX
        self.norm_h(self.dram['L%d_g_ffn%d_norm' % (l, which)])
        X.reset()
        actT = [v3(X.alloc(CPG * T * 2), T) for _ in range(2)]
        sg = [X.alloc(2048, F32) for _ in range(2)]
        r_act = [[Res() for _ in range(CPG)] for _ in range(2)]
        r_sg = [Res(), Res()]
        w1 = self.dram['L%d_f%dw1' % (l, which)]
        w2 = self.dram['L%d_f%dw2' % (l, which)]
        uT = self.uT
        for g in range(HG):
            ab = g % 2
            for cl in range(CPG):
                c = g * CPG + cl
                wt, r_wt = self.wload(w1[c])
                wv = v3(wt[:, 0:KC * 256], 256)
                for tg in range(2):
                    pg, pu = self.getps(), self.getps()

                    def mm(e, wv=wv, tg=tg, pg=pg, pu=pu):
                        ins = None
                        for half, pp in ((0, pg), (1, pu)):
                            for kc in range(KC):
                                ins = e.matmul(self.ps[pp][:, :], lhsT=wv[:, kc, half * 128:(half + 1) * 128],
                                               rhs=uT[:, kc, tg * 512:(tg + 1) * 512], start=(kc == 0), stop=(kc == KC - 1))
                        return ins
                    P.op('pe', mm, reads=[r_wt] + self.r_uT[tg * 4:(tg + 1) * 4], writes=[self.r_ps[pg], self.r_ps[pu]])
                    sb = (cl * 2 + tg) % 2
                    P.op('act', lambda e, pg=pg, sb=sb: e.activation(out=sg[sb][:, :], in_=self.ps[pg][:, :], func=AF.Silu),
                         reads=[self.r_ps[pg]], writes=[r_sg[sb]])
                    P.op('dve', lambda e, pu=pu, sb=sb, ab=ab, cl=cl, tg=tg: e.tensor_tensor(out=actT[ab][:, cl, tg * 512:(tg + 1) * 512], in0=sg[sb][:, :], in1=self.ps[pu][:, :], op=ALU.mult),
                         reads=[self.r_ps[pu], r_sg[sb]], writes=[r_act[ab][cl]])
            for ng in range(4):
                wt, r_wt = self.wload(w2[g * 4 + ng])
                wv = v3(wt[:, 0:CPG * 512], 512)
                for tb in range(NTB):
                    po = self.getps()

                    def mm2(e, wv=wv, tb=tb, po=po, ab=ab):
                        ins = None
                        for cl in range(CPG):
                            ins = e.matmul(self.ps[po][:, :], lhsT=actT[ab][:, cl, tb * 128:(tb + 1) * 128], rhs=wv[:, cl, :],
                                           start=(cl == 0), stop=(cl == CPG - 1))
                        return ins
                    P.op('pe', mm2, reads=[r_wt] + r_act[ab], writes=[self.r_ps[po]])
                    P.op('dve', lambda e, po=po, tb=tb, ng=ng: e.scalar_tensor_tensor(out=self.h[:, tb, ng * 512:(ng + 1) * 512], in0=self.ps[po][:, :], scalar=0.5,
                                                                                    in1=self.h[:, tb, ng * 512:(ng + 1) * 512], op0=ALU.mult, op1=ALU.add),
                         reads=[self.r_ps[po], self.r_h[tb]], writes=[self.r_h[tb]])
        P.barrier()
        X.reset()

    def latent(self, l, wkey, gkey, dstT, r_dst, X):
        P = self.P
        save = X.off
        ub = [X.alloc(1024), X.alloc(1024)]
        r_ub = [Res(), Res()]
        lat = [X.alloc(2048, F32) for _ in range(8)]
        r_lat = [Res() for _ in range(8)]
        junk = X.alloc(1024)
        self.load_gain(self.dram['L%d_%s' % (l, gkey)], 512)
        wt = self.dram['L%d_%s' % (l, wkey)]
        w0, r_w0 = self.wload(wt[0])
        w1, r_w1 = self.wload(wt[1])
        wv = [v3(w0[:, 0:KC * 256], 256), v3(w1[:, 0:KC * 256], 256)]
        for half4 in range(1):
            ssq, r_ssq = self.scal_block(8)
            for i in range(8):
                tb = i
                pi = self.getps()

                def mm(e, tb=tb, pi=pi):
                    ins = None
                    for half in range(2):
                        for kc in range(KC):
                            ins = e.matmul(self.ps[pi][:, half * 256:(half + 1) * 256], lhsT=self.uT[:, kc, tb * 128:(tb + 1) * 128], rhs=wv[half][:, kc, :],
                                           start=(kc == 0), stop=(kc == KC - 1))
                    return ins
                P.op('pe', mm, reads=[r_w0, r_w1, self.r_uT[tb]], writes=[self.r_ps[pi]])
                P.op('act', lambda e, i=i, pi=pi, ssq=ssq: e.activation(out=junk[:, 0:512], in_=self.ps[pi][:, :], func=AF.Square, accum_out=ssq[:, i:i + 1]),
                     reads=[], writes=[self.r_ps[pi], r_ssq[i]])
                P.op('dve', lambda e, i=i, pi=pi: e.tensor_copy(out=lat[i][:, :], in_=self.ps[pi][:, :]), reads=[], writes=[self.r_ps[pi], r_lat[i]])
            self.rstd_batch(ssq, r_ssq, 8, 512)
            for i in range(8):
                tb = i
                b = i % 2
                P.op('dve', lambda e, i=i, b=b, ssq=ssq: e.scalar_tensor_tensor(out=ub[b][:, 0:512], in0=lat[i][:, :], scalar=ssq[:, i:i + 1], in1=self.gb[:, 0:512], op0=ALU.mult, op1=ALU.mult),
                     reads=[r_lat[i], r_ssq[i], self.r_gb], writes=[r_ub[b]])
                p2 = self.getps()
                pt = self.ps[p2][:, :].bitcast(BF16)

                def tr(e, b=b, pt=pt):
                    ins = None
                    for j in range(4):
                        ins = e.transpose(out=pt[:, j * 128:(j + 1) * 128], in_=ub[b][:, j * 128:(j + 1) * 128], identity=self.ident)
                    return ins
                P.op('pe', tr, reads=[r_ub[b]], writes=[self.r_ps[p2]])
                P.op('act', lambda e, tb=tb, pt=pt: e.copy(out=dstT[:, 0:4, tb * 128:(tb + 1) * 128], in_=pt[:, 0:512].rearrange('p (a b) -> p a b', b=128)),
                     reads=[self.r_ps[p2]], writes=[r_dst])
        P.barrier()
        X.off = save

    def gla(self, l, mode, X, xs_out=None, gs=None, r_gs=None, oglaT=None, r_ogla=None):
        P = self.P
        uT = self.uT
        M1, M2, M3 = self.cf32[:, 0:128], self.cf32[:, 128:256], self.cf32[:, 256:384]
        alT = X.alloc(2048)
        r_al = Res()
        P.dma('pool', self.wa2[0:16, 0:256], self.dram['L%d_wa2' % l], writes=[self.r_lc], key='lc')
        P.dma('pool', self.ba[0:1, 0:256], self.dram['L%d_ba' % l], writes=[self.r_lc], key='lc')

        def epi_al(ti, mi, tg, pi):
            P.op('act', lambda e: e.copy(out=alT[0:16, tg * 512:(tg + 1) * 512], in_=self.ps[pi][0:16, :]), reads=[self.r_ps[pi]], writes=[r_al])
        self.lin_fm(uT, lambda tg: self.r_uT[tg * 4:(tg + 1) * 4], KC, self.dram['L%d_walow' % l], 16, [(0, 16)], epi_al)
        qk = v3(X.alloc(NTB * 512 * 4, F32), 512)
        V = v3(X.alloc(NTB * 512 * 2), 512)
        r_qk = [Res() for _ in range(NTB)]
        r_V = [Res() for _ in range(NTB)]
        if mode == 'B':
            sr = v3(X.alloc(NTB * 512 * 2), 512)
            r_sr = [Res() for _ in range(NTB)]

        def epi_g(ti, tb, pi):
            if ti < 2:
                P.op('act', lambda e: e.copy(out=qk[:, tb, ti * 256:(ti + 1) * 256], in_=self.ps[pi][:, 0:256]), reads=[self.r_ps[pi]], writes=[r_qk[tb]])
            elif ti < 4:
                P.op('dve', lambda e: e.tensor_copy(out=V[:, tb, (ti - 2) * 256:(ti - 1) * 256], in_=self.ps[pi][:, 0:256]), reads=[self.r_ps[pi]], writes=[r_V[tb]])
            elif mode == 'B':
                P.op('act', lambda e: e.activation(out=sr[:, tb, (ti - 4) * 256:(ti - 3) * 256], in_=self.ps[pi][:, 0:256], func=AF.Silu), reads=[self.r_ps[pi]], writes=[r_sr[tb]])
        wg = self.dram['L%d_wgla' % l]
        self.lin_tok(uT, lambda tb: [self.r_uT[tb]], KC, wg if mode == 'B' else wg[0:4], 256, epi_g)
        S = [X.alloc(512, F32) for _ in range(2)]
        Sb = [X.alloc(256) for _ in range(2)]
        r_S = [Res(), Res()]
        r_Sb = [Res(), Res()]
        Dt = X.alloc(64, F32)
        r_D = Res()
        lsp = X.alloc(1024, F32)
        r_lsp = Res()
        ex = [X.alloc(1024, F32) for _ in range(3)]
        r_ex = [Res() for _ in range(3)]
        kd = X.alloc(512)
        r_kd = Res()
        if mode == 'B':
            qt = X.alloc(512)
            kt = X.alloc(512)
            r_qt, r_kt = Res(), Res()
            qT = [X.alloc(256) for _ in range(2)]
            kT = [X.alloc(256) for _ in range(2)]
            r_qT, r_kT = [Res(), Res()], [Res(), Res()]
            AT = [X.alloc(512) for _ in range(2)]
            r_AT = [Res(), Res()]
            og = X.alloc(1024)
            r_og = Res()
            osb = X.alloc(2048, F32)
            gjunk = X.alloc(256)
            r_osb = Res()
            self.load_gain(self.dram['L%d_g_gla_norm' % l], 512)
        if mode == 'A':
            for hf in range(2):
                P.op('pool', lambda e, hf=hf: e.memset(S[hf][:, :], 0.0), writes=[r_S[hf]])
            P.op('pool', lambda e: e.memset(Dt[:, :], 1.0), writes=[r_D])
        else:
            tmpL = X.alloc(512, F32)
            r_tmpL = Res()
            coef, r_coef = self.scal()
            for hf in range(2):
                P.op('pool', lambda e, hf=hf: e.memset(S[hf][:, :], 0.0), writes=[r_S[hf]])
                for i in range(3):
                    P.op('dve', lambda e, hf=hf, i=i: e.tensor_scalar(out=coef, in0=gs[:, i, 256 + hf:257 + hf], scalar1=self.ctl[:, 4 + i:5 + i], scalar2=self.ctl[:, 8 + i:9 + i], op0=ALU.mult, op1=ALU.add),
                         reads=[r_gs], writes=[r_coef])
                    P.op('dve', lambda e, hf=hf, i=i: e.tensor_scalar(out=tmpL[:, :], in0=gs[:, i, hf * 128:(hf + 1) * 128], scalar1=self.ctl[:, 4 + i:5 + i], scalar2=None, op0=ALU.mult),
                         reads=[r_gs], writes=[r_tmpL])
                    P.op('dve', lambda e, hf=hf: e.scalar_tensor_tensor(out=S[hf][:, :], in0=S[hf][:, :], scalar=coef, in1=tmpL[:, :], op0=ALU.mult, op1=ALU.add),
                         reads=[r_coef, r_tmpL, r_S[hf]], writes=[r_S[hf]])
        if mode == 'A' and self.cc_after_xk is not None:
            self.cc_after_xk()
        for n in range(NTB):
            px = self.getps()

            def mmx(e, n=n, px=px):
                e.matmul(self.ps[px][:, 0:256], lhsT=alT[0:16, n * 128:(n + 1) * 128], rhs=self.wa2[0:16, 0:256], start=True, stop=False)
                return e.matmul(self.ps[px][:, 0:256], lhsT=self.ones[0:1, 0:128], rhs=self.ba[0:1, 0:256], start=False, stop=True)
            P.op('pe', mmx, reads=[r_al, self.r_lc], writes=[self.r_ps[px]])
            P.op('act', lambda e, px=px: e.activation(out=lsp[:, :], in_=self.ps[px][:, 0:256], func=AF.Exp, scale=-1.0), reads=[self.r_ps[px]], writes=[r_lsp])
            P.op('act', lambda e: e.activation(out=lsp[:, :], in_=lsp[:, :], func=AF.Ln, bias=1.0), reads=[r_lsp], writes=[r_lsp])
            pb = self.getps()

            def mmb(e, pb=pb):
                e.matmul(self.ps[pb][:, 0:256], lhsT=M1, rhs=lsp[:, :], start=True, stop=True)
                return e.matmul(self.ps[pb][:, 256:512], lhsT=M2, rhs=lsp[:, :], start=True, stop=True)
            P.op('pe', mmb, reads=[r_lsp], writes=[self.r_ps[pb]])
            pc = self.getps()

            def mmc(e, pc=pc):
                e.matmul(self.ps[pc][:, 0:2], lhsT=lsp[:, 0:128], rhs=M3[:, 0:2], start=True, stop=True)
                return e.matmul(self.ps[pc][:, 2:4], lhsT=lsp[:, 128:256], rhs=M3[:, 0:2], start=True, stop=True)
            P.op('pe', mmc, reads=[r_lsp], writes=[self.r_ps[pc]])
            cd, r_cd = self.scal()
            cd2, r_cd2 = self.scal()
            P.op('act', lambda e, pc=pc, cd=cd: e.activation(out=cd, in_=self.ps[pc][:, 0:1], func=AF.Exp), reads=[self.r_ps[pc]], writes=[r_cd])
            P.op('act', lambda e, pc=pc, cd2=cd2: e.activation(out=cd2, in_=self.ps[pc][:, 2:3], func=AF.Exp), reads=[self.r_ps[pc]], writes=[r_cd2])
            cds = [(cd, r_cd), (cd2, r_cd2)]
            P.op('act', lambda e, pb=pb: e.activation(out=ex[2][:, :], in_=self.ps[pb][:, 256:512], func=AF.Exp), reads=[self.r_ps[pb]], writes=[r_ex[2]])
            P.op('dve', lambda e, n=n: e.tensor_tensor(out=kd[:, :], in0=qk[:, n, 256:512], in1=ex[2][:, :], op=ALU.mult), reads=[r_qk[n], r_ex[2]], writes=[r_kd])
            if mode == 'B':
                P.op('act', lambda e, pb=pb: e.activation(out=ex[0][:, :], in_=self.ps[pb][:, 0:256], func=AF.Exp), reads=[self.r_ps[pb]], writes=[r_ex[0]])
                P.op('act', lambda e, pb=pb: e.activation(out=ex[1][:, :], in_=self.ps[pb][:, 0:256], func=AF.Exp, scale=-1.0), reads=[self.r_ps[pb]], writes=[r_ex[1]])
                P.op('dve', lambda e, n=n: e.scalar_tensor_tensor(out=qt[:, :], in0=qk[:, n, 0:256], scalar=0.125, in1=ex[0][:, :], op0=ALU.mult, op1=ALU.mult),
                     reads=[r_qk[n], r_ex[0]], writes=[r_qt])
                P.op('dve', lambda e, n=n: e.tensor_tensor(out=kt[:, :], in0=qk[:, n, 256:512], in1=ex[1][:, :], op=ALU.mult), reads=[r_qk[n], r_ex[1]], writes=[r_kt])
                ptq = self.getps()
                ptv = self.ps[ptq][:, :].bitcast(BF16)

                def trq(e, ptv=ptv):
                    e.transpose(out=ptv[:, 0:128], in_=qt[:, 0:128], identity=self.ident)
                    e.transpose(out=ptv[:, 128:256], in_=qt[:, 128:256], identity=self.ident)
                    e.transpose(out=ptv[:, 256:384], in_=kt[:, 0:128], identity=self.ident)
                    return e.transpose(out=ptv[:, 384:512], in_=kt[:, 128:256], identity=self.ident)
                P.op('pe', trq, reads=[r_qt, r_kt], writes=[self.r_ps[ptq]])
                for hf in range(2):
                    P.op('act', lambda e, hf=hf, ptv=ptv: e.copy(out=qT[hf][:, :], in_=ptv[:, hf * 128:(hf + 1) * 128]), reads=[self.r_ps[ptq]], writes=[r_qT[hf]])
                    P.op('act', lambda e, hf=hf, ptv=ptv: e.copy(out=kT[hf][:, :], in_=ptv[:, 256 + hf * 128:256 + (hf + 1) * 128]), reads=[self.r_ps[ptq]], writes=[r_kT[hf]])
                pa = [self.getps(), self.getps()]
                for e_ in range(2):
                    def mma(e, e_=e_, pa=pa):
                        ins = None
                        for hf in range(2):
                            ins = e.matmul(self.ps[pa[e_]][:, hf * 128:(hf + 1) * 128], lhsT=kT[hf][e_ * 64:(e_ + 1) * 64, :], rhs=qT[hf][e_ * 64:(e_ + 1) * 64, :],
                                           start=True, stop=True)
                        return ins
                    P.op('pe', mma, reads=r_qT + r_kT, writes=[self.r_ps[pa[e_]]])
                    for hf in range(2):
                        P.op('dve', lambda e, e_=e_, hf=hf, pa=pa: e.tensor_tensor(out=AT[e_][:, hf * 128:(hf + 1) * 128], in0=self.ps[pa[e_]][:, hf * 128:(hf + 1) * 128], in1=self.tri, op=ALU.mult),
                             reads=[self.r_ps[pa[e_]]], writes=[r_AT[e_]])
                for hf in range(2):
                    P.op('act', lambda e, hf=hf: e.copy(out=Sb[hf][:, :], in_=S[hf][:, :]), reads=[r_S[hf]], writes=[r_Sb[hf]])
                po = [self.getps(), self.getps()]
                for e_ in range(2):
                    def mmo(e, e_=e_, po=po, n=n):
                        ins = None
                        for hf in range(2):
                            hd = 2 * hf + e_
                            e.matmul(self.ps[po[e_]][:, hf * 128:(hf + 1) * 128], lhsT=AT[e_][:, hf * 128:(hf + 1) * 128], rhs=V[:, n, hd * 128:(hd + 1) * 128], start=True, stop=False)
                            ins = e.matmul(self.ps[po[e_]][:, hf * 128:(hf + 1) * 128], lhsT=qT[hf][e_ * 64:(e_ + 1) * 64, :], rhs=Sb[hf][e_ * 64:(e_ + 1) * 64, :], start=False, stop=True)
                        return ins
                    P.op('pe', mmo, reads=[r_AT[e_], r_V[n]] + r_qT + r_Sb, writes=[self.r_ps[po[e_]]])
                ssq4, r_ssq4 = self.scal_block(4)
                for e_ in range(2):
                    for hf in range(2):
                        hd = 2 * hf + e_
                        src = self.ps[po[e_]][:, hf * 128:(hf + 1) * 128]
                        P.op('act', lambda e, src=src, hd=hd, ssq4=ssq4: e.activation(out=gjunk[:, 0:128], in_=src, func=AF.Square, accum_out=ssq4[:, hd:hd + 1]),
                             reads=[], writes=[r_ssq4[hd], self.r_ps[po[e_]]])
                self.rstd_batch(ssq4, r_ssq4, 4, 128)
                for e_ in range(2):
                    for hf in range(2):
                        hd = 2 * hf + e_
                        src = self.ps[po[e_]][:, hf * 128:(hf + 1) * 128]
                        P.op('dve', lambda e, src=src, hd=hd, ssq4=ssq4: e.scalar_tensor_tensor(out=osb[:, hd * 128:(hd + 1) * 128], in0=src, scalar=ssq4[:, hd:hd + 1], in1=self.gb[:, hd * 128:(hd + 1) * 128], op0=ALU.mult, op1=ALU.mult),
                             reads=[r_ssq4[hd], self.r_gb], writes=[r_osb, self.r_ps[po[e_]]])
                P.op('dve', lambda e, n=n: e.tensor_tensor(out=og[:, :], in0=osb[:, :], in1=sr[:, n, :], op=ALU.mult), reads=[r_osb, r_sr[n]], writes=[r_og])
                pt2 = self.getps()
                ptw = self.ps[pt2][:, :].bitcast(BF16)

                def tro(e, ptw=ptw):
                    ins = None
                    for j in range(4):
                        ins = e.transpose(out=ptw[:, j * 128:(j + 1) * 128], in_=og[:, j * 128:(j + 1) * 128], identity=self.ident)
                    return ins
                P.op('pe', tro, reads=[r_og], writes=[self.r_ps[pt2]])
                P.op('act', lambda e, n=n, ptw=ptw: e.copy(out=oglaT[:, 0:4, n * 128:(n + 1) * 128], in_=ptw[:, 0:512].rearrange('p (a b) -> p a b', b=128)),
                     reads=[self.r_ps[pt2]], writes=[r_ogla])
            for hf in range(2):
                pp = self.getps()
                P.op('pe', lambda e, hf=hf, pp=pp, n=n: e.matmul(self.ps[pp][:, 0:256], lhsT=kd[:, hf * 128:(hf + 1) * 128], rhs=V[:, n, hf * 256:(hf + 1) * 256], start=True, stop=True),
                     reads=[r_kd, r_V[n]], writes=[self.r_ps[pp]])
                cdh, r_cdh = cds[hf]
                for e_ in range(2):
                    P.op('dve', lambda e, hf=hf, e_=e_, pp=pp, cdh=cdh: e.scalar_tensor_tensor(out=S[hf][e_ * 64:(e_ + 1) * 64, :], in0=S[hf][e_ * 64:(e_ + 1) * 64, :], scalar=cdh[e_ * 64:(e_ + 1) * 64, :],
                                                                                             in1=self.ps[pp][e_ * 64:(e_ + 1) * 64, e_ * 128:(e_ + 1) * 128], op0=ALU.mult, op1=ALU.add),
                         reads=[self.r_ps[pp], r_cdh, r_S[hf]] + ([r_Sb[hf]] if mode == 'B' else []), writes=[r_S[hf]])
                if mode == 'A':
                    P.op('dve', lambda e, hf=hf, cdh=cdh: e.tensor_tensor(out=Dt[:, hf:hf + 1], in0=Dt[:, hf:hf + 1], in1=cdh, op=ALU.mult), reads=[r_cdh, r_D], writes=[r_D])
        if mode == 'A':
            for hf in range(2):
                P.dma('sp', xs_out[:, hf * 128:(hf + 1) * 128], S[hf][:, :], reads=[r_S[hf]], writes=[self.r_xsd], key='xsS%d' % hf)
            P.dma('sp', xs_out[:, 256:258], Dt[:, 0:2], reads=[r_D], writes=[self.r_xsd], key='xsD')
        P.barrier()

    def phase_A_exchange(self, l, xk_out, xs_out):
        P = self.P
        X = self.RX
        X.reset()
        ckT = v3(X.alloc(4 * T * 2), T)
        r_ck = Res()
        self.latent(l, 'wkv', 'g_mla_kv_norm', ckT, r_ck, X)
        for kc in range(4):
            P.dma('sp', xk_out[kc], ckT[:, kc, :], reads=[r_ck], writes=[self.r_xkd], key='xk%d' % kc)
        rope = X.alloc(2 * T * 4, F32)
        r_rope = Res()
        P.dma('sp', rope[0:64, :], self.d_rope, writes=[r_rope], key='rope')
        kpe = X.alloc(T * 2)
        r_kpe = Res()
        t1 = X.alloc(2048, F32)
        t2 = X.alloc(2048, F32)
        r_t1, r_t2 = Res(), Res()
        hold = {}

        def epi_kr(ti, mi, tg, pi):
            if mi == 0:
                hold[tg] = pi
                P.op('dve', lambda e: e.tensor_tensor(out=t1[0:64, :], in0=self.ps[pi][0:64, :], in1=rope[0:64, tg * 512:(tg + 1) * 512], op=ALU.mult),
                     reads=[self.r_ps[pi], r_rope], writes=[r_t1])
            else:
                P.op('dve', lambda e: e.tensor_tensor(out=t2[0:64, :], in0=self.ps[pi][0:64, :], in1=rope[0:64, T + tg * 512:T + (tg + 1) * 512], op=ALU.mult),
                     reads=[self.r_ps[pi], r_rope], writes=[r_t2])
                P.op('dve', lambda e: e.tensor_tensor(out=kpe[0:64, tg * 512:(tg + 1) * 512], in0=t1[0:64, :], in1=t2[0:64, :], op=ALU.add),
                     reads=[r_t1, r_t2], writes=[r_kpe])
        for tg in range(2):
            self.lin_fm(self.uT, lambda tg_: self.r_uT[tg_ * 4:(tg_ + 1) * 4], KC, self.dram['L%d_wkr' % l], 128, [(0, 64), (64, 64)], epi_kr, tgs=(tg,))
        P.dma('sp', xk_out[4][0:64, :], kpe[0:64, :], reads=[r_kpe], writes=[self.r_xkd], key='xk4')
        zt = X.alloc(64, F32)
        cgs = X.alloc(2048, F32)
        zb = X.alloc(1024)
        r_zt, r_cgs, r_zb = Res(), Res(), Res()
        wc = self.dram['L%d_wconv' % l]
        pcg, phn = self.getps(), self.getps()
        for part, base, pp in (('cg', 2, pcg), ('hin', 4, phn)):
            for k in range(2):
                wt, r_wt = self.wload(wc[base + k])
                wv = v3(wt[:, 0:KC * 256], 256)

                def mm(e, wv=wv, k=k, pp=pp):
                    ins = None
                    for kc in range(KC):
                        ins = e.matmul(self.ps[pp][0:2, k * 256:(k + 1) * 256], lhsT=self.uT[:, kc, T - 2:T], rhs=wv[:, kc, :], start=(kc == 0), stop=(kc == KC - 1))
                    return ins
                P.op('pe', mm, reads=[r_wt] + self.r_uT[4:8], writes=[self.r_ps[pp]])
        P.op('act', lambda e: e.copy(out=cgs[0:2, :], in_=self.ps[pcg][0:2, :]), reads=[self.r_ps[pcg]], writes=[r_cgs])
        P.op('dve', lambda e: e.tensor_tensor(out=zb[0:2, :], in0=cgs[0:2, :], in1=self.ps[phn][0:2, :], op=ALU.mult), reads=[self.r_ps[phn], r_cgs], writes=[r_zb])
        ptz = self.getps()
        ptzv = self.ps[ptz][:, :].bitcast(BF16)

        def trz(e):
            ins = None
            for ch in range(4):
                ins = e.transpose(out=ptzv[:, ch * 2:ch * 2 + 2], in_=zb[0:2, ch * 128:(ch + 1) * 128], identity=self.ident[0:2, 0:2])
            return ins
        P.op('pe', trz, reads=[r_zb], writes=[self.r_ps[ptz]])
        P.op('act', lambda e: e.copy(out=zt[:, 0:8], in_=ptzv[:, 0:8]), reads=[self.r_ps[ptz]], writes=[r_zt])
        P.dma('sp', xs_out[:, 258:266], zt[:, 0:8], reads=[r_zt], writes=[self.r_xsd], key='xsz')
        P.barrier()
        X.reset()
        self.gla(l, 'A', X, xs_out=xs_out)
        X.reset()

    def mla(self, l, S2, X, gk, ownk, omlaT, r_omla):
        P = self.P
        cqT = v3(S2.alloc(4 * T * 2), T)
        r_cq = Res()
        self.latent(l, 'wq', 'g_mla_q_norm', cqT, r_cq, X)
        rope = S2.alloc(2 * T * 4, F32)
        r_rope = Res()
        P.dma('sp', rope[0:64, :], self.d_rope, writes=[r_rope], key='rope')
        segs = [X.alloc(XK_COLS * 2), X.alloc(XK_COLS * 2), S2.alloc(XK_COLS * 2), S2.alloc(XK_COLS * 2)]
        r_seg = [Res() for _ in range(4)]
        for s in range(4):
            for j5 in range(5):
                srcd = ownk[j5] if s == 0 else gk[j5][s - 1]
                P.dma('sp', segs[s][:, j5 * 1024:(j5 + 1) * 1024], srcd, reads=[self.r_gk, self.r_xkd], writes=[r_seg[s]], key='seg%d' % s)
        KT = [X.alloc(T * 2) for _ in range(4)]
        VV = [v3(X.alloc(T * 2), 128) for _ in range(4)]
        r_K = [[Res(), Res()] for _ in range(4)]
        r_Vv = [[Res(), Res()] for _ in range(4)]
        qT = [X.alloc(T * 2) for _ in range(2)]
        qpe = [X.alloc(T * 2) for _ in range(2)]
        r_q = [Res(), Res()]
        r_qpe = [Res(), Res()]
        PT = [X.alloc(1024) for _ in range(5)]
        r_PT = [Res() for _ in range(5)]
        t1 = X.alloc(2048, F32)
        t2 = X.alloc(2048, F32)
        r_t1, r_t2 = Res(), Res()
        rsb = X.alloc(2048, F32)
        r_rsb = Res()
        wm = self.dram['L%d_wmla' % l]
        pti = 0
        for hd in range(8):
            hb = hd % 2
            wt, r_wt = self.wload(wm[hd])
            wv = v3(wt[:, 0:4 * 512], 512)
            self.ps_set = [0, 1, 2, 3]
            for tg in range(2):
                pi = self.getps()

                def mmq(e, tg=tg, pi=pi, wv=wv):
                    ins = None
                    for kc in range(4):
                        ins = e.matmul(self.ps[pi][:, :], lhsT=wv[:, kc, 0:128], rhs=cqT[:, kc, tg * 512:(tg + 1) * 512], start=(kc == 0), stop=(kc == 3))
                    return ins
                P.op('pe', mmq, reads=[r_wt, r_cq], writes=[self.r_ps[pi]])
                P.op('act', lambda e, tg=tg, pi=pi, hb=hb: e.copy(out=qT[hb][:, tg * 512:(tg + 1) * 512], in_=self.ps[pi][:, :]), reads=[self.r_ps[pi]], writes=[r_q[hb]])
                for which in range(2):
                    pj = self.getps()

                    def mmr(e, tg=tg, pj=pj, wv=wv, which=which):
                        ins = None
                        for kc in range(4):
                            ins = e.matmul(self.ps[pj][0:64, :], lhsT=wv[:, kc, 128 + which * 64:192 + which * 64], rhs=cqT[:, kc, tg * 512:(tg + 1) * 512], start=(kc == 0), stop=(kc == 3))
                        return ins
                    P.op('pe', mmr, reads=[r_wt, r_cq], writes=[self.r_ps[pj]])
                    tt, r_tt = (t1, r_t1) if which == 0 else (t2, r_t2)
                    P.op('dve', lambda e, tg=tg, pj=pj, which=which, tt=tt: e.tensor_tensor(out=tt[0:64, :], in0=self.ps[pj][0:64, :], in1=rope[0:64, which * T + tg * 512:which * T + (tg + 1) * 512], op=ALU.mult),
                         reads=[self.r_ps[pj], r_rope], writes=[r_tt])
                P.op('dve', lambda e, tg=tg, hb=hb: e.tensor_tensor(out=qpe[hb][0:64, tg * 512:(tg + 1) * 512], in0=t1[0:64, :], in1=t2[0:64, :], op=ALU.add),
                     reads=[r_t1, r_t2], writes=[r_qpe[hb]])
            for s in range(4):
                sv = v3(segs[s][:, 0:4096], T)
                r_sg_ = r_seg[s]
                for ktg in range(2):
                    pi = self.getps()

                    def mmk(e, sv=sv, ktg=ktg, pi=pi, wv=wv):
                        ins = None
                        for kc in range(4):
                            ins = e.matmul(self.ps[pi][:, :], lhsT=wv[:, kc, 256:384], rhs=sv[:, kc, ktg * 512:(ktg + 1) * 512], start=(kc == 0), stop=(kc == 3))
                        return ins
                    P.op('pe', mmk, reads=[r_wt, r_sg_], writes=[self.r_ps[pi]])
                    eng = 'act' if ktg == 0 else 'dve'
                    P.op(eng, lambda e, s=s, ktg=ktg, pi=pi, eng=eng: (e.copy if eng == 'act' else e.tensor_copy)(out=KT[s][:, ktg * 512:(ktg + 1) * 512], in_=self.ps[pi][:, :]),
                         reads=[self.r_ps[pi]], writes=[r_K[s][ktg]])
                for kb4 in range(2):
                    pi = self.getps()

                    def mmv(e, sv=sv, kb4=kb4, pi=pi, wv=wv):
                        ins = None
                        for j in range(4):
                            kb = kb4 * 4 + j
                            for kc in range(4):
                                ins = e.matmul(self.ps[pi][:, j * 128:(j + 1) * 128], lhsT=sv[:, kc, kb * 128:(kb + 1) * 128], rhs=wv[:, kc, 384:512], start=(kc == 0), stop=(kc == 3))
                        return ins
                    P.op('pe', mmv, reads=[r_wt, r_sg_], writes=[self.r_ps[pi]])
                    eng = 'dve' if kb4 == 0 else 'act'
                    P.op(eng, lambda e, s=s, kb4=kb4, pi=pi, eng=eng: (e.copy if eng == 'act' else e.tensor_copy)(out=VV[s][:, kb4 * 4:(kb4 + 1) * 4, :], in_=self.ps[pi][:, :].rearrange('p (a b) -> p a b', b=128)),
                         reads=[self.r_ps[pi]], writes=[r_Vv[s][kb4]])
            for tg in range(2):
                units = []
                for s in range(4):
                    for kb in range(8):
                        if s == 0 and kb > 4 * tg + 3:
                            continue
                        r = (kb - 4 * tg) if (s == 0 and kb >= 4 * tg) else 0
                        units.append((s, kb, r, s == 0 and kb >= 4 * tg))
                self.ps_set = [0, 1, 2, 3, 4, 5]
                sp_of = {}

                def emit_scores(u):
                    s, kb, r, diag = units[u]
                    pi = self.getps()
                    sp_of[u] = pi
                    c0 = r * 128

                    def mms(e, s=s, kb=kb, c0=c0, pi=pi, tg=tg, hb=hb):
                        e.matmul(self.ps[pi][:, c0:512], lhsT=KT[s][:, kb * 128:(kb + 1) * 128], rhs=qT[hb][:, tg * 512 + c0:(tg + 1) * 512], start=True, stop=False)
                        return e.matmul(self.ps[pi][:, c0:512], lhsT=segs[s][0:64, 4096 + kb * 128:4096 + (kb + 1) * 128], rhs=qpe[hb][0:64, tg * 512 + c0:(tg + 1) * 512], start=False, stop=True)
                    P.op('pe', mms, reads=[r_K[s], r_seg[s], r_q[hb], r_qpe[hb]], writes=[self.r_ps[pi]])

                def emit_pv(u, pt, first, last):
                    s, kb, r, diag = units[u]
                    c0 = r * 128

                    def mmp(e, s=s, kb=kb, c0=c0, pt=pt, first=first, last=last, hb=hb):
                        e.matmul(self.ps[6][:, c0:512], lhsT=VV[s][:, kb, :], rhs=PT[pt][:, c0:512], start=first, stop=last)
                        return e.matmul(self.ps[7][:, c0:512], lhsT=self.ones, rhs=PT[pt][:, c0:512], start=first, stop=last)
                    P.op('pe', mmp, reads=[r_Vv[s], r_PT[pt]], writes=[self.r_ps[6], self.r_ps[7]])
                LA = 4
                for u0 in range(min(LA, len(units))):
                    emit_scores(u0)
                for u in range(len(units)):
                    s, kb, r, diag = units[u]
                    c0 = r * 128
                    pi = sp_of[u]
                    pt = pti % 5
                    pti += 1
                    if s == 0:
                        P.op('act', lambda e, pi=pi, pt=pt, c0=c0: e.activation(out=PT[pt][:, c0:512], in_=self.ps[pi][:, c0:512], func=AF.Exp, scale=SCALE_MLA),
                             reads=[self.r_ps[pi]], writes=[r_PT[pt]])
                    else:
                        P.op('act', lambda e, pi=pi, pt=pt, s=s: e.activation(out=PT[pt][:, :], in_=self.ps[pi][:, :], func=AF.Exp, scale=SCALE_MLA, bias=self.ctl[:, s - 1:s]),
                             reads=[self.r_ps[pi]], writes=[r_PT[pt]])
                    if diag:
                        P.op('pool', lambda e, pt=pt, c0=c0: e.tensor_tensor(out=PT[pt][:, c0:c0 + 128], in0=PT[pt][:, c0:c0 + 128], in1=self.tri, op=ALU.mult),
                             reads=[r_PT[pt]], writes=[r_PT[pt]])
                    if u + LA < len(units):
                        emit_scores(u + LA)
                    emit_pv(u, pt, u == 0, u == len(units) - 1)
                P.op('dve', lambda e: e.reciprocal(out=rsb[:, :], in_=self.ps[7][:, :]), reads=[self.r_ps[7]], writes=[r_rsb])
                P.op('dve', lambda e, tg=tg, hd=hd: e.tensor_tensor(out=omlaT[:, hd, tg * 512:(tg + 1) * 512], in0=self.ps[6][:, :], in1=rsb[:, :], op=ALU.mult),
                     reads=[self.r_ps[6], r_rsb], writes=[r_omla])
        self.ps_set = list(range(8))
        P.barrier()

    def conv(self, l, X, gs, r_gs, oconvT, r_oconv):
        P = self.P
        P.dma('sp', self.convw[:, 0:12], self.dram['L%d_convw' % l], writes=[self.r_lc], key='lc')
        z = X.alloc((T + 2) * 4 + 56, F32)
        y = X.alloc(T * 4, F32)
        cg = X.alloc(T * 4, F32)
        halo = X.alloc(64, F32)
        r_z, r_y, r_cg, r_halo = Res(), Res(), Res(), Res()
        for i in range(3):
            if i == 0:
                P.op('dve', lambda e, i=i: e.tensor_scalar(out=halo[:, 0:8], in0=gs[:, i, 258:266], scalar1=self.ctl[:, 12 + i:13 + i], scalar2=None, op0=ALU.mult), reads=[r_gs], writes=[r_halo])
            else:
                P.op('dve', lambda e, i=i: e.scalar_tensor_tensor(out=halo[:, 0:8], in0=gs[:, i, 258:266], scalar=self.ctl[:, 12 + i:13 + i], in1=halo[:, 0:8], op0=ALU.mult, op1=ALU.add),
                     reads=[r_gs, r_halo], writes=[r_halo])
        wc = self.dram['L%d_wconv' % l]
        for k in range(2):
            wts = [self.wload(wc[base + k]) for base in (2, 4, 0)]
            for sub in range(2):
                ch = 2 * k + sub
                for tg in range(2):
                    def mmfor(wt, r_wt, sub=sub, tg=tg):
                        pi = self.getps()
                        wv = v3(wt[:, 0:KC * 256], 256)

                        def mm(e, wv=wv, pi=pi, sub=sub, tg=tg):
                            ins = None
                            for kc in range(KC):
                                ins = e.matmul(self.ps[pi][:, :], lhsT=wv[:, kc, sub * 128:(sub + 1) * 128], rhs=self.uT[:, kc, tg * 512:(tg + 1) * 512], start=(kc == 0), stop=(kc == KC - 1))
                            return ins
                        P.op('pe', mm, reads=[r_wt] + self.r_uT[tg * 4:(tg + 1) * 4], writes=[self.r_ps[pi]])
                        return pi
                    pc = mmfor(*wts[0])
                    P.op('act', lambda e, pc=pc, tg=tg: e.copy(out=cg[:, tg * 512:(tg + 1) * 512], in_=self.ps[pc][:, :]), reads=[self.r_ps[pc]], writes=[r_cg])
                    ph = mmfor(*wts[1])
                    P.op('dve', lambda e, ph=ph, tg=tg: e.tensor_tensor(out=z[:, 2 + tg * 512:2 + (tg + 1) * 512], in0=cg[:, tg * 512:(tg + 1) * 512], in1=self.ps[ph][:, :], op=ALU.mult),
                         reads=[self.r_ps[ph], r_cg], writes=[r_z])
                P.op('dve', lambda e, ch=ch: e.tensor_copy(out=z[:, 0:2], in_=halo[:, ch * 2:ch * 2 + 2]), reads=[r_halo], writes=[r_z])
                cw = self.convw
                P.op('dve', lambda e, ch=ch: e.tensor_scalar(out=y[:, :], in0=z[:, 2:T + 2], scalar1=cw[:, ch * 3 + 2:ch * 3 + 3], scalar2=None, op0=ALU.mult), reads=[r_z, self.r_lc], writes=[r_y])
                P.op('dve', lambda e, ch=ch: e.scalar_tensor_tensor(out=y[:, :], in0=z[:, 1:T + 1], scalar=cw[:, ch * 3 + 1:ch * 3 + 2], in1=y[:, :], op0=ALU.mult, op1=ALU.add), reads=[r_z, r_y], writes=[r_y])
                P.op('dve', lambda e, ch=ch: e.scalar_tensor_tensor(out=y[:, :], in0=z[:, 0:T], scalar=cw[:, ch * 3:ch * 3 + 1], in1=y[:, :], op0=ALU.mult, op1=ALU.add), reads=[r_z, r_y], writes=[r_y])
                for tg in range(2):
                    pb = mmfor(*wts[2], sub=sub, tg=tg)
                    P.op('dve', lambda e, pb=pb, tg=tg, ch=ch: e.tensor_tensor(out=oconvT[:, ch, tg * 512:(tg + 1) * 512], in0=y[:, tg * 512:(tg + 1) * 512], in1=self.ps[pb][:, :], op=ALU.mult),
                         reads=[self.r_ps[pb], r_y], writes=[r_oconv])
        P.barrier()

    def merge(self, l, X, omlaT, r_omla, oglaT, r_ogla, oconvT, r_oconv, mT, r_mT):
        P = self.P
        P.dma('sp', self.bgate[:, 0:48], self.dram['L%d_bgate' % l], writes=[self.r_lc], key='lc')
        sig = [X.alloc(2048, F32) for _ in range(2)]
        acc = [X.alloc(2048, F32) for _ in range(2)]
        r_sig = [Res(), Res()]
        r_acc = [Res(), Res()]
        wmg = self.dram['L%d_wmerge' % l]
        srcs = [(omlaT, r_omla, 8), (oglaT, r_ogla, 4), (oconvT, r_oconv, 4)]
        k = 0
        for ncx in range(16):
            w0, r_w0 = self.wload(wmg[ncx * 2])
            w1, r_w1 = self.wload(wmg[ncx * 2 + 1])
            v0 = v3(w0[:, 0:4096], 128)
            v1 = v3(w1[:, 0:4096], 128)
            gate_w = [(v0, 0, r_w0), (v0, 16, r_w0), (v1, 0, r_w1)]
            proj_off = [16, 24, 28]
            for tg in range(2):
                a = k % 2
                k += 1
                for br in range(3):
                    gv, goff, r_g = gate_w[br]
                    pg = self.getps()

                    def mmg(e, gv=gv, goff=goff, pg=pg, tg=tg):
                        ins = None
                        for kc in range(KC):
                            ins = e.matmul(self.ps[pg][:, :], lhsT=gv[:, goff + kc, :], rhs=self.uT[:, kc, tg * 512:(tg + 1) * 512], start=(kc == 0), stop=(kc == KC - 1))
                        return ins
                    P.op('pe', mmg, reads=[r_g] + self.r_uT[tg * 4:(tg + 1) * 4], writes=[self.r_ps[pg]])
                    sb = br % 2
                    P.op('act', lambda e, pg=pg, sb=sb, br=br, ncx=ncx: e.activation(out=sig[sb][:, :], in_=self.ps[pg][:, :], func=AF.Sigmoid, bias=self.bgate[:, br * 16 + ncx:br * 16 + ncx + 1]),
                         reads=[self.r_ps[pg], self.r_lc], writes=[r_sig[sb]])
                    src, r_src, nk = srcs[br]
                    py = self.getps()

                    def mmy(e, src=src, nk=nk, po=proj_off[br], py=py, tg=tg, v1=v1):
                        ins = None
                        for kk in range(nk):
                            ins = e.matmul(self.ps[py][:, :], lhsT=v1[:, po + kk, :], rhs=src[:, kk, tg * 512:(tg + 1) * 512], start=(kk == 0), stop=(kk == nk - 1))
                        return ins
                    P.op('pe', mmy, reads=[r_w1, r_src], writes=[self.r_ps[py]])
                    if br == 0:
                        P.op('dve', lambda e, py=py, sb=sb, a=a: e.tensor_tensor(out=acc[a][:, :], in0=sig[sb][:, :], in1=self.ps[py][:, :], op=ALU.mult),
                             reads=[self.r_ps[py], r_sig[sb]], writes=[r_acc[a]])
                    else:
                        P.op('dve', lambda e, py=py, sb=sb: e.tensor_tensor(out=sig[sb][:, :], in0=sig[sb][:, :], in1=self.ps[py][:, :], op=ALU.mult),
                             reads=[self.r_ps[py], r_sig[sb]], writes=[r_sig[sb]])
                        if br == 1:
                            P.op('dve', lambda e, sb=sb, a=a: e.tensor_tensor(out=acc[a][:, :], in0=acc[a][:, :], in1=sig[sb][:, :], op=ALU.add), reads=[r_sig[sb], r_acc[a]], writes=[r_acc[a]])
                        else:
                            P.op('dve', lambda e, sb=sb, a=a, ncx=ncx, tg=tg: e.tensor_tensor(out=mT[:, ncx, tg * 512:(tg + 1) * 512], in0=acc[a][:, :], in1=sig[sb][:, :], op=ALU.add),
                                 reads=[r_sig[sb], r_acc[a]], writes=[r_mT[tg]])
        P.barrier()

    def proj_residual(self, xT, r_x, wtiles):
        P = self.P

        def epi(ti, tb, pi):
            P.op('dve', lambda e: e.tensor_tensor(out=self.h[:, tb, ti * 256:(ti + 1) * 256], in0=self.ps[pi][:, 0:256], in1=self.h[:, tb, ti * 256:(ti + 1) * 256], op=ALU.add),
                 reads=[self.r_ps[pi], self.r_h[tb]], writes=[self.r_h[tb]])
        self.lin_tok(xT, r_x, KC, wtiles, 256, epi)
        P.barrier()

    def xattn(self, l, mem_d):
        P = self.P
        X = self.RX
        X.reset()
        self.norm_h(self.dram['L%d_g_xattn_norm' % l])
        memT = v3(X.alloc(KC * MEM * 2), MEM)
        r_memT = [[Res() for _ in range(4)] for _ in range(2)]
        KTm = v3(X.alloc(KC * MEM * 2), MEM)
        r_KTm = [Res() for _ in range(KC)]
        Vm = v3(X.alloc(2 * D * 2), D)
        r_Vm = [Res(), Res()]
        save = X.off
        mm_ = v3(X.alloc(2 * D * 4, F32), D)
        r_mm = [Res(), Res()]
        for mb in range(2):
            P.dma('sp', mm_[:, mb, :], mem_d[mb * 128:(mb + 1) * 128, :], writes=[r_mm[mb]], key='mem%d' % mb)
        ub = [X.alloc(4096), X.alloc(4096)]
        r_ub = [Res(), Res()]
        junk = X.alloc(4096)
        self.load_gain(self.dram['L%d_g_mem_norm' % l])
        self.norm_T(lambda tb: mm_[:, tb, :], lambda tb: r_mm[tb], D, memT, lambda tb: r_memT[tb], 2, ub, r_ub, junk)
        P.barrier()
        X.off = save
        qx = v3(X.alloc(KC * T * 2), T)
        r_qx = [[Res() for _ in range(2)] for _ in range(KC)]
        PTm = [X.alloc(1024) for _ in range(4)]
        r_PTm = [Res() for _ in range(4)]
        rsb = X.alloc(2048, F32)
        r_rsb = Res()

        def epi_k(ti, mi, tg, pi):
            ch = ti * 2 + mi
            P.op('act' if ch % 2 == 0 else 'dve', lambda e: (e.copy if ch % 2 == 0 else e.tensor_copy)(out=KTm[:, ch, :], in_=self.ps[pi][:, 0:256]),
                 reads=[self.r_ps[pi]], writes=[r_KTm[ch]])
        self.lin_fm(memT, lambda tg: r_memT, KC, self.dram['L%d_xk' % l], 256, [(0, 128), (128, 128)], epi_k, tgs=(0,), ncol_T=256)

        def epi_v(ti, tb, pi):
            P.op('act' if tb == 0 else 'dve', lambda e: (e.copy if tb == 0 else e.tensor_copy)(out=Vm[:, tb, ti * 256:(ti + 1) * 256], in_=self.ps[pi][:, 0:256]),
                 reads=[self.r_ps[pi]], writes=[r_Vm[tb]])
        self.lin_tok(memT, lambda tb: [r_memT[tb]], KC, self.dram['L%d_xv' % l], 256, epi_v, ntb=2)

        def epi_q(ti, mi, tg, pi):
            ch = ti * 2 + mi
            P.op('act' if tg == 0 else 'dve', lambda e: (e.copy if tg == 0 else e.tensor_copy)(out=qx[:, ch, tg * 512:(tg + 1) * 512], in_=self.ps[pi][:, :]),
                 reads=[self.r_ps[pi]], writes=[r_qx[ch][tg]])
        self.lin_fm(self.uT, lambda tg: self.r_uT[tg * 4:(tg + 1) * 4], KC, self.dram['L%d_xq' % l], 256, [(0, 128), (128, 128)], epi_q)
        k = 0
        for hx in range(4):
            for tg in range(2):
                pts = []
                for mb in range(2):
                    pi = self.getps()

                    def mms(e, mb=mb, pi=pi, hx=hx, tg=tg):
                        ins = None
                        for dc in range(4):
                            ins = e.matmul(self.ps[pi][:, :], lhsT=KTm[:, hx * 4 + dc, mb * 128:(mb + 1) * 128], rhs=qx[:, hx * 4 + dc, tg * 512:(tg + 1) * 512], start=(dc == 0), stop=(dc == 3))
                        return ins
                    P.op('pe', mms, reads=[r_KTm[hx * 4 + dc] for dc in range(4)] + [r_qx[hx * 4 + dc][tg] for dc in range(4)], writes=[self.r_ps[pi]])
                    pt = k % 4
                    k += 1
                    pts.append(pt)
                    P.op('act', lambda e, pi=pi, pt=pt: e.activation(out=PTm[pt][:, :], in_=self.ps[pi][:, :], func=AF.Exp, scale=SCALE_X), reads=[self.r_ps[pi]], writes=[r_PTm[pt]])
                psu = self.getps()

                def mmsum(e, psu=psu, pts=pts):
                    e.matmul(self.ps[psu][:, :], lhsT=self.ones, rhs=PTm[pts[0]][:, :], start=True, stop=False)
                    return e.matmul(self.ps[psu][:, :], lhsT=self.ones, rhs=PTm[pts[1]][:, :], start=False, stop=True)
                P.op('pe', mmsum, reads=[r_PTm[pts[0]], r_PTm[pts[1]]], writes=[self.r_ps[psu]])
                P.op('dve', lambda e, psu=psu: e.reciprocal(out=rsb[:, :], in_=self.ps[psu][:, :]), reads=[self.r_ps[psu]], writes=[r_rsb])
                for dvc in range(4):
                    po = self.getps()

                    def mmo(e, po=po, pts=pts, hx=hx, dvc=dvc):
                        e.matmul(self.ps[po][:, :], lhsT=Vm[:, 0, hx * 512 + dvc * 128:hx * 512 + (dvc + 1) * 128], rhs=PTm[pts[0]][:, :], start=True, stop=False)
                        return e.matmul(self.ps[po][:, :], lhsT=Vm[:, 1, hx * 512 + dvc * 128:hx * 512 + (dvc + 1) * 128], rhs=PTm[pts[1]][:, :], start=False, stop=True)
                    P.op('pe', mmo, reads=r_Vm + [r_PTm[pts[0]], r_PTm[pts[1]]], writes=[self.r_ps[po]])
                    ch = hx * 4 + dvc
                    P.op('dve', lambda e, po=po, ch=ch, tg=tg: e.tensor_tensor(out=qx[:, ch, tg * 512:(tg + 1) * 512], in0=self.ps[po][:, :], in1=rsb[:, :], op=ALU.mult),
                         reads=[self.r_ps[po], r_rsb], writes=[r_qx[ch][tg]])
        P.barrier()
        allq = [r_qx[ch][tg] for ch in range(KC) for tg in range(2)]
        self.proj_residual(qx, lambda tb: allq, self.dram['L%d_xo' % l])
        X.reset()

    def phase_B_mixer(self, l, gk, ownk, gs_d, hspill):
        P = self.P
        X = self.RX
        X.reset()
        S2 = Region(self.arena, 0, 65536)
        omlaT = v3(S2.alloc(8 * T * 2), T)
        gs = v3(S2.alloc(4 * XS_COLS * 4 + 32, F32)[:, 0:4 * XS_COLS], XS_COLS)
        r_omla, r_ogla, r_oconv, r_gs = Res(), Res(), Res(), Res()
        for i in range(4):
            P.dma('sp', gs[:, i, :], gs_d[i], reads=[self.r_gsd], writes=[r_gs], key='gs')
        s2save = S2.off
        if 'skip_mla' not in self.dbg:
            self.mla(l, S2, X, gk, ownk, omlaT, r_omla)
        if 'omla' in self.dbg:
            self.dump('omla', omlaT[:, :, :], [r_omla], BF16)
        if 'stop_mla' in self.dbg:
            return True
        X.reset()
        S2.off = s2save
        oglaT = v3(S2.alloc(4 * T * 2), T)
        oconvT = v3(S2.alloc(4 * T * 2), T)
        self.gla(l, 'B', X, gs=gs, r_gs=r_gs, oglaT=oglaT, r_ogla=r_ogla)
        if 'ogla' in self.dbg:
            self.dump('ogla', oglaT[:, :, :], [r_ogla], BF16)
        if 'stop_gla' in self.dbg:
            return True
        X.reset()
        self.conv(l, X, gs, r_gs, oconvT, r_oconv)
        if 'oconv' in self.dbg:
            self.dump('oconv', oconvT[:, :, :], [r_oconv], BF16)
        if 'stop_conv' in self.dbg:
            return True
        X.reset()
        mT = v3(X.alloc(KC * T * 2), T)
        r_mT = [Res(), Res()]
        self.merge(l, X, omlaT, r_omla, oglaT, r_ogla, oconvT, r_oconv, mT, r_mT)
        self.load_h(hspill)
        self.proj_residual(mT, lambda tb: r_mT, self.dram['L%d_wout' % l])
        X.reset()

    def load_h(self, src):
        for tb in range(NTB):
            self.P.dma('sp', self.h[:, tb, :], src[tb * 128:(tb + 1) * 128, :], reads=[self.r_hsp], writes=[self.r_h[tb]], key='h%d' % tb)

    def store_h(self, dst, key='hs'):
        for tb in range(NTB):
            self.P.dma('sp', dst[tb * 128:(tb + 1) * 128, :], self.h[:, tb, :], reads=[self.r_h[tb]], writes=[self.r_hsp], key=key)

    def final_norm(self, g_d, y_d):
        P = self.P
        X = self.RX
        X.reset()
        self.load_gain(g_d)
        ob = [X.alloc(8192, F32), X.alloc(8192, F32)]
        r_ob = [Res(), Res()]
        junk = X.alloc(4096)
        r_junk = Res()
        for tb in range(NTB):
            ss, r_ss = self.scal()
            rs, r_rs = self.scal()
            b = tb % 2
            P.op('act', lambda e, tb=tb, ss=ss: e.activation(out=junk[:, :], in_=self.h[:, tb, :], func=AF.Square, accum_out=ss), reads=[self.r_h[tb]], writes=[r_junk, r_ss])
            P.op('dve', lambda e, ss=ss, rs=rs: e.tensor_scalar(out=rs, in0=ss, scalar1=1.0 / D, scalar2=EPS, op0=ALU.mult, op1=ALU.add), reads=[r_ss], writes=[r_rs])
            P.op('act', lambda e, rs=rs: e.activation(out=rs, in_=rs, func=AF.Sqrt), reads=[r_rs], writes=[r_rs])
            P.op('dve', lambda e, rs=rs: e.reciprocal(out=rs, in_=rs), reads=[r_rs], writes=[r_rs])
            P.op('dve', lambda e, tb=tb, b=b, rs=rs: e.scalar_tensor_tensor(out=ob[b][:, :], in0=self.h[:, tb, :], scalar=rs, in1=self.gb[:, :], op0=ALU.mult, op1=ALU.mult),
                 reads=[self.r_h[tb], r_rs, self.r_gb], writes=[r_ob[b]])
            P.dma('sp', y_d[tb * 128:(tb + 1) * 128, :], ob[b][:, :], reads=[r_ob[b]], writes=[r_ob[b]], key='y%d' % b)
        P.barrier()


def build_program(stages, fused=False, dbg=None):
    nc = bass.Bass("TRN2", target_bir_lowering=False)
    layers = sorted({int(s[1]) for s in stages})
    st = ExitStack()
    with st:
        B = Builder(nc, st, layers, SHAPES, dbg)
        P = B.P
        hin = nc.dram_tensor('hin', [T, D], F32, kind='ExternalInput').ap()
        hout = nc.dram_tensor('hout', [T, D], F32, kind='ExternalOutput').ap()
        mem = None
        if any(s[0] == 'B' for s in stages):
            mem = nc.dram_tensor('mem', [MEM, D], F32, kind='ExternalInput').ap()
        ends_with_A = stages[-1][0] == 'A'
        xk_out = xs_out = None
        if not fused:
            if ends_with_A:
                xk_out = nc.dram_tensor('xk', [128, XK_COLS], BF16, kind='ExternalOutput').ap()
                xs_out = nc.dram_tensor('xs', [128, XS_COLS], F32, kind='ExternalOutput').ap()
            if stages[0][0] == 'B':
                gk_ = nc.dram_tensor('gk', [4, 128, XK_COLS], BF16, kind='ExternalInput').ap()
                ownk_ = nc.dram_tensor('ownk', [128, XK_COLS], BF16, kind='ExternalInput').ap()
                gk = [gk_[:, :, j * 1024:(j + 1) * 1024] for j in range(5)]
                ownk = [ownk_[:, j * 1024:(j + 1) * 1024] for j in range(5)]
                gs_d = nc.dram_tensor('gs', [4, 128, XS_COLS], F32, kind='ExternalInput').ap()
        RG = [[0, 1, 2, 3], [4, 5, 6, 7]]
        if fused:
            hsp = nc.dram_tensor('hspill', [T, D], F32).ap()
            xk_i = {l: [nc.dram_tensor('xk%d_%d' % (l, j), [128, 1024], BF16) for j in range(5)] for l in layers}
            gk_i = {l: [nc.dram_tensor('gk%d_%d' % (l, j), [4 * 128, 1024], BF16) for j in range(5)] for l in layers}
            xs_i = {l: nc.dram_tensor('xs%d' % l, [128, XS_COLS], F32) for l in layers}
            gs_i = {l: nc.dram_tensor('gs%d' % l, [4 * 128, XS_COLS], F32) for l in layers}
        hsrc = hin
        first = True
        for sname in stages:
            l = int(sname[1])
            if sname[0] == 'A':
                if first:
                    B.load_h(hin)
                B.ffn(l, 1)
                B.norm_h(B.dram['L%d_g_mix_norm' % l])
                if 'h_ffn1' in B.dbg:
                    B.dump('h_ffn1', B.h[:, :, :], B.r_h)
                if fused:
                    def issue_xk(l=l):
                        for a, b in zip(xk_i[l], gk_i[l]):
                            def ccf(e, a=a, b=b):
                                return e.collective_compute("AllGather", ALU.bypass, replica_groups=RG, ins=[a.ap()], outs=[b.ap()])
                            P.custom('pool', ccf, 1, reads=[B.r_xkd], writes=[B.r_gk], key='ccx')
                    B.cc_after_xk = issue_xk
                    B.store_h(hsp)
                    B.phase_A_exchange(l, [t.ap() for t in xk_i[l]], xs_i[l].ap())
                    hsrc = hsp

                    def ccs(e, a=xs_i[l], b=gs_i[l]):
                        return e.collective_compute("AllGather", ALU.bypass, replica_groups=RG, ins=[a.ap()], outs=[b.ap()])
                    P.custom('pool', ccs, 1, reads=[B.r_xsd], writes=[B.r_gsd], key='ccs')
                    gk = [t.ap().rearrange('(s p) c -> s p c', p=128) for t in gk_i[l]]
                    gs_d = gs_i[l].ap().rearrange('(s p) c -> s p c', p=128)
                    ownk = [t.ap() for t in xk_i[l]]
                else:
                    B.phase_A_exchange(l, [xk_out[:, j * 1024:(j + 1) * 1024] for j in range(5)], xs_out)
                    B.store_h(hout)
                    hsrc = hout
            else:
                if first:
                    B.load_h(hin)
                    B.norm_h(B.dram['L%d_g_mix_norm' % l])
                if B.phase_B_mixer(l, gk, ownk, gs_d, hsrc):
                    break
                if 'h_mix' in B.dbg:
                    B.dump('h_mix', B.h[:, :, :], B.r_h)
                if 'stop_mix' in B.dbg:
                    break
                B.xattn(l, mem)
                if 'h_x' in B.dbg:
                    B.dump('h_x', B.h[:, :, :], B.r_h)
                if 'stop_x' in B.dbg:
                    break
                B.ffn(l, 2)
                if l == DEPTH - 1:
                    gfin = nc.dram_tensor('gfinal', [1, D], F32, kind='ExternalInput').ap()
                    B.final_norm(gfin, hout)
                elif sname == stages[-1]:
                    B.store_h(hout)
            first = False
        P.barrier(full=True)
        P.emit()
    return nc


def _input_names(nc):
    names = []
    for alloc in nc.allocations:
        if isinstance(alloc, mybir.MemoryLocationSet) and alloc.kind == 'ExternalInput':
            names.append(alloc.memorylocations[0].name)
    return names


_PROGS = {}


def _run(stages, per_core, dbg=None, fused=False):
    key = (tuple(stages), tuple(sorted(dbg)) if dbg else None, fused)
    if key not in _PROGS:
        _PROGS[key] = build_program(stages, fused=fused, dbg=dbg)
    nc = _PROGS[key]
    names = _input_names(nc)
    in_maps = [{n: pc[n] for n in names if n in pc} for pc in per_core]
    res = run_bass_kernel_spmd(nc, in_maps, core_ids=list(range(NCORES)))
    return res.results


def prepare(inputs):
    inp = {k: np.asarray(v) for k, v in inputs.items()}
    x = np.ascontiguousarray(inp['x'], dtype=np.float32).reshape(NCORES, T, D)
    mem = np.ascontiguousarray(inp['mem'], dtype=np.float32)
    base = {'cbf': const_bf(), 'cf32': const_f32(), 'gfinal': np.ascontiguousarray(inp['final_norm'][None, :], dtype=np.float32)}
    for l in range(DEPTH):
        for k, v in layer_weights(inp, l).items():
            base['L%d_%s' % (l, k)] = v
    per_core = []
    for c in range(NCORES):
        ctl, rope = core_tables(c)
        d = dict(base)
        d.update({'ctl': ctl, 'rope': rope, 'mem': mem[c // 4], 'hin': x[c]})
        per_core.append(d)
    return per_core


def kernel(**inputs):
    per_core = prepare(inputs)

    def exchange(res):
        for c in range(NCORES):
            b0 = (c // 4) * 4
            per_core[c]['hin'] = res[c]['hout']
            per_core[c]['gk'] = np.ascontiguousarray(np.stack([res[b0 + i]['xk'] for i in range(4)]))
            per_core[c]['gs'] = np.ascontiguousarray(np.stack([res[b0 + i]['xs'] for i in range(4)]))
            per_core[c]['ownk'] = res[c]['xk']

    if FUSED:
        r = _run(['A0', 'B0', 'A1', 'B1'], per_core, fused=True)
    else:
        r = _run(['A0'], per_core)
        exchange(r)
        r = _run(['B0', 'A1'], per_core)
        exchange(r)
        r = _run(['B1'], per_core)
    y = np.stack([r[c]['hout'] for c in range(NCORES)]).reshape(2, 4 * T, D)
    return y.astype(np.float32)
```
